# Optimizing a Trainium2 kernel written in Bass

```python
import math
import jax
import jax.numpy as jnp
from jax import lax
import numpy as np

D_MODEL = 1024
BATCH = 8
SEQ = 4096
DEPTH = 2

HEAD_DIM = 64
N_MIXERS = 4
HEADS_PER_MIXER = D_MODEL // (N_MIXERS * HEAD_DIM)
MIX_WIDTH = N_MIXERS * HEADS_PER_MIXER * HEAD_DIM
A_HEADS = HEADS_PER_MIXER
DILATED_BRANCHES = ((128, 1), (512, 4), (2048, 16))
B_Q_HEADS = HEADS_PER_MIXER
B_KV_HEADS = HEADS_PER_MIXER // 2
B_GROUP = B_Q_HEADS // B_KV_HEADS
B_RADIUS = 128
C_HEADS = HEADS_PER_MIXER
C_QK_DIM = HEAD_DIM // 2
C_V_DIM = HEAD_DIM
D_HEADS = HEADS_PER_MIXER
D_Q_RANK = 3 * D_MODEL // 8
D_KV_RANK = D_MODEL // 8
D_NOPE_DIM = HEAD_DIM
D_ROPE_DIM = HEAD_DIM // 2
D_V_DIM = HEAD_DIM
ROPE_THETA = 10000.0
N_ALIBI_HEADS = A_HEADS + B_Q_HEADS + C_HEADS
QUERY_BLOCK = 128
D_FF = ((8 * D_MODEL + 767) // 768) * 256
A_IN = 3 * A_HEADS * HEAD_DIM
B_IN = (B_Q_HEADS + 2 * B_KV_HEADS) * HEAD_DIM
C_IN = C_HEADS * (4 * C_QK_DIM + C_V_DIM)
D_IN = D_Q_RANK + D_KV_RANK + D_ROPE_DIM
IN_WIDTH = A_IN + B_IN + C_IN + D_IN
IN_SPLITS = (A_IN, A_IN + B_IN, A_IN + B_IN + C_IN)
RMS_EPS = 1e-6
NEG_INF = -1e30
LAMBDA_STD = 0.1

kernel_name = 'hybrid_parallel_head_group_encoder'


def _rms_norm(x, g, eps=RMS_EPS):
    xf = x.astype(jnp.float32)
    y = xf * lax.rsqrt(jnp.mean(xf * xf, axis=-1, keepdims=True) + eps)
    return (y * g.astype(jnp.float32)).astype(x.dtype)


def _alibi_slopes():
    j = jnp.arange(1, N_ALIBI_HEADS + 1, dtype=jnp.float32)
    return jnp.exp2(-8.0 * j / N_ALIBI_HEADS)


def _rope_tables(pos):
    half = D_ROPE_DIM // 2
    inv = jnp.power(ROPE_THETA, -jnp.arange(half, dtype=jnp.float32) / half)
    ang = pos[..., None] * inv
    return jnp.cos(ang), jnp.sin(ang)


def _rope(x, cos, sin):
    half = x.shape[-1] // 2
    xf = x.astype(jnp.float32)
    x1, x2 = xf[..., :half], xf[..., half:]
    return jnp.concatenate([x1 * cos - x2 * sin, x1 * sin + x2 * cos], axis=-1).astype(x.dtype)


def _halo_blocks(a, axis, blk, nb):
    T = a.shape[axis]
    pad = [(0, 0)] * a.ndim
    pad[axis] = (blk, nb * blk - T + blk)
    ap = jnp.pad(a, pad)
    ap = ap.reshape(a.shape[:axis] + (nb + 2, blk) + a.shape[axis + 1:])
    prev = lax.slice_in_dim(ap, 0, nb, axis=axis)
    cur = lax.slice_in_dim(ap, 1, nb + 1, axis=axis)
    nxt = lax.slice_in_dim(ap, 2, nb + 2, axis=axis)
    return jnp.concatenate([prev, cur, nxt], axis=axis + 1)


def _banded_attention(q, k, v, pos, slope, radius, sink=None):
    bsz, G, R, T, Dk = q.shape
    W = radius
    nb = -(-T // W)
    qb = jnp.pad(q, ((0, 0), (0, 0), (0, 0), (0, nb * W - T), (0, 0))).reshape(bsz, G, R, nb, W, Dk)
    kb = _halo_blocks(k, 2, W, nb)
    vb = _halo_blocks(v, 2, W, nb)
    qp = jnp.pad(pos, ((0, 0), (0, nb * W - T))).reshape(bsz, nb, W)
    kp = _halo_blocks(pos, 1, W, nb)
    qi = jnp.arange(nb * W).reshape(nb, W)
    ki = (jnp.arange(nb)[:, None] - 1) * W + jnp.arange(3 * W)[None, :]
    valid = ((ki[:, None, :] >= 0) & (ki[:, None, :] < T)
             & (jnp.abs(qi[:, :, None] - ki[:, None, :]) <= radius))
    dist = jnp.abs(qp[:, :, :, None] - kp[:, :, None, :])
    s = jnp.einsum('bgrnqd,bgnkd->bgrnqk', qb, kb, preferred_element_type=jnp.float32)
    s = s - slope.astype(jnp.float32)[None, :, :, None, None, None] * dist[:, None, None]
    s = jnp.where(valid, s, NEG_INF)
    m = jnp.max(s, axis=-1)
    if sink is not None:
        sk = sink.astype(jnp.float32)[None, :, :, None, None]
        m = jnp.maximum(m, sk)
    p = jnp.exp(s - m[..., None])
    den = jnp.sum(p, axis=-1)
    if sink is not None:
        den = den + jnp.exp(sk - m)
    o = jnp.einsum('bgrnqk,bgnkd->bgrnqd', p.astype(v.dtype), vb,
                   preferred_element_type=jnp.float32) / den[..., None]
    lse = m + jnp.log(den)
    o = o.reshape(bsz, G, R, nb * W, -1)[:, :, :, :T]
    lse = lse.reshape(bsz, G, R, nb * W)[..., :T]
    return o, lse


def _dilated_mixture(q, k, v, pos, slopes):
    bsz, H, T, Dh = q.shape
    outs, lses = [], []
    for window, d in DILATED_BRANCHES:
        n = T // d

        def fold(a, d=d, n=n):
            return a.reshape(bsz, H, n, d, -1).transpose(0, 3, 1, 2, 4).reshape(bsz * d, H, n, a.shape[-1])

        pf = pos.reshape(bsz, n, d).transpose(0, 2, 1).reshape(bsz * d, n)
        o, lse = _banded_attention(fold(q)[:, :, None], fold(k), fold(v), pf,
                                   slopes[:, None], window // (2 * d))
        outs.append(o[:, :, 0].reshape(bsz, d, H, n, Dh).transpose(0, 2, 3, 1, 4).reshape(bsz, H, T, Dh))
        lses.append(lse[:, :, 0].reshape(bsz, d, H, n).transpose(0, 2, 3, 1).reshape(bsz, H, T))
    wts = jax.nn.softmax(jnp.stack(lses), axis=0)
    return jnp.einsum('ibht,ibhtd->bhtd', wts, jnp.stack(outs)).astype(q.dtype)


def _query_blocks(a):
    bsz, H, T, D = a.shape
    return a.reshape(bsz, H, T // QUERY_BLOCK, QUERY_BLOCK, D).transpose(2, 0, 1, 3, 4)


def _merge_query_blocks(o):
    nq, bsz, H, Q, D = o.shape
    return o.transpose(1, 2, 0, 3, 4).reshape(bsz, H, nq * Q, D)


def _diff_attention(q1, q2, k1, k2, v, pos, slopes, lam):
    bsz, H, T, _ = q1.shape
    pb = pos.reshape(bsz, T // QUERY_BLOCK, QUERY_BLOCK).transpose(1, 0, 2)

    def block(args):
        q1b, q2b, pq = args
        bias = -slopes[None, :, None, None] * jnp.abs(pq[:, None, :, None] - pos[:, None, None, :])
        a1 = jax.nn.softmax(jnp.einsum('bhqd,bhkd->bhqk', q1b, k1, preferred_element_type=jnp.float32) + bias, axis=-1)
        a2 = jax.nn.softmax(jnp.einsum('bhqd,bhkd->bhqk', q2b, k2, preferred_element_type=jnp.float32) + bias, axis=-1)
        return jnp.einsum('bhqk,bhkd->bhqd', (a1 - lam * a2).astype(v.dtype), v,
                          preferred_element_type=jnp.float32)

    return _merge_query_blocks(lax.map(block, (_query_blocks(q1), _query_blocks(q2), pb)))


def _mla_attention(q_nope, q_rope, k_nope, k_rope, v):
    def block(args):
        qn, qr = args
        s = (jnp.einsum('bhqd,bhkd->bhqk', qn, k_nope, preferred_element_type=jnp.float32)
             + jnp.einsum('bhqd,bkd->bhqk', qr, k_rope, preferred_element_type=jnp.float32))
        p = jax.nn.softmax(s, axis=-1)
        return jnp.einsum('bhqk,bhkd->bhqd', p.astype(v.dtype), v, preferred_element_type=jnp.float32)

    return _merge_query_blocks(lax.map(block, (_query_blocks(q_nope), _query_blocks(q_rope))))


def _hybrid_token_mixer(h, pos, cos, sin, layer, w_in, sink, lq1, lk1, lq2, lk2,
                        g_diff, g_q, g_kv, w_uq, w_ukv, w_out):
    bsz, T, _ = h.shape
    slopes = _alibi_slopes()
    sl_b = slopes[:B_Q_HEADS]
    sl_c = slopes[B_Q_HEADS:B_Q_HEADS + C_HEADS]
    sl_a = slopes[B_Q_HEADS + C_HEADS:]
    z = jnp.einsum('btd,de->bte', h, w_in)
    z_a, z_b, z_c, z_d = jnp.split(z, IN_SPLITS, axis=-1)

    qa, ka, va = [t.reshape(bsz, T, A_HEADS, HEAD_DIM).transpose(0, 2, 1, 3)
                  for t in jnp.split(z_a, 3, axis=-1)]
    o_a = _dilated_mixture(qa * HEAD_DIM ** -0.5, ka, va, pos, sl_a)
    o_a = o_a.transpose(0, 2, 1, 3).reshape(bsz, T, A_HEADS * HEAD_DIM)

    qb, kb, vb = jnp.split(z_b, (B_Q_HEADS * HEAD_DIM, (B_Q_HEADS + B_KV_HEADS) * HEAD_DIM), axis=-1)
    qb = qb.reshape(bsz, T, B_KV_HEADS, B_GROUP, HEAD_DIM).transpose(0, 2, 3, 1, 4) * HEAD_DIM ** -0.5
    kb = kb.reshape(bsz, T, B_KV_HEADS, HEAD_DIM).transpose(0, 2, 1, 3)
    vb = vb.reshape(bsz, T, B_KV_HEADS, HEAD_DIM).transpose(0, 2, 1, 3)
    o_b, _ = _banded_attention(qb, kb, vb, pos, sl_b.reshape(B_KV_HEADS, B_GROUP), B_RADIUS,
                               sink.reshape(B_KV_HEADS, B_GROUP))
    o_b = o_b.transpose(0, 3, 1, 2, 4).reshape(bsz, T, B_Q_HEADS * HEAD_DIM).astype(h.dtype)

    qc, kc, vc = jnp.split(z_c, (C_HEADS * 2 * C_QK_DIM, C_HEADS * 4 * C_QK_DIM), axis=-1)
    qc = qc.reshape(bsz, T, C_HEADS, 2, C_QK_DIM).transpose(3, 0, 2, 1, 4) * C_QK_DIM ** -0.5
    kc = kc.reshape(bsz, T, C_HEADS, 2, C_QK_DIM).transpose(3, 0, 2, 1, 4)
    vc = vc.reshape(bsz, T, C_HEADS, C_V_DIM).transpose(0, 2, 1, 3)
    lam_init = 0.8 - 0.6 * math.exp(-0.3 * layer)
    f32 = jnp.float32
    lam = (jnp.exp(jnp.sum(lq1.astype(f32) * lk1.astype(f32)))
           - jnp.exp(jnp.sum(lq2.astype(f32) * lk2.astype(f32))) + lam_init)
    o_c = _diff_attention(qc[0], qc[1], kc[0], kc[1], vc, pos, sl_c, lam)
    o_c = _rms_norm(o_c, g_diff) * (1.0 - lam_init)
    o_c = o_c.transpose(0, 2, 1, 3).reshape(bsz, T, C_HEADS * C_V_DIM).astype(h.dtype)

    cq, ckv, kr = jnp.split(z_d, (D_Q_RANK, D_Q_RANK + D_KV_RANK), axis=-1)
    qd = jnp.einsum('btr,re->bte', _rms_norm(cq, g_q), w_uq).reshape(bsz, T, D_HEADS, D_NOPE_DIM + D_ROPE_DIM)
    q_nope = qd[..., :D_NOPE_DIM]
    q_rope = _rope(qd[..., D_NOPE_DIM:], cos[:, :, None], sin[:, :, None])
    kv = jnp.einsum('btr,re->bte', _rms_norm(ckv, g_kv), w_ukv).reshape(bsz, T, D_HEADS, D_NOPE_DIM + D_V_DIM)
    k_nope, v_d = kv[..., :D_NOPE_DIM], kv[..., D_NOPE_DIM:]
    k_rope = _rope(kr, cos, sin)
    d_scale = (D_NOPE_DIM + D_ROPE_DIM) ** -0.5
    o_d = _mla_attention(q_nope.transpose(0, 2, 1, 3) * d_scale, q_rope.transpose(0, 2, 1, 3) * d_scale,
                         k_nope.transpose(0, 2, 1, 3), k_rope, v_d.transpose(0, 2, 1, 3))
    o_d = o_d.transpose(0, 2, 1, 3).reshape(bsz, T, D_HEADS * D_V_DIM).astype(h.dtype)

    mix = jnp.concatenate([o_a, o_b, o_c, o_d], axis=-1)
    return jnp.einsum('btm,md->btd', mix, w_out)


def _swiglu(h, w_gate_up, w_down):
    g, u = jnp.split(jnp.einsum('btd,df->btf', h, w_gate_up), 2, axis=-1)
    return jnp.einsum('btf,fd->btd', jax.nn.silu(g) * u, w_down)


def setup_inputs(seed: int = 0) -> dict:
    key = jax.random.key(seed)
    ks = jax.random.split(key, 23)
    L, D = DEPTH, D_MODEL
    f32 = jnp.float32

    def dense(k, shape, fan_in, gain=1.0):
        return gain * fan_in ** -0.5 * jax.random.normal(k, shape, f32)

    def norm_gain(k, n):
        return 1.0 + 0.05 * jax.random.normal(k, (L, n), f32)

    return {
        'x': jax.random.normal(ks[0], (BATCH, SEQ, D), f32),
        'c': jax.random.normal(ks[1], (BATCH, D), f32),
        'positions': (jax.random.randint(ks[2], (BATCH, 1), 0, SEQ, dtype=jnp.int32)
                      + jnp.arange(SEQ, dtype=jnp.int32)[None, :]),
        'w_ada': dense(ks[3], (L, D, 6 * D), D, 0.5),
        'b_ada': 0.01 * jax.random.normal(ks[4], (L, 6 * D), f32),
        'g_pre_mix': norm_gain(ks[5], D),
        'g_post_mix': norm_gain(ks[6], D),
        'w_in': dense(ks[7], (L, D, IN_WIDTH), D),
        'sink_logits': 0.5 * jax.random.normal(ks[8], (L, B_Q_HEADS), f32),
        'lam_q1': LAMBDA_STD * jax.random.normal(ks[9], (L, C_QK_DIM), f32),
        'lam_k1': LAMBDA_STD * jax.random.normal(ks[10], (L, C_QK_DIM), f32),
        'lam_q2': LAMBDA_STD * jax.random.normal(ks[11], (L, C_QK_DIM), f32),
        'lam_k2': LAMBDA_STD * jax.random.normal(ks[12], (L, C_QK_DIM), f32),
        'g_diff': norm_gain(ks[13], C_V_DIM),
        'g_mla_q': norm_gain(ks[14], D_Q_RANK),
        'g_mla_kv': norm_gain(ks[15], D_KV_RANK),
        'w_uq': dense(ks[16], (L, D_Q_RANK, D_HEADS * (D_NOPE_DIM + D_ROPE_DIM)), D_Q_RANK),
        'w_ukv': dense(ks[17], (L, D_KV_RANK, D_HEADS * (D_NOPE_DIM + D_V_DIM)), D_KV_RANK),
        'w_out': dense(ks[18], (L, MIX_WIDTH, D), MIX_WIDTH),
        'g_pre_ffn': norm_gain(ks[19], D),
        'g_post_ffn': norm_gain(ks[20], D),
        'w_gate_up': dense(ks[21], (L, D, 2 * D_FF), D),
        'w_down': dense(ks[22], (L, D_FF, D), D_FF),
    }


def reference(x, c, positions, w_ada, b_ada, g_pre_mix, g_post_mix, w_in, sink_logits,
              lam_q1, lam_k1, lam_q2, lam_k2, g_diff, g_mla_q, g_mla_kv, w_uq, w_ukv,
              w_out, g_pre_ffn, g_post_ffn, w_gate_up, w_down):
    pos = positions.astype(jnp.float32)
    cos, sin = _rope_tables(pos)
    c_act = jax.nn.silu(c)
    for layer in range(DEPTH):
        mod = jnp.einsum('bd,de->be', c_act, w_ada[layer]) + b_ada[layer]
        sh_m, sc_m, gt_m, sh_f, sc_f, gt_f = [m[:, None, :] for m in jnp.split(mod, 6, axis=-1)]
        h = _rms_norm(x, g_pre_mix[layer]) * (1.0 + sc_m) + sh_m
        mix = _hybrid_token_mixer(h, pos, cos, sin, layer, w_in[layer], sink_logits[layer],
                                  lam_q1[layer], lam_k1[layer], lam_q2[layer], lam_k2[layer],
                                  g_diff[layer], g_mla_q[layer], g_mla_kv[layer],
                                  w_uq[layer], w_ukv[layer], w_out[layer])
        x = x + gt_m * _rms_norm(mix, g_post_mix[layer])
        h = _rms_norm(x, g_pre_ffn[layer]) * (1.0 + sc_f) + sh_f
        x = x + gt_f * _rms_norm(_swiglu(h, w_gate_up[layer], w_down[layer]), g_post_ffn[layer])
    return x
```

```python
import math
from contextlib import ExitStack

import numpy as np
import concourse.bass as bass
import concourse.mybir as mybir
from concourse.bass_utils import run_bass_kernel_spmd

F32 = mybir.dt.float32
BF16 = mybir.dt.bfloat16
I32 = mybir.dt.int32
U8 = mybir.dt.uint8
AF = mybir.ActivationFunctionType
ALU = mybir.AluOpType
AX = mybir.AxisListType

D = 1024
NL = 2
DFF = 2816
INW = 2592
EPS = 1e-6
SLOPES = [2.0 ** (-8.0 * j / 12.0) for j in range(1, 13)]
SL_B = SLOPES[0:4]
SL_C = SLOPES[4:8]
SL_A = SLOPES[8:12]
MAGIC = 12582912.0
TWO_PI = 2.0 * math.pi
C1 = 6.28125
C2 = TWO_PI - C1


class Tile:
    __slots__ = ("name", "lw", "rd")

    def __init__(self, name):
        self.name = name
        self.lw = None
        self.rd = {}


class Sched:
    ENGS = ("pe", "act", "dve", "pool", "sp")

    def __init__(self):
        self.ops = {e: [] for e in self.ENGS}
        self.counts = {"pe": 0, "act": 0, "dve": 0, "pool": 0}
        self.waited = {e: {} for e in self.ENGS}
        self.dma_keys = []
        self.n = 0

    def op(self, eng, fn, reads=(), writes=(), dma_tile=None):
        deps = {}
        for t in reads:
            if t.lw is not None and deps.get(t.lw[0], 0) < t.lw[1]:
                deps[t.lw[0]] = t.lw[1]
        for t in writes:
            if t.lw is not None and deps.get(t.lw[0], 0) < t.lw[1]:
                deps[t.lw[0]] = t.lw[1]
            for k, v in t.rd.items():
                if deps.get(k, 0) < v:
                    deps[k] = v
        if eng == "pe":
            deps.pop("pe", None)
        waits = []
        wd = self.waited[eng]
        for k, v in deps.items():
            if wd.get(k, 0) < v:
                wd[k] = v
                waits.append((k, v))
        if dma_tile is not None:
            key = "dma:" + dma_tile.name
            if key not in self.counts:
                self.counts[key] = 0
                self.dma_keys.append(key)
            self.counts[key] += 16
            inc = (key, 16)
        else:
            key = eng
            self.counts[key] += 1
            inc = (key, 1)
        val = self.counts[key]
        for t in writes:
            t.lw = (key, val)
            t.rd = {}
        for t in reads:
            if t.rd.get(key, 0) < val:
                t.rd[key] = val
        self.ops[eng].append((fn, waits, inc))
        self.n += 1

    def barrier(self):
        for e in self.ENGS:
            waits = []
            wd = self.waited[e]
            for k, v in self.counts.items():
                if v > 0 and wd.get(k, 0) < v:
                    wd[k] = v
                    waits.append((k, v))
            if waits:
                self.ops[e].append((None, waits, None))

    def emit(self, nc):
        keys = ["pe", "act", "dve", "pool"] + self.dma_keys
        with ExitStack() as es:
            sems = {}
            for i, k in enumerate(keys):
                sems[k] = es.enter_context(nc.semaphore("s%d" % i))
            block = es.enter_context(nc.Block())
            engmap = {"pe": block.tensor, "act": block.scalar, "dve": block.vector,
                      "pool": block.gpsimd, "sp": block.sync}
            for e in self.ENGS:
                ops = self.ops[e]

                def body(eng, ops=ops):
                    for fn, waits, inc in ops:
                        for k, v in waits:
                            eng.wait_ge(sems[k], v)
                        if fn is not None:
                            fn(eng).then_inc(sems[inc[0]], inc[1])
                engmap[e](body)


class Rot:
    def __init__(self, items):
        self.items = items
        self.i = 0

    def next(self):
        it = self.items[self.i % len(self.items)]
        self.i += 1
        return it


def build(T, nlayers=NL, stop_after=None, debug=False):
    NB = T // 128
    NCH = T // 512
    nc = bass.Bass("TRN2", target_bir_lowering=False)

    def din(name, shape, dt=F32):
        return nc.dram_tensor(name, list(shape), dt, kind="ExternalInput").ap()

    x_d = din("x", [T, D])
    c_d = din("c", [128, 8])
    posrow_d = din("posrow", [128, T], I32)
    poscol_d = din("poscol", [128, NB], I32)
    w_ada_d = din("w_ada", [NL, D, 6 * D])
    b_ada_d = din("b_ada", [NL, 128, 48])
    g_pre_mix_d = din("g_pre_mix", [NL, 128, 8])
    g_post_mix_d = din("g_post_mix", [NL, 128, 8])
    g_pre_ffn_d = din("g_pre_ffn", [NL, 128, 8])
    g_post_ffn_d = din("g_post_ffn", [NL, 128, 8])
    w_in_d = din("w_in", [NL, D, INW])
    sink_d = din("sink", [NL, 128, 4])
    lam_d = din("lam", [NL, 128, 4, 32])
    g_diff_d = din("g_diff", [NL, 128, 64])
    g_q_d = din("g_q", [NL, 128, 3])
    g_kv_d = din("g_kv", [NL, 128, 1])
    w_uq_d = din("w_uq", [NL, 384, 384])
    w_ukv_d = din("w_ukv", [NL, 128, 512])
    w_out_d = din("w_out", [NL, D, D])
    w_gu_d = din("w_gu", [NL, D, 2 * DFF])
    w_dn_d = din("w_dn", [NL, DFF, D])
    maskA_d = din("maskA", [128, 20, 512])
    maskB_d = din("maskB", [128, 6, 512])
    ident_d = din("ident", [128, 128])
    invf_d = din("invf", [128, 1])
    out_d = nc.dram_tensor("out", [T, D], F32, kind="ExternalOutput").ap()
    xT_d = nc.dram_tensor("xT_scr", [8, 128, T], F32, kind="ExternalOutput" if debug else "Internal").ap()
    mix_d = nc.dram_tensor("mix_scr", [T, D], BF16, kind="Internal").ap()
    if debug:
        mixdbg_d = nc.dram_tensor("mix_dbg", [T, D], F32, kind="ExternalOutput").ap()
        hdbg_d = nc.dram_tensor("h_dbg", [8, 128, T], F32, kind="ExternalOutput").ap()

    S = Sched()
    es = ExitStack()
    ARENA = 207 * 1024
    arena = es.enter_context(nc.sbuf_tensor("arena", [128, ARENA], U8))
    psum = es.enter_context(nc.psum_tensor("psum", [128, 8, 512], F32))
    bank_t = [Tile("bank%d" % i) for i in range(8)]

    def bank(i):
        return psum[:, i, :]

    class Alloc:
        def __init__(self, lo, hi):
            self.lo, self.hi, self.p = lo, hi, lo

        def __call__(self, name, shape, dt):
            esz = {F32: 4, BF16: 2, I32: 4}[dt]
            n = int(np.prod(shape)) * esz
            n = (n + 63) // 64 * 64
            assert self.p + n <= self.hi, (name, self.p, n, self.hi)
            ap = arena[:, self.p:self.p + n].bitcast(dt)
            ap = ap[:, 0:int(np.prod(shape))]
            if len(shape) == 2:
                ap = ap.rearrange("p (a b) -> p a b", a=shape[0])
            elif len(shape) == 3:
                ap = ap.rearrange("p (a b c) -> p a b c", a=shape[0], b=shape[1])
            elif len(shape) == 4:
                ap = ap.rearrange("p (a b c d) -> p a b c d", a=shape[0], b=shape[1], c=shape[2])
            self.p += n
            return Tile(name), ap

    CONST_SZ = 4 * 1024
    ca = Alloc(ARENA - CONST_SZ, ARENA)
    identb_t, identb = ca("identb", [128], BF16)
    identf_t, identf = ca("identf", [128], F32)
    onesb_t, onesb = ca("onesb", [128], BF16)
    eps_t, eps_ap = ca("eps", [1], F32)
    cact_t, cact = ca("cact", [8], BF16)
    cf_t, cf = ca("cf", [8], F32)
    modT_t, modT = ca("modT", [48], F32)
    gs_t, gs = ca("gs", [4, 8], F32)
    graw_t, graw = ca("graw", [4, 8], F32)
    poscol_t, poscol = ca("poscol", [NB], F32)
    nposcol_t, nposcol = ca("nposcol", [NB], F32)
    posci_t, posci = ca("posci", [NB], I32)
    invf_t, invf = ca("invf", [1], F32)
    small_t, small = ca("small", [64], F32)
    lamraw_t, lamraw = ca("lamraw", [4, 32], F32)
    gdf_t, gdf = ca("gdf", [64], F32)
    gq_t, gq = ca("gq", [3], F32)
    gkv_t, gkv = ca("gkv", [1], F32)
    ARENA_W = ARENA - CONST_SZ

    def dma(eng, out, in_, reads=(), writes=(), tile=None):
        S.op(eng, lambda e: e.dma_start(out=out, in_=in_), reads=reads, writes=writes, dma_tile=tile)

    def mm(out, lhsT, rhs, start, stop, reads, writes):
        S.op("pe", lambda e: e.matmul(out, lhsT, rhs, start=start, stop=stop, skip_group_check=True),
             reads=reads, writes=writes)

    def tr(out, in_, ident, reads, writes):
        S.op("pe", lambda e: e.transpose(out, in_, ident), reads=reads, writes=writes)

    def act(out, in_, func, reads, writes, scale=1.0, bias=None, eng="act"):
        if bias is None:
            S.op(eng, lambda e: e.activation(out=out, in_=in_, func=func, scale=scale), reads=reads, writes=writes)
        else:
            S.op(eng, lambda e: e.activation(out=out, in_=in_, func=func, scale=scale, bias=bias),
                 reads=reads, writes=writes)

    def tt(eng, out, in0, in1, op, reads, writes):
        S.op(eng, lambda e: e.tensor_tensor(out=out, in0=in0, in1=in1, op=op), reads=reads, writes=writes)

    def ts(eng, out, in0, s1, op0, reads, writes, s2=None, op1=None):
        if op1 is None:
            S.op(eng, lambda e: e.tensor_scalar(out=out, in0=in0, scalar1=s1, scalar2=None, op0=op0),
                 reads=reads, writes=writes)
        else:
            S.op(eng, lambda e: e.tensor_scalar(out=out, in0=in0, scalar1=s1, scalar2=s2, op0=op0, op1=op1),
                 reads=reads, writes=writes)

    def stt(out, in0, scalar, in1, op0, op1, reads, writes):
        S.op("dve", lambda e: e.scalar_tensor_tensor(out=out, in0=in0, scalar=scalar, in1=in1, op0=op0, op1=op1),
             reads=reads, writes=writes)

    def cp(eng, out, in_, reads, writes):
        S.op(eng, lambda e: e.tensor_copy(out=out, in_=in_), reads=reads, writes=writes)

    def memset(eng, ap, val, writes):
        S.op(eng, lambda e: e.memset(ap, val), writes=writes)

    def recip(out, in_, reads, writes):
        S.op("dve", lambda e: e.reciprocal(out=out, in_=in_), reads=reads, writes=writes)

    def rsum(out, in_, reads, writes):
        S.op("dve", lambda e: e.reduce_sum(out=out, in_=in_, axis=AX.X), reads=reads, writes=writes)

    def rstd_from(out, in_, n, reads, writes):
        act(out, in_, AF.Ln, reads + [eps_t], writes, scale=1.0 / n, bias=eps_ap[:, 0:1])
        act(out, out, AF.Exp, writes, writes, scale=-0.5)

    dma("pool", identb, ident_d, writes=[identb_t], tile=identb_t)
    dma("sp", identf, ident_d, writes=[identf_t], tile=identf_t)
    memset("dve", onesb, 1.0, [onesb_t])
    memset("dve", eps_ap, EPS, [eps_t])
    dma("sp", cf, c_d, writes=[cf_t], tile=cf_t)
    act(cact, cf, AF.Silu, [cf_t], [cact_t])
    dma("sp", posci, poscol_d, writes=[posci_t], tile=posci_t)
    cp("dve", poscol, posci, [posci_t], [poscol_t])
    ts("dve", nposcol, poscol, -1.0, ALU.mult, [poscol_t], [nposcol_t])
    dma("sp", invf, invf_d, writes=[invf_t], tile=invf_t)

    def phase_x_in():
        a = Alloc(0, ARENA_W)
        xin = Rot([a("xin%d" % i, [D], F32) for i in range(3)])
        xo = Rot([a("xo%d" % i, [8, 128], F32) for i in range(3)])
        pbk = Rot([(bank_t[i], i) for i in range(4)])
        for b in range(NB):
            xi_t, xi = xin.next()
            dma("sp", xi, x_d[b * 128:(b + 1) * 128, :], writes=[xi_t], tile=xi_t)
            xo_t, xoap = xo.next()
            for half in range(2):
                bt, bi = pbk.next()
                for kk in range(4):
                    k = half * 4 + kk
                    tr(psum[:, bi, kk * 128:(kk + 1) * 128], xi[:, k * 128:(k + 1) * 128], identf,
                       [xi_t, identf_t], [bt])
                eng = "dve" if half == 0 else "act"
                if eng == "dve":
                    cp("dve", xoap[:, half * 4:(half + 1) * 4, :],
                       psum[:, bi, :].rearrange("p (a b) -> p a b", a=4), [bt], [xo_t])
                else:
                    act(xoap[:, half * 4:(half + 1) * 4, :],
                        psum[:, bi, :].rearrange("p (a b) -> p a b", a=4), AF.Copy, [bt], [xo_t])
            dma("sp", xT_d[:, :, b * 128:(b + 1) * 128].rearrange("k p t -> p k t"), xoap,
                reads=[xo_t], tile=xo_t)
        return

    xT_dt = Tile("xT_dram")
    xTc_t = [Tile("xTd%d" % c) for c in range(NCH)]
    mixc_t = [Tile("mixd%d" % c) for c in range(NCH)]

    phase_x_in()
    S.barrier()

    def load_layer_small(l):
        dma("sp", graw[:, 0, :], g_pre_mix_d[l], writes=[graw_t], tile=graw_t)
        dma("sp", graw[:, 1, :], g_post_mix_d[l], writes=[graw_t], tile=graw_t)
        dma("sp", graw[:, 2, :], g_pre_ffn_d[l], writes=[graw_t], tile=graw_t)
        dma("sp", graw[:, 3, :], g_post_ffn_d[l], writes=[graw_t], tile=graw_t)
        dma("sp", modT, b_ada_d[l], writes=[modT_t], tile=modT_t)
        dma("sp", lamraw, lam_d[l], writes=[lamraw_t], tile=lamraw_t)
        dma("sp", gdf, g_diff_d[l], writes=[gdf_t], tile=gdf_t)
        dma("sp", gq, g_q_d[l], writes=[gq_t], tile=gq_t)
        dma("sp", gkv, g_kv_d[l], writes=[gkv_t], tile=gkv_t)
        dma("sp", small[:, 0:4], sink_d[l], writes=[small_t], tile=small_t)

    def phase_mod(l):
        a = Alloc(0, ARENA_W)
        wa = Rot([a("wada%d" % i, [8, 1024], BF16) for i in range(2)])
        bt = bank_t[0]
        for piece in range(6):
            w_t, w = wa.next()
            dma("pool", w, w_ada_d[l, :, piece * 1024:(piece + 1) * 1024].rearrange("(k p) c -> p k c", p=128),
                writes=[w_t], tile=w_t)
            for jj in range(8):
                j = piece * 8 + jj
                for k in range(8):
                    mm(psum[:, 0, j:j + 1], w[:, k, jj * 128:(jj + 1) * 128], cact[:, k:k + 1],
                       k == 0, k == 7, [w_t, cact_t], [bt])
        tt("dve", modT, modT, psum[:, 0, 0:48], ALU.add, [modT_t, bt], [modT_t])
        stt(gs[:, 0, :], modT[:, 8:16], 1.0, graw[:, 0, :], ALU.add, ALU.mult, [modT_t, graw_t], [gs_t])
        tt("dve", gs[:, 1, :], modT[:, 16:24], graw[:, 1, :], ALU.mult, [modT_t, graw_t], [gs_t])
        stt(gs[:, 2, :], modT[:, 32:40], 1.0, graw[:, 2, :], ALU.add, ALU.mult, [modT_t, graw_t], [gs_t])
        tt("dve", gs[:, 3, :], modT[:, 40:48], graw[:, 3, :], ALU.mult, [modT_t, graw_t], [gs_t])
        act(small[:, 4:8], small[:, 0:4], AF.Exp, [small_t], [small_t])
        tt("dve", lamraw[:, 0, :], lamraw[:, 0, :], lamraw[:, 1, :], ALU.mult, [lamraw_t], [lamraw_t])
        tt("dve", lamraw[:, 2, :], lamraw[:, 2, :], lamraw[:, 3, :], ALU.mult, [lamraw_t], [lamraw_t])
        rsum(small[:, 10:11], lamraw[:, 0, :], [lamraw_t], [small_t])
        rsum(small[:, 11:12], lamraw[:, 2, :], [lamraw_t], [small_t])
        act(small[:, 10:12], small[:, 10:12], AF.Exp, [small_t], [small_t])
        lam_init = 0.8 - 0.6 * math.exp(-0.3 * l)
        tt("dve", small[:, 8:9], small[:, 10:11], small[:, 11:12], ALU.subtract, [small_t], [small_t])
        ts("dve", small[:, 8:9], small[:, 8:9], lam_init, ALU.add, [small_t], [small_t])
        ts("dve", small[:, 9:10], small[:, 8:9], -1.0, ALU.mult, [small_t], [small_t])
        ts("dve", gdf, gdf, 1.0 - lam_init, ALU.mult, [gdf_t], [gdf_t])

    OFF_HT = 0
    OFF_KT = OFF_HT + 16 * T
    OFF_VA = OFF_KT + 8 * T
    OFF_MR = OFF_VA + NB * 4 * 65 * 2 + 64
    MR_SZ = max(32 * 1024, 8 * T)
    OFF_PR = OFF_MR + MR_SZ
    OFF_WK = OFF_PR + max(4 * T, 16384)
    hT_t, hT = Alloc(OFF_HT, OFF_KT)("hT", [8, T], BF16)
    posrow_t, posrow = Alloc(OFF_PR, OFF_WK)("posrow", [T], F32)

    def norm_to_h(c, src_t, src, gidx, shcol, a_work, dst_t, dst, bt_s, bi_s):
        sq = a_work["sq"]
        for k in range(8):
            sq_t, sqap = sq.next()
            act(sqap, src[:, k, :], AF.Square, [src_t], [sq_t], eng="act")
            mm(psum[:, bi_s, :], onesb, sqap, k == 0, k == 7, [onesb_t, sq_t], [bt_s])
        r_t, r = a_work["rstd"].next()
        rstd_from(r, psum[:, bi_s, :], float(D), [bt_s], [r_t])
        for k in range(8):
            tmp_t, tmp = a_work["tmp"].next()
            stt(tmp, src[:, k, :], gs[:, gidx, k:k + 1], r, ALU.mult, ALU.mult, [src_t, gs_t, r_t], [tmp_t])
            act(dst(k), tmp, AF.Identity, [tmp_t, modT_t], [dst_t], bias=modT[:, shcol + k:shcol + k + 1],
                eng="act")

    def phase_norm1(l):
        a2 = Alloc(OFF_KT, OFF_PR)
        xs = Rot([a2("xs%d" % i, [8, 512], F32) for i in range(2)])
        a = Alloc(OFF_WK, ARENA_W)
        w = {"sq": Rot([a("sq%d" % i, [512], BF16) for i in range(3)]),
             "rstd": Rot([a("rstd%d" % i, [512], F32) for i in range(2)]),
             "tmp": Rot([a("tmp%d" % i, [512], F32) for i in range(3)])}
        dma("sp", posrow.bitcast(I32), posrow_d, writes=[posrow_t], tile=posrow_t)
        cp("pool", posrow, posrow.bitcast(I32), [posrow_t], [posrow_t])
        sb = Rot([(bank_t[4], 4), (bank_t[5], 5)])
        for c in range(NCH):
            x_t, xap = xs.next()
            dma("sp", xap, xT_d[:, :, c * 512:(c + 1) * 512].rearrange("k p t -> p k t"), writes=[x_t], tile=x_t)
            bt_s, bi_s = sb.next()
            norm_to_h(c, x_t, xap, 0, 0, w, hT_t, lambda k, c=c: hT[:, k, c * 512:(c + 1) * 512], bt_s, bi_s)

    class AttnCtx:
        def __init__(self, a, bias=True, a_pt=None, sbanks=(0, 1, 2), depth=2):
            self.sb = Rot([(bank_t[i], i) for i in sbanks])
            self.pending = []
            self.depth = depth
            if bias:
                self.tt_ = Rot([a("tt%d" % i, [512], F32) for i in range(depth + 2)])
                self.dt = Rot([a("dt%d" % i, [512], F32) for i in range(2)])
            ap_ = a_pt if a_pt is not None else a
            self.pt = Rot([ap_("pt%d" % i, [512], BF16) for i in range(depth + 2)])

    def flush(ctx):
        while ctx.pending:
            ctx.pending.pop(0)()

    def dist_tile(ctx, c0, W, j):
        d_t, d = ctx.dt.next()
        act(d[:, 0:W], posrow[:, c0:c0 + W], AF.Abs, [posrow_t, nposcol_t], [d_t], bias=nposcol[:, j:j + 1])
        return d_t, d

    def attn_step(ctx, qk, slope, dtile, mask, W, G, vaug, acc, first, last):
        st, si = ctx.sb.next()
        pe_mask = mask is not None and mask[2] == "pe"
        if len(qk) == 1 and G > 1:
            lhsT, rhs, rds = qk[0]
            mm(psum[:, si, 0:G * W], lhsT, rhs, True, True, rds, [st])
        else:
            for g in range(G):
                lhsT, rhs, rds = qk[g]
                mm(psum[:, si, g * W:(g + 1) * W], lhsT, rhs, True, not pe_mask, rds, [st])
        if pe_mask:
            mm(psum[:, si, 0:W], identb, mask[0], False, True, [identb_t, mask[1]], [st])
            mask = None
        if slope is not None:
            d_t, d = dtile
            t_t, tap = ctx.tt_.next()
            if G == 1:
                stt(tap, d[:, 0:W], -slope, psum[:, si, :], ALU.mult, ALU.add, [d_t, st], [t_t])
            else:
                stt(tap.rearrange("p (g w) -> p g w", g=G), d[:, 0:W].unsqueeze(1).broadcast_to([128, G, W]),
                    -slope, psum[:, si, :].rearrange("p (g w) -> p g w", g=G), ALU.mult, ALU.add, [d_t, st], [t_t])
            src, src_t = tap, t_t
        else:
            src, src_t = psum[:, si, :], st
        p_t, p = ctx.pt.next()
        act(p, src, AF.Exp, [src_t], [p_t])
        if mask is not None:
            m_ap, m_t, m_eng = mask
            tt(m_eng, p, p, m_ap, ALU.mult, [p_t, m_t], [p_t])
        nsub = W // 128

        def back():
            for g in range(G):
                v_ap, v_rd = vaug[g]
                a_ap, a_t = acc[g]
                for s in range(nsub):
                    mm(a_ap[:, s, :], p[:, g * W + s * 128:g * W + (s + 1) * 128], v_ap,
                       first and g == 0 and s == 0, last, [p_t] + v_rd, [a_t])
        ctx.pending.append(back)
        if len(ctx.pending) > ctx.depth:
            ctx.pending.pop(0)()

    def proj_fm(bi, bt, w, w_t, col0, M, c, extra_cols=None):
        for k in range(8):
            mm(psum[:M, bi, :], w[:, k, col0:col0 + M], hT[:, k, c * 512:(c + 1) * 512], k == 0, k == 7,
               [w_t, hT_t], [bt])

    def store_mix(ost_t, ost, c0, ntok, col0, ncols):
        nsub = ntok // 128
        dma("sp", mix_d[c0:c0 + ntok, col0:col0 + ncols].rearrange("(s p) c -> p s c", p=128), ost,
            reads=[ost_t], tile=ost_t)

    def phase_A(l):
        kt_t, KT = Alloc(OFF_KT, OFF_VA)("KTa", [2, T], BF16)
        va_t, VA = Alloc(OFF_VA, OFF_MR)("VAa", [NB, 4, 65], BF16)
        mk_t, MK = Alloc(OFF_MR, OFF_PR)("maskA", [20, 512], BF16)
        a = Alloc(OFF_WK, ARENA_W)
        w_t, w = a("wA", [8, 768], BF16)
        ctx = AttnCtx(a, sbanks=(0, 1, 2, 7), depth=3)
        qts = Rot([a("qtA%d" % i, [4, 512], BF16) for i in range(2)])
        for q_t, q in qts.items:
            memset("pool", q, 0.0, [q_t])
        a_sp = Alloc(OFF_MR + 26 * 1024, OFF_PR)
        osts = Rot([a_sp("ostA%d" % i, [4, 256], BF16) for i in range(2)])
        rec_t, rec = a("recA", [4, 4], F32)
        dma("pool", w, w_in_d[l, :, 0:768].rearrange("(k p) c -> p k c", p=128), writes=[w_t], tile=w_t)
        dma("pool", MK, maskA_d, writes=[mk_t], tile=mk_t)
        memset("pool", VA[:, :, :, 64:65], 1.0, [va_t])
        pb = Rot([(bank_t[i], i) for i in range(3)])
        for c in range(NCH):
            for pr in range(2):
                bt, bi = pb.next()
                proj_fm(bi, bt, w, w_t, 256 + pr * 128, 128, c)
                cp("dve", KT[:, pr, c * 512:(c + 1) * 512], psum[:, bi, :], [bt], [kt_t])
        for b in range(NB):
            bt, bi = pb.next()
            for k in range(8):
                mm(psum[:, bi, 0:256], hT[:, k, b * 128:(b + 1) * 128], w[:, k, 512:768], k == 0, k == 7,
                   [hT_t, w_t], [bt])
            act(VA[:, b, :, 0:64], psum[:, bi, 0:256].rearrange("p (h d) -> p h d", h=4), AF.Copy, [bt], [va_t])
        for c in range(NCH):
            q_t, q = qts.next()
            for pr in range(2):
                qb_t, qbi = ctx.sb.next()
                proj_fm(qbi, qb_t, w, w_t, pr * 128, 128, c)
                act(q[0:64, 2 * pr, :], psum[0:64, qbi, :], AF.Identity, [qb_t], [q_t], scale=0.125)
                act(q[64:128, 2 * pr + 1, :], psum[64:128, qbi, :], AF.Identity, [qb_t], [q_t], scale=0.125)
            j0 = max(0, 4 * c - 8)
            j1 = min(NB - 1, 4 * c + 11)
            d_next = dist_tile(ctx, c * 512, 512, j0)
            for j in range(j0, j1 + 1):
                dtile = d_next
                if j < j1:
                    d_next = dist_tile(ctx, c * 512, 512, j + 1)
                o = j - 4 * c + 8
                for h in range(4):
                    hp, pr = (h % 2) * 64, h // 2
                    attn_step(ctx,
                              [(KT[:, pr, j * 128:(j + 1) * 128], q[:, h, :], [kt_t, q_t])],
                              SL_A[h], dtile, (MK[:, o, :], mk_t, "pe"), 512, 1,
                              [(VA[:, j, h, :], [va_t])],
                              [(psum[:, 3 + h, 0:260].rearrange("p (s e) -> p s e", s=4), bank_t[3 + h])],
                              j == j0, j == j1)
            flush(ctx)
            o_t, ost = osts.next()
            for h in range(4):
                accv = psum[:, 3 + h, 0:260].rearrange("p (s e) -> p s e", s=4)
                recip(rec[:, h, :], accv[:, :, 64], [bank_t[3 + h]], [rec_t])
                tt("dve", ost[:, :, h * 64:(h + 1) * 64], accv[:, :, 0:64],
                   rec[:, h, :].unsqueeze(2).broadcast_to([128, 4, 64]), ALU.mult, [bank_t[3 + h], rec_t], [o_t])
            store_mix(o_t, ost, c * 512, 512, 0, 256)

    def phase_B(l):
        kt_t, KT = Alloc(OFF_KT, OFF_VA)("KTb", [T], BF16)
        va_t, VA = Alloc(OFF_VA, OFF_MR)("VAb", [NB, 2, 65], BF16)
        mk_t, MK = Alloc(OFF_MR + 20 * 1024, OFF_PR)("maskB", [6, 512], BF16)
        a = Alloc(OFF_WK, ARENA_W)
        w_t, w = a("wB", [8, 512], BF16)
        ctx = AttnCtx(a, sbanks=(0, 1, 2, 7), depth=3)
        qts = Rot([a("qtB%d" % i, [4, 512], BF16) for i in range(2)])
        for q_t, q in qts.items:
            memset("pool", q, 0.0, [q_t])
        a_sp = Alloc(OFF_MR + 26 * 1024, OFF_PR)
        osts = Rot([a_sp("ostB%d" % i, [4, 256], BF16) for i in range(2)])
        rec_t, rec = a("recB", [4, 4], F32)
        for r in range(2):
            for g in range(2):
                dma("pool", w[:, :, r * 128 + g * 64:r * 128 + (g + 1) * 64],
                    w_in_d[l, :, 768 + (g * 2 + r) * 64:768 + (g * 2 + r + 1) * 64].rearrange("(k p) c -> p k c", p=128),
                    writes=[w_t], tile=w_t)
        dma("pool", w[:, :, 256:512], w_in_d[l, :, 1024:1280].rearrange("(k p) c -> p k c", p=128),
            writes=[w_t], tile=w_t)
        dma("pool", MK, maskB_d, writes=[mk_t], tile=mk_t)
        memset("pool", VA[:, :, :, 64:65], 1.0, [va_t])
        pb = Rot([(bank_t[i], i) for i in range(3)])
        for c in range(NCH):
            bt, bi = pb.next()
            proj_fm(bi, bt, w, w_t, 256, 128, c)
            cp("dve", KT[:, c * 512:(c + 1) * 512], psum[:, bi, :], [bt], [kt_t])
        for b in range(NB):
            bt, bi = pb.next()
            for k in range(8):
                mm(psum[:, bi, 0:128], hT[:, k, b * 128:(b + 1) * 128], w[:, k, 384:512], k == 0, k == 7,
                   [hT_t, w_t], [bt])
            act(VA[:, b, :, 0:64], psum[:, bi, 0:128].rearrange("p (h d) -> p h d", h=2), AF.Copy, [bt], [va_t])
        for c in range(NCH):
            q_t, q = qts.next()
            for r in range(2):
                qb_t, qbi = ctx.sb.next()
                proj_fm(qbi, qb_t, w, w_t, r * 128, 128, c)
                act(q[0:64, r, :], psum[0:64, qbi, :], AF.Identity, [qb_t], [q_t], scale=0.125)
                act(q[64:128, 2 + r, :], psum[64:128, qbi, :], AF.Identity, [qb_t], [q_t], scale=0.125)
            j0 = max(0, 4 * c - 1)
            j1 = min(NB - 1, 4 * c + 4)
            d_next = dist_tile(ctx, c * 512, 512, j0)
            for j in range(j0, j1 + 1):
                dtile = d_next
                if j < j1:
                    d_next = dist_tile(ctx, c * 512, 512, j + 1)
                o = j - 4 * c + 1
                for h in range(4):
                    g, r = h // 2, h % 2
                    attn_step(ctx,
                              [(KT[:, j * 128:(j + 1) * 128], q[:, h, :], [kt_t, q_t])],
                              SL_B[h], dtile, (MK[:, o, :], mk_t, "pe"), 512, 1,
                              [(VA[:, j, g, :], [va_t])],
                              [(psum[:, 3 + h, 0:260].rearrange("p (s e) -> p s e", s=4), bank_t[3 + h])],
                              j == j0, j == j1)
            flush(ctx)
            o_t, ost = osts.next()
            for h in range(4):
                accv = psum[:, 3 + h, 0:260].rearrange("p (s e) -> p s e", s=4)
                ts("dve", rec[:, h, :], accv[:, :, 64], small[:, 4 + h:5 + h], ALU.add,
                   [bank_t[3 + h], small_t], [rec_t])
                recip(rec[:, h, :], rec[:, h, :], [rec_t], [rec_t])
                tt("dve", ost[:, :, h * 64:(h + 1) * 64], accv[:, :, 0:64],
                   rec[:, h, :].unsqueeze(2).broadcast_to([128, 4, 64]), ALU.mult, [bank_t[3 + h], rec_t], [o_t])
            store_mix(o_t, ost, c * 512, 512, 256, 256)

    def phase_C(l):
        kt_t, KT = Alloc(OFF_KT, OFF_VA)("KTc", [2, T], BF16)
        va_t, VA = Alloc(OFF_VA, OFF_MR)("VAc", [NB, 4, 65], BF16)
        a = Alloc(OFF_WK, ARENA_W)
        w_t, w = a("wC", [8, 768], BF16)
        ctx = AttnCtx(a, sbanks=(0, 1, 2, 7), depth=3)
        qts = Rot([a("qtC%d" % i, [8, 256], BF16) for i in range(2)])
        for q_t, q in qts.items:
            memset("pool", q, 0.0, [q_t])
        osts = Rot([a("ostC%d" % i, [2, 256], BF16) for i in range(2)])
        rec_t, rec = a("recC", [4, 8], F32)
        o1_t, o1 = a("o1C", [64], F32)
        o2_t, o2 = a("o2C", [64], F32)
        sqc_t, sqc = a("sqC", [64], F32)
        ss_t, ss = a("ssC", [2], F32)
        dma("pool", w, w_in_d[l, :, 1280:2048].rearrange("(k p) c -> p k c", p=128), writes=[w_t], tile=w_t)
        memset("pool", VA[:, :, :, 64:65], 1.0, [va_t])
        pb = Rot([(bank_t[i], i) for i in range(3)])
        for c in range(NCH):
            for pr in range(2):
                bt, bi = pb.next()
                proj_fm(bi, bt, w, w_t, 256 + pr * 128, 128, c)
                cp("dve", KT[:, pr, c * 512:(c + 1) * 512], psum[:, bi, :], [bt], [kt_t])
        for b in range(NB):
            bt, bi = pb.next()
            for k in range(8):
                mm(psum[:, bi, 0:256], hT[:, k, b * 128:(b + 1) * 128], w[:, k, 512:768], k == 0, k == 7,
                   [hT_t, w_t], [bt])
            act(VA[:, b, :, 0:64], psum[:, bi, 0:256].rearrange("p (h d) -> p h d", h=4), AF.Copy, [bt], [va_t])
        qscale = 32.0 ** -0.5
        for c2 in range(T // 256):
            q_t, q = qts.next()
            for pr in range(2):
                qb_t, qbi = ctx.sb.next()
                for k in range(8):
                    mm(psum[:, qbi, 0:256], w[:, k, pr * 128:(pr + 1) * 128], hT[:, k, c2 * 256:(c2 + 1) * 256],
                       k == 0, k == 7, [w_t, hT_t], [qb_t])
                for blk in range(4):
                    eng = "dve" if blk % 2 == 0 else "act"
                    if eng == "dve":
                        ts("dve", q[blk * 32:(blk + 1) * 32, pr * 4 + blk, :], psum[blk * 32:(blk + 1) * 32, qbi, 0:256],
                           qscale, ALU.mult, [qb_t], [q_t])
                    else:
                        act(q[blk * 32:(blk + 1) * 32, pr * 4 + blk, :], psum[blk * 32:(blk + 1) * 32, qbi, 0:256],
                            AF.Identity, [qb_t], [q_t], scale=qscale)
            d_next = dist_tile(ctx, c2 * 256, 256, 0)
            for j in range(NB):
                dtile = d_next
                if j < NB - 1:
                    d_next = dist_tile(ctx, c2 * 256, 256, j + 1)
                for h in range(4):
                    hp, pr = (h % 2) * 64, h // 2
                    accv = psum[:, 3 + h, 0:260].rearrange("p (m s e) -> p m s e", m=2, s=2)
                    attn_step(ctx,
                              [(KT[:, pr, j * 128:(j + 1) * 128], q[:, h * 2:h * 2 + 2, :], [kt_t, q_t])],
                              SL_C[h], dtile, None, 256, 2,
                              [(VA[:, j, h, :], [va_t])] * 2,
                              [(accv[:, m, :, :], bank_t[3 + h]) for m in range(2)],
                              j == 0, j == NB - 1)
            flush(ctx)
            o_t, ost = osts.next()
            for h in range(4):
                bt = bank_t[3 + h]
                accf = psum[:, 3 + h, 0:260].rearrange("p (ms e) -> p ms e", ms=4)
                recip(rec[:, h, 0:4], accf[:, :, 64], [bt], [rec_t])
                ts("dve", rec[:, h, 2:4], rec[:, h, 2:4], small[:, 9:10], ALU.mult, [rec_t, small_t], [rec_t])
                for s in range(2):
                    ts("dve", o1, accf[:, s, 0:64], rec[:, h, s:s + 1], ALU.mult, [bt, rec_t], [o1_t])
                    stt(o2, accf[:, 2 + s, 0:64], rec[:, h, 2 + s:3 + s], o1, ALU.mult, ALU.add,
                        [bt, rec_t, o1_t], [o2_t])
                    tt("dve", sqc, o2, o2, ALU.mult, [o2_t], [sqc_t])
                    rsum(ss[:, 0:1], sqc, [sqc_t], [ss_t])
                    rstd_from(ss[:, 1:2], ss[:, 0:1], 64.0, [ss_t], [ss_t])
                    stt(ost[:, s, h * 64:(h + 1) * 64], o2, ss[:, 1:2], gdf, ALU.mult, ALU.mult,
                        [o2_t, ss_t, gdf_t], [o_t])
            store_mix(o_t, ost, c2 * 256, 256, 512, 256)

    def phase_D(l):
        kt_t, KT = Alloc(OFF_KT, OFF_VA)("KTd", [4, T], BF16)
        va_t, VA = Alloc(OFF_VA, OFF_MR)("VAd", [NB, 4, 65], BF16)
        ar = Alloc(OFF_MR, OFF_PR)
        cos_t, cosT = ar("cosT", [T], F32)
        sin_t, sinT = ar("sinT", [T], F32)
        a = Alloc(OFF_WK, ARENA_W)
        w_t, w = a("wD", [8, 544], BF16)
        ws_t, ws = a("wDs", [8, 96], BF16)
        wq_t, wq = a("wuq", [3, 384], BF16)
        wqs_t, wqs = a("wuqs", [3, 4, 96], BF16)
        wkv_t, wkv = a("wukv", [512], BF16)
        stg_t, stg = a("stgD", [3, 384], F32)
        cq_t, cqT = a("cqT", [3, 512], BF16)
        ckv_t, ckvT = a("ckvT", [512], BF16)
        sq = Rot([a("sqD%d" % i, [512], BF16) for i in range(2)])
        rq_t, rq = a("rstdq", [512], F32)
        rkv_t, rkv = a("rstdkv", [512], F32)
        rtm_t, rtm = a("rstdtm", [4], F32)
        t1_t, t1 = a("t1D", [512], F32)
        t2_t, t2 = a("t2D", [512], F32)
        rec_t, rec = a("recD", [4, 4], F32)
        dma("pool", w, w_in_d[l, :, 2048:2592].rearrange("(k p) c -> p k c", p=128), writes=[w_t], tile=w_t)
        memset("pool", ws[:, :, 0:64], 0.0, [ws_t])
        dma("pool", ws[:, :, 64:80], w_in_d[l, :, 2576:2592].rearrange("(k p) c -> p k c", p=128), writes=[ws_t], tile=ws_t)
        dma("pool", ws[:, :, 80:96], w_in_d[l, :, 2560:2576].rearrange("(k p) c -> p k c", p=128), writes=[ws_t], tile=ws_t)
        ts("pool", ws[:, :, 64:80], ws[:, :, 64:80], -1.0, ALU.mult, [ws_t], [ws_t], s2=0.0, op1=ALU.add)
        dma("sp", stg, w_uq_d[l].rearrange("(k p) c -> p k c", p=128), writes=[stg_t], tile=stg_t)
        memset("pool", wqs[:, :, :, 0:64], 0.0, [wqs_t])
        for k in range(3):
            ts("dve", wq[:, k, :], stg[:, k, :], gq[:, k:k + 1], ALU.mult, [stg_t, gq_t], [wq_t])
            sv = stg[:, k, :].rearrange("p (h e) -> p h e", h=4)
            ts("dve", wqs[:, k, :, 64:80], sv[:, :, 80:96], gq[:, k:k + 1], ALU.mult, [stg_t, gq_t], [wqs_t],
               s2=-1.0, op1=ALU.mult)
            ts("dve", wqs[:, k, :, 80:96], sv[:, :, 64:80], gq[:, k:k + 1], ALU.mult, [stg_t, gq_t], [wqs_t])
        stg2_t, stg2 = a("stgD2", [512], F32)
        dma("sp", stg2, w_ukv_d[l], writes=[stg2_t], tile=stg2_t)
        ts("dve", wkv, stg2, gkv[:, 0:1], ALU.mult, [stg2_t, gkv_t], [wkv_t])
        memset("pool", VA[:, :, :, 64:65], 1.0, [va_t])
        R = slice(64, 96)
        for c in range(NCH):
            cs = slice(c * 512, (c + 1) * 512)
            for tab_t, tab, shift in ((sin_t, sinT, 0.0), (cos_t, cosT, math.pi / 2)):
                ts("dve", t1[R, :], posrow[R, cs], invf[R, 0:1], ALU.mult, [posrow_t, invf_t], [t1_t],
                   s2=shift, op1=ALU.add)
                ts("dve", t2[R, :], t1[R, :], 1.0 / TWO_PI, ALU.mult, [t1_t], [t2_t], s2=MAGIC, op1=ALU.add)
                ts("dve", t2[R, :], t2[R, :], MAGIC, ALU.subtract, [t2_t], [t2_t])
                stt(t1[R, :], t2[R, :], -C1, t1[R, :], ALU.mult, ALU.add, [t2_t, t1_t], [t1_t])
                stt(t1[R, :], t2[R, :], -C2, t1[R, :], ALU.mult, ALU.add, [t2_t, t1_t], [t1_t])
                ts("dve", t1[R, :], t1[R, :], math.pi, ALU.min, [t1_t], [t1_t], s2=-math.pi, op1=ALU.max)
                act(tab[R, cs], t1[R, :], AF.Sin, [t1_t], [tab_t])
        S.barrier()
        a3 = Alloc(OFF_PR, OFF_WK)
        ctx = AttnCtx(a3, bias=False)
        qts = Rot([a3("qtD%d" % i, [4, 512], BF16) for i in range(2)])
        osts = Rot([a3("ostD%d" % i, [4, 256], BF16) for i in range(2)])
        pb = Rot([(bank_t[i], i) for i in range(3)])
        for c in range(NCH):
            cs = slice(c * 512, (c + 1) * 512)
            bt, bi = pb.next()
            proj_fm(bi, bt, w, w_t, 384, 128, c)
            act(ckvT, psum[:, bi, :], AF.Copy, [bt], [ckv_t])
            s_t, sqap = sq.next()
            act(sqap, psum[:, bi, :], AF.Square, [bt], [s_t])
            mm(psum[:, 6, :], onesb, sqap, True, True, [onesb_t, s_t], [bank_t[6]])
            rstd_from(rkv, psum[:, 6, :], 128.0, [bank_t[6]], [rkv_t])
            for sblk in range(4):
                mm(psum[:, 6, sblk:sblk + 1], sqap[:, sblk * 128:(sblk + 1) * 128], onesb[:, 0:1], True, True,
                   [s_t, onesb_t], [bank_t[6]])
            rstd_from(rtm, psum[:, 6, 0:4], 128.0, [bank_t[6]], [rtm_t])
            bta, bia = pb.next()
            proj_fm(bia, bta, w, w_t, 448, 96, c)
            btb, bib = pb.next()
            for k in range(8):
                mm(psum[0:96, bib, :], ws[:, k, :], hT[:, k, cs], k == 0, k == 7, [ws_t, hT_t], [btb])
            tt("dve", t1[R, :], psum[R, bia, :], cosT[R, cs], ALU.mult, [bta, cos_t], [t1_t])
            tt("dve", t2[R, :], psum[R, bib, :], sinT[R, cs], ALU.mult, [btb, sin_t], [t2_t])
            for h in range(4):
                tt("pool", KT[R, h, cs], t1[R, :], t2[R, :], ALU.add, [t1_t, t2_t], [kt_t])
            for h in range(4):
                bt, bi = pb.next()
                mm(psum[0:64, bi, :], wkv[:, h * 128:h * 128 + 64], ckvT, True, True, [wkv_t, ckv_t], [bt])
                tt("dve", KT[0:64, h, cs], psum[0:64, bi, :], rkv[0:64, :], ALU.mult, [bt, rkv_t], [kt_t])
            for sblk in range(4):
                b = c * 4 + sblk
                bt, bi = pb.next()
                mm(psum[:, bi, 0:256],
                   ckvT[:, sblk * 128:(sblk + 1) * 128],
                   wkv.rearrange("p (h t d) -> p h t d", h=4, t=2)[:, :, 1, :], True, True, [ckv_t, wkv_t], [bt])
                ts("dve", VA[:, b, :, 0:64], psum[:, bi, 0:256].rearrange("p (h d) -> p h d", h=4),
                   rtm[:, sblk:sblk + 1], ALU.mult, [bt, rtm_t], [va_t])
        dsc = 96.0 ** -0.5
        accb = Rot([(bank_t[3 + i], 3 + i) for i in range(3)])
        for c in range(NCH):
            cs = slice(c * 512, (c + 1) * 512)
            for k in range(3):
                proj_fm(7, bank_t[7], w, w_t, k * 128, 128, c)
                act(cqT[:, k, :], psum[:, 7, :], AF.Copy, [bank_t[7]], [cq_t])
                s_t, sqap = sq.next()
                act(sqap, psum[:, 7, :], AF.Square, [bank_t[7]], [s_t])
                mm(psum[:, 6, :], onesb, sqap, k == 0, k == 2, [onesb_t, s_t], [bank_t[6]])
            rstd_from(rq, psum[:, 6, :], 384.0, [bank_t[6]], [rq_t])
            q_t, q = qts.next()
            for h in range(4):
                for k in range(3):
                    mm(psum[0:96, 7, :], wq[:, k, h * 96:(h + 1) * 96], cqT[:, k, :], k == 0, k == 2,
                       [wq_t, cq_t], [bank_t[7]])
                for k in range(3):
                    mm(psum[0:96, 6, :], wqs[:, k, h, :], cqT[:, k, :], k == 0, k == 2, [wqs_t, cq_t], [bank_t[6]])
                stt(q[0:64, h, :], psum[0:64, 7, :], dsc, rq[0:64, :], ALU.mult, ALU.mult, [bank_t[7], rq_t], [q_t])
                tt("dve", t1[R, :], psum[R, 7, :], cosT[R, cs], ALU.mult, [bank_t[7], cos_t], [t1_t])
                tt("dve", t2[R, :], psum[R, 6, :], sinT[R, cs], ALU.mult, [bank_t[6], sin_t], [t2_t])
                tt("pool", t1[R, :], t1[R, :], t2[R, :], ALU.add, [t1_t, t2_t], [t1_t])
                stt(q[R, h, :], t1[R, :], dsc, rq[R, :], ALU.mult, ALU.mult, [t1_t, rq_t], [q_t])
            o_t, ost = osts.next()
            for h in range(4):
                at, ai = accb.next()
                accv = psum[:, ai, 0:260].rearrange("p (s e) -> p s e", s=4)
                for j in range(NB):
                    attn_step(ctx, [(KT[0:96, h, j * 128:(j + 1) * 128], q[0:96, h, :], [kt_t, q_t])],
                              None, None, None, 512, 1, [(VA[:, j, h, :], [va_t])], [(accv, at)],
                              j == 0, j == NB - 1)
                flush(ctx)
                recip(rec[:, h, :], accv[:, :, 64], [at], [rec_t])
                tt("dve", ost[:, :, h * 64:(h + 1) * 64], accv[:, :, 0:64],
                   rec[:, h, :].unsqueeze(2).broadcast_to([128, 4, 64]), ALU.mult, [at, rec_t], [o_t])
            store_mix(o_t, ost, c * 512, 512, 768, 256)

    OFF_WGU = 0
    OFF_WDN = OFF_WGU + 8 * 2 * DFF * 2
    OFF_PW = OFF_WDN + 22 * D * 2

    def sumsq_a(a_w, src, which="acc"):
        acc_t, acc = a_w[which]
        for k in range(8):
            ap, t_ = src(k)
            if k == 0:
                act(acc, ap, AF.Square, [t_], [acc_t])
            else:
                tmp_t, tmp = a_w["tmp"].next()
                act(tmp, ap, AF.Square, [t_], [tmp_t])
                tt("pool", acc, acc, tmp, ALU.add, [acc_t, tmp_t], [acc_t])

    def sumsq_b(a_w, which="acc"):
        acc_t, acc = a_w[which]
        of_t, of_ = a_w["onesf"]
        mm(psum[:, 6, :], of_, acc, True, True, [of_t, acc_t], [bank_t[6]])

    def post_update(yT_t, yT, gidx, c, a_w, last_layer_ffn, stats_done=False):
        if not stats_done:
            sumsq_a(a_w, lambda j: (yT[:, j, :], yT_t))
        sumsq_b(a_w)
        r_t, r = a_w["rstd"].next()
        rstd_from(r, psum[:, 6, :], float(D), [bank_t[6]], [r_t])
        for j in range(8):
            x_t, xap = a_w["xj"].next()
            dma("sp", xap, xT_d[j, :, c * 512:(c + 1) * 512], writes=[x_t], tile=x_t)
            tmp_t, tmp = a_w["tmp"].next()
            stt(tmp, yT[:, j, :], gs[:, gidx, j:j + 1], r, ALU.mult, ALU.mult, [yT_t, gs_t, r_t], [tmp_t])
            tt("pool" if j % 2 == 0 else "dve", xap, xap, tmp, ALU.add, [x_t, tmp_t], [x_t])
            if not last_layer_ffn:
                dma("sp", xT_d[j, :, c * 512:(c + 1) * 512], xap, reads=[x_t], tile=x_t)
            else:
                o_t, oap = a_w["otok"].next()
                for sblk in range(4):
                    tr(psum[:, 7, sblk * 128:(sblk + 1) * 128], xap[:, sblk * 128:(sblk + 1) * 128], identf,
                       [x_t, identf_t], [bank_t[7]])
                cp("dve", oap, psum[:, 7, :].rearrange("p (s d) -> p s d", s=4), [bank_t[7]], [o_t])
                dma("sp", out_d[c * 512:(c + 1) * 512, j * 128:(j + 1) * 128].rearrange("(s p) d -> p s d", p=128),
                    oap, reads=[o_t], tile=o_t)

    def phase_post(l):
        last = (l == nlayers - 1)
        wgu_t, wgu = Alloc(OFF_WGU, OFF_WDN)("wgu", [8, 2 * DFF], BF16)
        wdn_t, wdn = Alloc(OFF_WDN, OFF_PW)("wdn", [22, D], BF16)
        a = Alloc(OFF_PW, ARENA_W)
        yT_t, yT = a("yT", [8, 512], F32)
        aw = {"acc": a("accP", [512], F32), "onesf": a("onesf", [128], F32),
              "rstd": Rot([a("rstdP%d" % i, [512], F32) for i in range(1)]),
              "tmp": Rot([a("tmpP%d" % i, [512], F32) for i in range(2)]),
              "xj": Rot([a("xjP%d" % i, [512], F32) for i in range(3)])}
        memset("pool", aw["onesf"][1], 1.0, [aw["onesf"][0]])
        P2 = a.p
        wo_t, wo = a("wout", [8, D], BF16)
        mts = Rot([a("mixtok%d" % i, [4, D], BF16) for i in range(2)])
        mT_t, mixT = a("mixT", [8, 512], BF16)
        dma("pool", wo, w_out_d[l].rearrange("(k p) c -> p k c", p=128), writes=[wo_t], tile=wo_t)
        for piece in range(4):
            cs = slice(piece * 1408, (piece + 1) * 1408)
            dma("pool", wgu[:, :, cs], w_gu_d[l, :, cs].rearrange("(k p) c -> p k c", p=128), writes=[wgu_t], tile=wgu_t)
        for piece in range(2):
            fs = slice(piece * 11, (piece + 1) * 11)
            dma("pool", wdn[:, fs, :], w_dn_d[l, piece * 1408:(piece + 1) * 1408, :].rearrange("(f p) c -> p f c", p=128),
                writes=[wdn_t], tile=wdn_t)
        psb = psum[:, 0:2, :].bitcast(BF16)
        yb = Rot([(bank_t[i], i) for i in (2, 3, 4)])

        def wo_front(c):
            mt_t, mtok = mts.next()
            dma("sp", mtok, mix_d[c * 512:(c + 1) * 512, :].rearrange("(s p) d -> p s d", p=128), writes=[mt_t], tile=mt_t)
            for k in range(8):
                bi = k % 2
                for sblk in range(4):
                    tr(psb[:, bi, sblk * 128:(sblk + 1) * 128], mtok[:, sblk, k * 128:(k + 1) * 128], identb,
                       [mt_t, identb_t], [bank_t[bi]])
                cp("dve", mixT[:, k, :], psb[:, bi, 0:512], [bank_t[bi]], [mT_t])

        def wo_back(c):
            for j in range(8):
                bt, bi = yb.next()
                for k in range(8):
                    mm(psum[:, bi, :], wo[:, k, j * 128:(j + 1) * 128], mixT[:, k, :], k == 0, k == 7,
                       [wo_t, mT_t], [bt])
                act(yT[:, j, :], psum[:, bi, :], AF.Copy, [bt], [yT_t])

        wo_front(0)
        wo_back(0)
        for c in range(NCH):
            if c + 1 < NCH:
                wo_front(c + 1)
            post_update(yT_t, yT, 1, c, aw, False)
            if c + 1 < NCH:
                wo_back(c + 1)
        S.barrier()
        if stop_after == "wout%d" % l:
            return
        a = Alloc(P2, ARENA_W)
        hc_t, hc = a("hTc", [8, 512], BF16)
        at_t, actT = a("actT", [22, 512], BF16)
        sl_t, slu = a("silu", [2, 512], F32)
        if last:
            aw["otok"] = Rot([a("otok%d" % i, [4, 128], F32) for i in range(1)])
        aw["acc2"] = a("acc2", [512], F32)
        rstd2_t, rstd2 = a("rstd2", [512], F32)
        gb = Rot([(bank_t[0], 0), (bank_t[2], 2)])
        ub = Rot([(bank_t[1], 1), (bank_t[3], 3)])
        yb2 = Rot([(bank_t[4], 4), (bank_t[5], 5)])
        def norm_a(c):
            def src(k):
                x_t, xap = aw["xj"].next()
                dma("sp", xap, xT_d[k, :, c * 512:(c + 1) * 512], writes=[x_t], tile=x_t)
                return xap, x_t
            sumsq_a(aw, src, "acc2")

        def norm_b(c):
            sumsq_b(aw, "acc2")
            rstd_from(rstd2, psum[:, 6, :], float(D), [bank_t[6]], [rstd2_t])
            for k in range(8):
                x_t, xap = aw["xj"].next()
                dma("sp", xap, xT_d[k, :, c * 512:(c + 1) * 512], writes=[x_t], tile=x_t)
                tmp_t, tmp = aw["tmp"].next()
                stt(tmp, xap, gs[:, 2, k:k + 1], rstd2, ALU.mult, ALU.mult, [x_t, gs_t, rstd2_t], [tmp_t])
                act(hc[:, k, :], tmp, AF.Identity, [tmp_t, modT_t], [hc_t], bias=modT[:, 24 + k:25 + k])

        def gu(f0, f1):
            for f in range(f0, f1):
                gt, gi = gb.next()
                ut, ui = ub.next()
                for k in range(8):
                    mm(psum[:, gi, :], wgu[:, k, f * 128:(f + 1) * 128], hc[:, k, :], k == 0, k == 7, [wgu_t, hc_t], [gt])
                for k in range(8):
                    mm(psum[:, ui, :], wgu[:, k, DFF + f * 128:DFF + (f + 1) * 128], hc[:, k, :], k == 0, k == 7,
                       [wgu_t, hc_t], [ut])
                act(slu[:, f % 2, :], psum[:, gi, :], AF.Silu, [gt], [sl_t])
                tt("dve", actT[:, f, :], slu[:, f % 2, :], psum[:, ui, :], ALU.mult, [sl_t, ut], [at_t])

        def down(c):
            for j in range(8):
                bt, bi = yb2.next()
                for f in range(22):
                    mm(psum[:, bi, :], wdn[:, f, j * 128:(j + 1) * 128], actT[:, f, :], f == 0, f == 21, [wdn_t, at_t], [bt])
                cp("dve", yT[:, j, :], psum[:, bi, :], [bt], [yT_t])

        norm_a(0)
        norm_b(0)
        gu(0, 22)
        if NCH > 1:
            norm_a(1)
            norm_b(1)
        for c in range(NCH):
            down(c)
            sumsq_a(aw, lambda j: (yT[:, j, :], yT_t))
            if c + 2 < NCH:
                norm_a(c + 2)
            if c + 1 < NCH:
                gu(0, 8)
            post_update(yT_t, yT, 3, c, aw, last, stats_done=True)
            if c + 1 < NCH:
                gu(8, 22)
            if c + 2 < NCH:
                norm_b(c + 2)

    def run_phases():
        yield "xin", None
        for l in range(nlayers):
            def ph_mod(l=l):
                load_layer_small(l)
                phase_mod(l)
            yield "mod%d" % l, ph_mod
            yield "norm%d" % l, lambda l=l: phase_norm1(l)
            yield "A%d" % l, lambda l=l: phase_A(l)
            yield "B%d" % l, lambda l=l: phase_B(l)
            yield "C%d" % l, lambda l=l: phase_C(l)
            yield "D%d" % l, lambda l=l: phase_D(l)
            yield "post%d" % l, lambda l=l: phase_post(l)

    for name, fn in run_phases():
        if fn is not None:
            fn()
            S.barrier()
        if debug and name.startswith("norm"):
            dt_ = Tile("hdbg")
            dma("pool", hdbg_d.rearrange("k p t -> p k t"), hT, reads=[hT_t], tile=dt_)
            S.barrier()
        if name == stop_after or (stop_after is not None and name.startswith("post") and stop_after == "wout" + name[4:]):
            break
    if debug:
        dt2 = Tile("mixdbg")
        dma("pool", mixdbg_d, mix_d, tile=dt2)
    S.barrier()
    S.emit(nc)
    es.close()
    return nc, S


def _fm(v, n):
    return np.ascontiguousarray(v.reshape(v.shape[0], n, 128).transpose(0, 2, 1))


def _masks():
    p = np.arange(128)[:, None]
    q = np.arange(512)[None, :]
    mA = np.zeros((128, 20, 512), np.float32)
    for o in range(20):
        diff = q - p - (o - 8) * 128
        m = np.zeros((128, 512), np.float32)
        for d in (1, 4, 16):
            m += ((diff % d == 0) & (np.abs(diff) <= 64 * d)).astype(np.float32)
        mA[:, o, :] = np.where(m > 0, np.log(np.maximum(m, 1.0)), -30000.0)
    mB = np.zeros((128, 6, 512), np.float32)
    for o in range(6):
        diff = q - p - (o - 1) * 128
        mB[:, o, :] = np.where(np.abs(diff) <= 128, 0.0, -30000.0)
    return mA, mB


def make_in_maps(inputs, T, nb):
    f = np.float32
    mA, mB = _masks()
    invf = np.zeros((128, 1), f)
    half = 16
    inv = np.power(np.float32(10000.0), -np.arange(half, dtype=f) / np.float32(half)).astype(f)
    for r in range(128):
        invf[r, 0] = inv[r % 16]
    shared = {
        "w_ada": np.ascontiguousarray(inputs["w_ada"], f),
        "b_ada": _fm(np.asarray(inputs["b_ada"], f), 48),
        "g_pre_mix": _fm(np.asarray(inputs["g_pre_mix"], f), 8),
        "g_post_mix": _fm(np.asarray(inputs["g_post_mix"], f), 8),
        "g_pre_ffn": _fm(np.asarray(inputs["g_pre_ffn"], f), 8),
        "g_post_ffn": _fm(np.asarray(inputs["g_post_ffn"], f), 8),
        "w_in": np.ascontiguousarray(inputs["w_in"], f),
        "sink": np.ascontiguousarray(np.broadcast_to(np.asarray(inputs["sink_logits"], f)[:, None, :], (NL, 128, 4))),
        "lam": np.ascontiguousarray(np.broadcast_to(
            np.stack([np.asarray(inputs[k], f) for k in ("lam_q1", "lam_k1", "lam_q2", "lam_k2")], axis=1)[:, None],
            (NL, 128, 4, 32))),
        "g_diff": np.ascontiguousarray(np.broadcast_to(np.asarray(inputs["g_diff"], f)[:, None, :], (NL, 128, 64))),
        "g_q": _fm(np.asarray(inputs["g_mla_q"], f), 3),
        "g_kv": _fm(np.asarray(inputs["g_mla_kv"], f), 1),
        "w_uq": np.ascontiguousarray(inputs["w_uq"], f),
        "w_ukv": np.ascontiguousarray(inputs["w_ukv"], f),
        "w_out": np.ascontiguousarray(inputs["w_out"], f),
        "w_gu": np.ascontiguousarray(inputs["w_gate_up"], f),
        "w_dn": np.ascontiguousarray(inputs["w_down"], f),
        "maskA": mA, "maskB": mB,
        "ident": np.eye(128, dtype=f),
        "invf": invf,
    }
    x = np.asarray(inputs["x"], f)
    c = np.asarray(inputs["c"], f)
    pos = np.asarray(inputs["positions"], np.int32)
    maps = []
    for b in range(nb):
        m = dict(shared)
        m["x"] = np.ascontiguousarray(x[b])
        m["c"] = np.ascontiguousarray(c[b].reshape(8, 128).T)
        m["posrow"] = np.ascontiguousarray(np.broadcast_to(pos[b][None, :], (128, T)))
        m["poscol"] = np.ascontiguousarray(pos[b].reshape(T // 128, 128).T)
        maps.append(m)
    return maps


_CACHE = {}


def kernel(**inputs):
    x = np.asarray(inputs["x"])
    B, T, _ = x.shape
    if T not in _CACHE:
        _CACHE[T] = build(T)[0]
    nc = _CACHE[T]
    maps = make_in_maps(inputs, T, B)
    res = run_bass_kernel_spmd(nc, maps, core_ids=list(range(B)))
    return np.stack([np.asarray(r["out"], np.float32) for r in res.results], axis=0)
```

```python
import math
from contextlib import ExitStack

import numpy as np
import concourse.bass as bass
import concourse.mybir as mybir
from concourse.bass_utils import run_bass_kernel_spmd

F32 = mybir.dt.float32
BF16 = mybir.dt.bfloat16
I32 = mybir.dt.int32
U8 = mybir.dt.uint8
AF = mybir.ActivationFunctionType
ALU = mybir.AluOpType
AX = mybir.AxisListType

D = 1024
NL = 2
DFF = 2816
INW = 2592
EPS = 1e-6
SLOPES = [2.0 ** (-8.0 * j / 12.0) for j in range(1, 13)]
SL_B = SLOPES[0:4]
SL_C = SLOPES[4:8]
SL_A = SLOPES[8:12]
MAGIC = 12582912.0
TWO_PI = 2.0 * math.pi
C1 = 6.28125
C2 = TWO_PI - C1


class Tile:
    __slots__ = ("name", "lw", "rd")

    def __init__(self, name):
        self.name = name
        self.lw = None
        self.rd = {}


class Sched:
    ENGS = ("pe", "act", "dve", "pool", "sp")

    def __init__(self):
        self.ops = {e: [] for e in self.ENGS}
        self.counts = {"pe": 0, "act": 0, "dve": 0, "pool": 0}
        self.waited = {e: {} for e in self.ENGS}
        self.dma_keys = []
        self.n = 0

    def op(self, eng, fn, reads=(), writes=(), dma_tile=None):
        deps = {}
        for t in reads:
            if t.lw is not None and deps.get(t.lw[0], 0) < t.lw[1]:
                deps[t.lw[0]] = t.lw[1]
        for t in writes:
            if t.lw is not None and deps.get(t.lw[0], 0) < t.lw[1]:
                deps[t.lw[0]] = t.lw[1]
            for k, v in t.rd.items():
                if deps.get(k, 0) < v:
                    deps[k] = v
        if eng == "pe":
            deps.pop("pe", None)
        waits = []
        wd = self.waited[eng]
        for k, v in deps.items():
            if wd.get(k, 0) < v:
                wd[k] = v
                waits.append((k, v))
        if dma_tile is not None:
            key = "dma:" + dma_tile.name
            if key not in self.counts:
                self.counts[key] = 0
                self.dma_keys.append(key)
            self.counts[key] += 16
            inc = (key, 16)
        else:
            key = eng
            self.counts[key] += 1
            inc = (key, 1)
        val = self.counts[key]
        for t in writes:
            t.lw = (key, val)
            t.rd = {}
        for t in reads:
            if t.rd.get(key, 0) < val:
                t.rd[key] = val
        self.ops[eng].append((fn, waits, inc))
        self.n += 1

    def barrier(self):
        for e in self.ENGS:
            waits = []
            wd = self.waited[e]
            for k, v in self.counts.items():
                if v > 0 and wd.get(k, 0) < v:
                    wd[k] = v
                    waits.append((k, v))
            if waits:
                self.ops[e].append((None, waits, None))

    def emit(self, nc):
        keys = ["pe", "act", "dve", "pool"] + self.dma_keys
        with ExitStack() as es:
            sems = {}
            for i, k in enumerate(keys):
                sems[k] = es.enter_context(nc.semaphore("s%d" % i))
            block = es.enter_context(nc.Block())
            engmap = {"pe": block.tensor, "act": block.scalar, "dve": block.vector,
                      "pool": block.gpsimd, "sp": block.sync}
            for e in self.ENGS:
                ops = self.ops[e]

                def body(eng, ops=ops):
                    for fn, waits, inc in ops:
                        for k, v in waits:
                            eng.wait_ge(sems[k], v)
                        if fn is not None:
                            fn(eng).then_inc(sems[inc[0]], inc[1])
                engmap[e](body)


class Rot:
    def __init__(self, items):
        self.items = items
        self.i = 0

    def next(self):
        it = self.items[self.i % len(self.items)]
        self.i += 1
        return it


def build(T, nlayers=NL, stop_after=None, debug=False):
    NB = T // 128
    NCH = T // 512
    nc = bass.Bass("TRN2", target_bir_lowering=False)

    def din(name, shape, dt=F32):
        return nc.dram_tensor(name, list(shape), dt, kind="ExternalInput").ap()

    x_d = din("x", [T, D])
    c_d = din("c", [128, 8])
    posrow_d = din("posrow", [128, T], I32)
    poscol_d = din("poscol", [128, NB], I32)
    w_ada_d = din("w_ada", [NL, D, 6 * D])
    b_ada_d = din("b_ada", [NL, 128, 48])
    g_pre_mix_d = din("g_pre_mix", [NL, 128, 8])
    g_post_mix_d = din("g_post_mix", [NL, 128, 8])
    g_pre_ffn_d = din("g_pre_ffn", [NL, 128, 8])
    g_post_ffn_d = din("g_post_ffn", [NL, 128, 8])
    w_in_d = din("w_in", [NL, D, INW])
    sink_d = din("sink", [NL, 128, 4])
    lam_d = din("lam", [NL, 128, 4, 32])
    g_diff_d = din("g_diff", [NL, 128, 64])
    g_q_d = din("g_q", [NL, 128, 3])
    g_kv_d = din("g_kv", [NL, 128, 1])
    w_uq_d = din("w_uq", [NL, 384, 384])
    w_ukv_d = din("w_ukv", [NL, 128, 512])
    w_out_d = din("w_out", [NL, D, D])
    w_gu_d = din("w_gu", [NL, D, 2 * DFF])
    w_dn_d = din("w_dn", [NL, DFF, D])
    maskA_d = din("maskA", [128, 20, 512])
    maskB_d = din("maskB", [128, 6, 512])
    ident_d = din("ident", [128, 128])
    invf_d = din("invf", [128, 1])
    out_d = nc.dram_tensor("out", [T, D], F32, kind="ExternalOutput").ap()
    xT_d = nc.dram_tensor("xT_scr", [8, 128, T], F32, kind="ExternalOutput" if debug else "Internal").ap()
    mix_d = nc.dram_tensor("mix_scr", [T, D], BF16, kind="Internal").ap()
    if debug:
        mixdbg_d = nc.dram_tensor("mix_dbg", [T, D], F32, kind="ExternalOutput").ap()
        hdbg_d = nc.dram_tensor("h_dbg", [8, 128, T], F32, kind="ExternalOutput").ap()

    S = Sched()
    es = ExitStack()
    ARENA = 207 * 1024
    arena = es.enter_context(nc.sbuf_tensor("arena", [128, ARENA], U8))
    psum = es.enter_context(nc.psum_tensor("psum", [128, 8, 512], F32))
    bank_t = [Tile("bank%d" % i) for i in range(8)]

    def bank(i):
        return psum[:, i, :]

    class Alloc:
        def __init__(self, lo, hi):
            self.lo, self.hi, self.p = lo, hi, lo

        def __call__(self, name, shape, dt):
            esz = {F32: 4, BF16: 2, I32: 4}[dt]
            n = int(np.prod(shape)) * esz
            n = (n + 63) // 64 * 64
            assert self.p + n <= self.hi, (name, self.p, n, self.hi)
            ap = arena[:, self.p:self.p + n].bitcast(dt)
            ap = ap[:, 0:int(np.prod(shape))]
            if len(shape) == 2:
                ap = ap.rearrange("p (a b) -> p a b", a=shape[0])
            elif len(shape) == 3:
                ap = ap.rearrange("p (a b c) -> p a b c", a=shape[0], b=shape[1])
            elif len(shape) == 4:
                ap = ap.rearrange("p (a b c d) -> p a b c d", a=shape[0], b=shape[1], c=shape[2])
            self.p += n
            return Tile(name), ap

    CONST_SZ = 4 * 1024
    ca = Alloc(ARENA - CONST_SZ, ARENA)
    identb_t, identb = ca("identb", [128], BF16)
    identf_t, identf = ca("identf", [128], F32)
    onesb_t, onesb = ca("onesb", [128], BF16)
    eps_t, eps_ap = ca("eps", [1], F32)
    cact_t, cact = ca("cact", [8], BF16)
    cf_t, cf = ca("cf", [8], F32)
    modT_t, modT = ca("modT", [48], F32)
    gs_t, gs = ca("gs", [4, 8], F32)
    graw_t, graw = ca("graw", [4, 8], F32)
    poscol_t, poscol = ca("poscol", [NB], F32)
    nposcol_t, nposcol = ca("nposcol", [NB], F32)
    posci_t, posci = ca("posci", [NB], I32)
    invf_t, invf = ca("invf", [1], F32)
    small_t, small = ca("small", [64], F32)
    lamraw_t, lamraw = ca("lamraw", [4, 32], F32)
    gdf_t, gdf = ca("gdf", [64], F32)
    gq_t, gq = ca("gq", [3], F32)
    gkv_t, gkv = ca("gkv", [1], F32)
    ARENA_W = ARENA - CONST_SZ

    def dma(eng, out, in_, reads=(), writes=(), tile=None):
        S.op(eng, lambda e: e.dma_start(out=out, in_=in_), reads=reads, writes=writes, dma_tile=tile)

    def mm(out, lhsT, rhs, start, stop, reads, writes):
        S.op("pe", lambda e: e.matmul(out, lhsT, rhs, start=start, stop=stop, skip_group_check=True),
             reads=reads, writes=writes)

    def tr(out, in_, ident, reads, writes):
        S.op("pe", lambda e: e.transpose(out, in_, ident), reads=reads, writes=writes)

    def act(out, in_, func, reads, writes, scale=1.0, bias=None, eng="act"):
        if bias is None:
            S.op(eng, lambda e: e.activation(out=out, in_=in_, func=func, scale=scale), reads=reads, writes=writes)
        else:
            S.op(eng, lambda e: e.activation(out=out, in_=in_, func=func, scale=scale, bias=bias),
                 reads=reads, writes=writes)

    def tt(eng, out, in0, in1, op, reads, writes):
        S.op(eng, lambda e: e.tensor_tensor(out=out, in0=in0, in1=in1, op=op), reads=reads, writes=writes)

    def ts(eng, out, in0, s1, op0, reads, writes, s2=None, op1=None):
        if op1 is None:
            S.op(eng, lambda e: e.tensor_scalar(out=out, in0=in0, scalar1=s1, scalar2=None, op0=op0),
                 reads=reads, writes=writes)
        else:
            S.op(eng, lambda e: e.tensor_scalar(out=out, in0=in0, scalar1=s1, scalar2=s2, op0=op0, op1=op1),
                 reads=reads, writes=writes)

    def stt(out, in0, scalar, in1, op0, op1, reads, writes):
        S.op("dve", lambda e: e.scalar_tensor_tensor(out=out, in0=in0, scalar=scalar, in1=in1, op0=op0, op1=op1),
             reads=reads, writes=writes)

    def cp(eng, out, in_, reads, writes):
        S.op(eng, lambda e: e.tensor_copy(out=out, in_=in_), reads=reads, writes=writes)

    def memset(eng, ap, val, writes):
        S.op(eng, lambda e: e.memset(ap, val), writes=writes)

    def recip(out, in_, reads, writes):
        S.op("dve", lambda e: e.reciprocal(out=out, in_=in_), reads=reads, writes=writes)

    def rsum(out, in_, reads, writes):
        S.op("dve", lambda e: e.reduce_sum(out=out, in_=in_, axis=AX.X), reads=reads, writes=writes)

    def rstd_from(out, in_, n, reads, writes):
        act(out, in_, AF.Ln, reads + [eps_t], writes, scale=1.0 / n, bias=eps_ap[:, 0:1])
        act(out, out, AF.Exp, writes, writes, scale=-0.5)

    dma("pool", identb, ident_d, writes=[identb_t], tile=identb_t)
    dma("sp", identf, ident_d, writes=[identf_t], tile=identf_t)
    memset("dve", onesb, 1.0, [onesb_t])
    memset("dve", eps_ap, EPS, [eps_t])
    dma("sp", cf, c_d, writes=[cf_t], tile=cf_t)
    act(cact, cf, AF.Silu, [cf_t], [cact_t])
    dma("sp", posci, poscol_d, writes=[posci_t], tile=posci_t)
    cp("dve", poscol, posci, [posci_t], [poscol_t])
    ts("dve", nposcol, poscol, -1.0, ALU.mult, [poscol_t], [nposcol_t])
    dma("sp", invf, invf_d, writes=[invf_t], tile=invf_t)

    def phase_x_in():
        a = Alloc(0, ARENA_W)
        xin = Rot([a("xin%d" % i, [D], F32) for i in range(3)])
        xo = Rot([a("xo%d" % i, [8, 128], F32) for i in range(3)])
        pbk = Rot([(bank_t[i], i) for i in range(4)])
        for b in range(NB):
            xi_t, xi = xin.next()
            dma("sp", xi, x_d[b * 128:(b + 1) * 128, :], writes=[xi_t], tile=xi_t)
            xo_t, xoap = xo.next()
            for half in range(2):
                bt, bi = pbk.next()
                for kk in range(4):
                    k = half * 4 + kk
                    tr(psum[:, bi, kk * 128:(kk + 1) * 128], xi[:, k * 128:(k + 1) * 128], identf,
                       [xi_t, identf_t], [bt])
                eng = "dve" if half == 0 else "act"
                if eng == "dve":
                    cp("dve", xoap[:, half * 4:(half + 1) * 4, :],
                       psum[:, bi, :].rearrange("p (a b) -> p a b", a=4), [bt], [xo_t])
                else:
                    act(xoap[:, half * 4:(half + 1) * 4, :],
                        psum[:, bi, :].rearrange("p (a b) -> p a b", a=4), AF.Copy, [bt], [xo_t])
            dma("sp", xT_d[:, :, b * 128:(b + 1) * 128].rearrange("k p t -> p k t"), xoap,
                reads=[xo_t], tile=xo_t)
        return

    xT_dt = Tile("xT_dram")
    xTc_t = [Tile("xTd%d" % c) for c in range(NCH)]
    mixc_t = [Tile("mixd%d" % c) for c in range(NCH)]

    phase_x_in()
    S.barrier()

    def load_layer_small(l):
        dma("sp", graw[:, 0, :], g_pre_mix_d[l], writes=[graw_t], tile=graw_t)
        dma("sp", graw[:, 1, :], g_post_mix_d[l], writes=[graw_t], tile=graw_t)
        dma("sp", graw[:, 2, :], g_pre_ffn_d[l], writes=[graw_t], tile=graw_t)
        dma("sp", graw[:, 3, :], g_post_ffn_d[l], writes=[graw_t], tile=graw_t)
        dma("sp", modT, b_ada_d[l], writes=[modT_t], tile=modT_t)
        dma("sp", lamraw, lam_d[l], writes=[lamraw_t], tile=lamraw_t)
        dma("sp", gdf, g_diff_d[l], writes=[gdf_t], tile=gdf_t)
        dma("sp", gq, g_q_d[l], writes=[gq_t], tile=gq_t)
        dma("sp", gkv, g_kv_d[l], writes=[gkv_t], tile=gkv_t)
        dma("sp", small[:, 0:4], sink_d[l], writes=[small_t], tile=small_t)

    def phase_mod(l):
        a = Alloc(0, ARENA_W)
        wa = Rot([a("wada%d" % i, [8, 1024], BF16) for i in range(2)])
        bt = bank_t[0]
        for piece in range(6):
            w_t, w = wa.next()
            dma("pool", w, w_ada_d[l, :, piece * 1024:(piece + 1) * 1024].rearrange("(k p) c -> p k c", p=128),
                writes=[w_t], tile=w_t)
            for jj in range(8):
                j = piece * 8 + jj
                for k in range(8):
                    mm(psum[:, 0, j:j + 1], w[:, k, jj * 128:(jj + 1) * 128], cact[:, k:k + 1],
                       k == 0, k == 7, [w_t, cact_t], [bt])
        tt("dve", modT, modT, psum[:, 0, 0:48], ALU.add, [modT_t, bt], [modT_t])
        stt(gs[:, 0, :], modT[:, 8:16], 1.0, graw[:, 0, :], ALU.add, ALU.mult, [modT_t, graw_t], [gs_t])
        tt("dve", gs[:, 1, :], modT[:, 16:24], graw[:, 1, :], ALU.mult, [modT_t, graw_t], [gs_t])
        stt(gs[:, 2, :], modT[:, 32:40], 1.0, graw[:, 2, :], ALU.add, ALU.mult, [modT_t, graw_t], [gs_t])
        tt("dve", gs[:, 3, :], modT[:, 40:48], graw[:, 3, :], ALU.mult, [modT_t, graw_t], [gs_t])
        act(small[:, 4:8], small[:, 0:4], AF.Exp, [small_t], [small_t])
        tt("dve", lamraw[:, 0, :], lamraw[:, 0, :], lamraw[:, 1, :], ALU.mult, [lamraw_t], [lamraw_t])
        tt("dve", lamraw[:, 2, :], lamraw[:, 2, :], lamraw[:, 3, :], ALU.mult, [lamraw_t], [lamraw_t])
        rsum(small[:, 10:11], lamraw[:, 0, :], [lamraw_t], [small_t])
        rsum(small[:, 11:12], lamraw[:, 2, :], [lamraw_t], [small_t])
        act(small[:, 10:12], small[:, 10:12], AF.Exp, [small_t], [small_t])
        lam_init = 0.8 - 0.6 * math.exp(-0.3 * l)
        tt("dve", small[:, 8:9], small[:, 10:11], small[:, 11:12], ALU.subtract, [small_t], [small_t])
        ts("dve", small[:, 8:9], small[:, 8:9], lam_init, ALU.add, [small_t], [small_t])
        ts("dve", small[:, 9:10], small[:, 8:9], -1.0, ALU.mult, [small_t], [small_t])
        ts("dve", gdf, gdf, 1.0 - lam_init, ALU.mult, [gdf_t], [gdf_t])

    OFF_HT = 0
    OFF_KT = OFF_HT + 16 * T
    OFF_VA = OFF_KT + 8 * T
    OFF_MR = OFF_VA + NB * 4 * 65 * 2 + 64
    MR_SZ = max(32 * 1024, 8 * T)
    OFF_PR = OFF_MR + MR_SZ
    OFF_WK = OFF_PR + max(4 * T, 16384)
    hT_t, hT = Alloc(OFF_HT, OFF_KT)("hT", [8, T], BF16)
    posrow_t, posrow = Alloc(OFF_PR, OFF_WK)("posrow", [T], F32)

    def norm_to_h(c, src_t, src, gidx, shcol, a_work, dst_t, dst, bt_s, bi_s):
        sq = a_work["sq"]
        for k in range(8):
            sq_t, sqap = sq.next()
            act(sqap, src[:, k, :], AF.Square, [src_t], [sq_t], eng="act")
            mm(psum[:, bi_s, :], onesb, sqap, k == 0, k == 7, [onesb_t, sq_t], [bt_s])
        r_t, r = a_work["rstd"].next()
        rstd_from(r, psum[:, bi_s, :], float(D), [bt_s], [r_t])
        for k in range(8):
            tmp_t, tmp = a_work["tmp"].next()
            stt(tmp, src[:, k, :], gs[:, gidx, k:k + 1], r, ALU.mult, ALU.mult, [src_t, gs_t, r_t], [tmp_t])
            act(dst(k), tmp, AF.Identity, [tmp_t, modT_t], [dst_t], bias=modT[:, shcol + k:shcol + k + 1],
                eng="act")

    def phase_norm1(l):
        a2 = Alloc(OFF_KT, OFF_PR)
        xs = Rot([a2("xs%d" % i, [8, 512], F32) for i in range(2)])
        a = Alloc(OFF_WK, ARENA_W)
        w = {"sq": Rot([a("sq%d" % i, [512], BF16) for i in range(3)]),
             "rstd": Rot([a("rstd%d" % i, [512], F32) for i in range(2)]),
             "tmp": Rot([a("tmp%d" % i, [512], F32) for i in range(3)])}
        dma("sp", posrow.bitcast(I32), posrow_d, writes=[posrow_t], tile=posrow_t)
        cp("pool", posrow, posrow.bitcast(I32), [posrow_t], [posrow_t])
        sb = Rot([(bank_t[4], 4), (bank_t[5], 5)])
        for c in range(NCH):
            x_t, xap = xs.next()
            dma("sp", xap, xT_d[:, :, c * 512:(c + 1) * 512].rearrange("k p t -> p k t"), writes=[x_t], tile=x_t)
            bt_s, bi_s = sb.next()
            norm_to_h(c, x_t, xap, 0, 0, w, hT_t, lambda k, c=c: hT[:, k, c * 512:(c + 1) * 512], bt_s, bi_s)

    class AttnCtx:
        def __init__(self, a, bias=True, a_pt=None, sbanks=(0, 1, 2), depth=2):
            self.sb = Rot([(bank_t[i], i) for i in sbanks])
            self.pending = []
            self.depth = depth
            if bias:
                self.tt_ = Rot([a("tt%d" % i, [512], F32) for i in range(depth + 2)])
                self.dt = Rot([a("dt%d" % i, [512], F32) for i in range(2)])
            ap_ = a_pt if a_pt is not None else a
            self.pt = Rot([ap_("pt%d" % i, [512], BF16) for i in range(depth + 2)])

    def flush(ctx):
        while ctx.pending:
            ctx.pending.pop(0)()

    def dist_tile(ctx, c0, W, j):
        d_t, d = ctx.dt.next()
        act(d[:, 0:W], posrow[:, c0:c0 + W], AF.Abs, [posrow_t, nposcol_t], [d_t], bias=nposcol[:, j:j + 1])
        return d_t, d

    def attn_step(ctx, qk, slope, dtile, mask, W, G, vaug, acc, first, last):
        st, si = ctx.sb.next()
        pe_mask = mask is not None and mask[2] == "pe"
        if len(qk) == 1 and G > 1:
            lhsT, rhs, rds = qk[0]
            mm(psum[:, si, 0:G * W], lhsT, rhs, True, True, rds, [st])
        else:
            for g in range(G):
                lhsT, rhs, rds = qk[g]
                mm(psum[:, si, g * W:(g + 1) * W], lhsT, rhs, True, not pe_mask, rds, [st])
        if pe_mask:
            mm(psum[:, si, 0:W], identb, mask[0], False, True, [identb_t, mask[1]], [st])
            mask = None
        if slope is not None:
            d_t, d = dtile
            t_t, tap = ctx.tt_.next()
            if G == 1:
                stt(tap, d[:, 0:W], -slope, psum[:, si, :], ALU.mult, ALU.add, [d_t, st], [t_t])
            else:
                stt(tap.rearrange("p (g w) -> p g w", g=G), d[:, 0:W].unsqueeze(1).broadcast_to([128, G, W]),
                    -slope, psum[:, si, :].rearrange("p (g w) -> p g w", g=G), ALU.mult, ALU.add, [d_t, st], [t_t])
            src, src_t = tap, t_t
        else:
            src, src_t = psum[:, si, :], st
        p_t, p = ctx.pt.next()
        act(p, src, AF.Exp, [src_t], [p_t])
        if mask is not None:
            m_ap, m_t, m_eng = mask
            tt(m_eng, p, p, m_ap, ALU.mult, [p_t, m_t], [p_t])
        nsub = W // 128

        def back():
            for g in range(G):
                v_ap, v_rd = vaug[g]
                a_ap, a_t = acc[g]
                for s in range(nsub):
                    mm(a_ap[:, s, :], p[:, g * W + s * 128:g * W + (s + 1) * 128], v_ap,
                       first and g == 0 and s == 0, last, [p_t] + v_rd, [a_t])
        ctx.pending.append(back)
        if len(ctx.pending) > ctx.depth:
            ctx.pending.pop(0)()

    def proj_fm(bi, bt, w, w_t, col0, M, c, extra_cols=None):
        for k in range(8):
            mm(psum[:M, bi, :], w[:, k, col0:col0 + M], hT[:, k, c * 512:(c + 1) * 512], k == 0, k == 7,
               [w_t, hT_t], [bt])

    def store_mix(ost_t, ost, c0, ntok, col0, ncols):
        nsub = ntok // 128
        dma("sp", mix_d[c0:c0 + ntok, col0:col0 + ncols].rearrange("(s p) c -> p s c", p=128), ost,
            reads=[ost_t], tile=ost_t)

    def phase_A(l):
        kt_t, KT = Alloc(OFF_KT, OFF_VA)("KTa", [2, T], BF16)
        va_t, VA = Alloc(OFF_VA, OFF_MR)("VAa", [NB, 4, 65], BF16)
        mk_t, MK = Alloc(OFF_MR, OFF_PR)("maskA", [20, 512], BF16)
        a = Alloc(OFF_WK, ARENA_W)
        w_t, w = a("wA", [8, 768], BF16)
        ctx = AttnCtx(a, sbanks=(0, 1, 2, 7), depth=3)
        qts = Rot([a("qtA%d" % i, [4, 512], BF16) for i in range(2)])
        for q_t, q in qts.items:
            memset("pool", q, 0.0, [q_t])
        a_sp = Alloc(OFF_MR + 26 * 1024, OFF_PR)
        osts = Rot([a_sp("ostA%d" % i, [4, 256], BF16) for i in range(2)])
        rec_t, rec = a("recA", [4, 4], F32)
        dma("pool", w, w_in_d[l, :, 0:768].rearrange("(k p) c -> p k c", p=128), writes=[w_t], tile=w_t)
        dma("pool", MK, maskA_d, writes=[mk_t], tile=mk_t)
        memset("pool", VA[:, :, :, 64:65], 1.0, [va_t])
        pb = Rot([(bank_t[i], i) for i in range(3)])
        for c in range(NCH):
            for pr in range(2):
                bt, bi = pb.next()
                proj_fm(bi, bt, w, w_t, 256 + pr * 128, 128, c)
                cp("dve", KT[:, pr, c * 512:(c + 1) * 512], psum[:, bi, :], [bt], [kt_t])
        for b in range(NB):
            bt, bi = pb.next()
            for k in range(8):
                mm(psum[:, bi, 0:256], hT[:, k, b * 128:(b + 1) * 128], w[:, k, 512:768], k == 0, k == 7,
                   [hT_t, w_t], [bt])
            act(VA[:, b, :, 0:64], psum[:, bi, 0:256].rearrange("p (h d) -> p h d", h=4), AF.Copy, [bt], [va_t])
        for c in range(NCH):
            q_t, q = qts.next()
            for pr in range(2):
                qb_t, qbi = ctx.sb.next()
                proj_fm(qbi, qb_t, w, w_t, pr * 128, 128, c)
                act(q[0:64, 2 * pr, :], psum[0:64, qbi, :], AF.Identity, [qb_t], [q_t], scale=0.125)
                act(q[64:128, 2 * pr + 1, :], psum[64:128, qbi, :], AF.Identity, [qb_t], [q_t], scale=0.125)
            j0 = max(0, 4 * c - 8)
            j1 = min(NB - 1, 4 * c + 11)
            d_next = dist_tile(ctx, c * 512, 512, j0)
            for j in range(j0, j1 + 1):
                dtile = d_next
                if j < j1:
                    d_next = dist_tile(ctx, c * 512, 512, j + 1)
                o = j - 4 * c + 8
                for h in range(4):
                    hp, pr = (h % 2) * 64, h // 2
                    attn_step(ctx,
                              [(KT[:, pr, j * 128:(j + 1) * 128], q[:, h, :], [kt_t, q_t])],
                              SL_A[h], dtile, (MK[:, o, :], mk_t, "pe"), 512, 1,
                              [(VA[:, j, h, :], [va_t])],
                              [(psum[:, 3 + h, 0:260].rearrange("p (s e) -> p s e", s=4), bank_t[3 + h])],
                              j == j0, j == j1)
            flush(ctx)
            o_t, ost = osts.next()
            for h in range(4):
                accv = psum[:, 3 + h, 0:260].rearrange("p (s e) -> p s e", s=4)
                recip(rec[:, h, :], accv[:, :, 64], [bank_t[3 + h]], [rec_t])
                tt("dve", ost[:, :, h * 64:(h + 1) * 64], accv[:, :, 0:64],
                   rec[:, h, :].unsqueeze(2).broadcast_to([128, 4, 64]), ALU.mult, [bank_t[3 + h], rec_t], [o_t])
            store_mix(o_t, ost, c * 512, 512, 0, 256)

    def phase_B(l):
        kt_t, KT = Alloc(OFF_KT, OFF_VA)("KTb", [T], BF16)
        va_t, VA = Alloc(OFF_VA, OFF_MR)("VAb", [NB, 2, 65], BF16)
        mk_t, MK = Alloc(OFF_MR + 20 * 1024, OFF_PR)("maskB", [6, 512], BF16)
        a = Alloc(OFF_WK, ARENA_W)
        w_t, w = a("wB", [8, 512], BF16)
        ctx = AttnCtx(a, sbanks=(0, 1, 2, 7), depth=3)
        qts = Rot([a("qtB%d" % i, [4, 512], BF16) for i in range(2)])
        for q_t, q in qts.items:
            memset("pool", q, 0.0, [q_t])
        a_sp = Alloc(OFF_MR + 26 * 1024, OFF_PR)
        osts = Rot([a_sp("ostB%d" % i, [4, 256], BF16) for i in range(2)])
        rec_t, rec = a("recB", [4, 4], F32)
        for r in range(2):
            for g in range(2):
                dma("pool", w[:, :, r * 128 + g * 64:r * 128 + (g + 1) * 64],
                    w_in_d[l, :, 768 + (g * 2 + r) * 64:768 + (g * 2 + r + 1) * 64].rearrange("(k p) c -> p k c", p=128),
                    writes=[w_t], tile=w_t)
        dma("pool", w[:, :, 256:512], w_in_d[l, :, 1024:1280].rearrange("(k p) c -> p k c", p=128),
            writes=[w_t], tile=w_t)
        dma("pool", MK, maskB_d, writes=[mk_t], tile=mk_t)
        memset("pool", VA[:, :, :, 64:65], 1.0, [va_t])
        pb = Rot([(bank_t[i], i) for i in range(3)])
        for c in range(NCH):
            bt, bi = pb.next()
            proj_fm(bi, bt, w, w_t, 256, 128, c)
            cp("dve", KT[:, c * 512:(c + 1) * 512], psum[:, bi, :], [bt], [kt_t])
        for b in range(NB):
            bt, bi = pb.next()
            for k in range(8):
                mm(psum[:, bi, 0:128], hT[:, k, b * 128:(b + 1) * 128], w[:, k, 384:512], k == 0, k == 7,
                   [hT_t, w_t], [bt])
            act(VA[:, b, :, 0:64], psum[:, bi, 0:128].rearrange("p (h d) -> p h d", h=2), AF.Copy, [bt], [va_t])
        for c in range(NCH):
            q_t, q = qts.next()
            for r in range(2):
                qb_t, qbi = ctx.sb.next()
                proj_fm(qbi, qb_t, w, w_t, r * 128, 128, c)
                act(q[0:64, r, :], psum[0:64, qbi, :], AF.Identity, [qb_t], [q_t], scale=0.125)
                act(q[64:128, 2 + r, :], psum[64:128, qbi, :], AF.Identity, [qb_t], [q_t], scale=0.125)
            j0 = max(0, 4 * c - 1)
            j1 = min(NB - 1, 4 * c + 4)
            d_next = dist_tile(ctx, c * 512, 512, j0)
            for j in range(j0, j1 + 1):
                dtile = d_next
                if j < j1:
                    d_next = dist_tile(ctx, c * 512, 512, j + 1)
                o = j - 4 * c + 1
                for h in range(4):
                    g, r = h // 2, h % 2
                    attn_step(ctx,
                              [(KT[:, j * 128:(j + 1) * 128], q[:, h, :], [kt_t, q_t])],
                              SL_B[h], dtile, (MK[:, o, :], mk_t, "pe"), 512, 1,
                              [(VA[:, j, g, :], [va_t])],
                              [(psum[:, 3 + h, 0:260].rearrange("p (s e) -> p s e", s=4), bank_t[3 + h])],
                              j == j0, j == j1)
            flush(ctx)
            o_t, ost = osts.next()
            for h in range(4):
                accv = psum[:, 3 + h, 0:260].rearrange("p (s e) -> p s e", s=4)
                ts("dve", rec[:, h, :], accv[:, :, 64], small[:, 4 + h:5 + h], ALU.add,
                   [bank_t[3 + h], small_t], [rec_t])
                recip(rec[:, h, :], rec[:, h, :], [rec_t], [rec_t])
                tt("dve", ost[:, :, h * 64:(h + 1) * 64], accv[:, :, 0:64],
                   rec[:, h, :].unsqueeze(2).broadcast_to([128, 4, 64]), ALU.mult, [bank_t[3 + h], rec_t], [o_t])
            store_mix(o_t, ost, c * 512, 512, 256, 256)

    def phase_C(l):
        kt_t, KT = Alloc(OFF_KT, OFF_VA)("KTc", [2, T], BF16)
        va_t, VA = Alloc(OFF_VA, OFF_MR)("VAc", [NB, 4, 65], BF16)
        a = Alloc(OFF_WK, ARENA_W)
        w_t, w = a("wC", [8, 768], BF16)
        ctx = AttnCtx(a, sbanks=(0, 1, 2, 7), depth=3)
        qts = Rot([a("qtC%d" % i, [8, 256], BF16) for i in range(2)])
        for q_t, q in qts.items:
            memset("pool", q, 0.0, [q_t])
        osts = Rot([a("ostC%d" % i, [2, 256], BF16) for i in range(2)])
        rec_t, rec = a("recC", [4, 8], F32)
        o1_t, o1 = a("o1C", [64], F32)
        o2_t, o2 = a("o2C", [64], F32)
        sqc_t, sqc = a("sqC", [64], F32)
        ss_t, ss = a("ssC", [2], F32)
        dma("pool", w, w_in_d[l, :, 1280:2048].rearrange("(k p) c -> p k c", p=128), writes=[w_t], tile=w_t)
        memset("pool", VA[:, :, :, 64:65], 1.0, [va_t])
        pb = Rot([(bank_t[i], i) for i in range(3)])
        for c in range(NCH):
            for pr in range(2):
                bt, bi = pb.next()
                proj_fm(bi, bt, w, w_t, 256 + pr * 128, 128, c)
                cp("dve", KT[:, pr, c * 512:(c + 1) * 512], psum[:, bi, :], [bt], [kt_t])
        for b in range(NB):
            bt, bi = pb.next()
            for k in range(8):
                mm(psum[:, bi, 0:256], hT[:, k, b * 128:(b + 1) * 128], w[:, k, 512:768], k == 0, k == 7,
                   [hT_t, w_t], [bt])
            act(VA[:, b, :, 0:64], psum[:, bi, 0:256].rearrange("p (h d) -> p h d", h=4), AF.Copy, [bt], [va_t])
        qscale = 32.0 ** -0.5
        for c2 in range(T // 256):
            q_t, q = qts.next()
            for pr in range(2):
                qb_t, qbi = ctx.sb.next()
                for k in range(8):
                    mm(psum[:, qbi, 0:256], w[:, k, pr * 128:(pr + 1) * 128], hT[:, k, c2 * 256:(c2 + 1) * 256],
                       k == 0, k == 7, [w_t, hT_t], [qb_t])
                for blk in range(4):
                    eng = "dve" if blk % 2 == 0 else "act"
                    if eng == "dve":
                        ts("dve", q[blk * 32:(blk + 1) * 32, pr * 4 + blk, :], psum[blk * 32:(blk + 1) * 32, qbi, 0:256],
                           qscale, ALU.mult, [qb_t], [q_t])
                    else:
                        act(q[blk * 32:(blk + 1) * 32, pr * 4 + blk, :], psum[blk * 32:(blk + 1) * 32, qbi, 0:256],
                            AF.Identity, [qb_t], [q_t], scale=qscale)
            d_next = dist_tile(ctx, c2 * 256, 256, 0)
            for j in range(NB):
                dtile = d_next
                if j < NB - 1:
                    d_next = dist_tile(ctx, c2 * 256, 256, j + 1)
                for h in range(4):
                    hp, pr = (h % 2) * 64, h // 2
                    accv = psum[:, 3 + h, 0:260].rearrange("p (m s e) -> p m s e", m=2, s=2)
                    attn_step(ctx,
                              [(KT[:, pr, j * 128:(j + 1) * 128], q[:, h * 2:h * 2 + 2, :], [kt_t, q_t])],
                              SL_C[h], dtile, None, 256, 2,
                              [(VA[:, j, h, :], [va_t])] * 2,
                              [(accv[:, m, :, :], bank_t[3 + h]) for m in range(2)],
                              j == 0, j == NB - 1)
            flush(ctx)
            o_t, ost = osts.next()
            for h in range(4):
                bt = bank_t[3 + h]
                accf = psum[:, 3 + h, 0:260].rearrange("p (ms e) -> p ms e", ms=4)
                recip(rec[:, h, 0:4], accf[:, :, 64], [bt], [rec_t])
                ts("dve", rec[:, h, 2:4], rec[:, h, 2:4], small[:, 9:10], ALU.mult, [rec_t, small_t], [rec_t])
                for s in range(2):
                    ts("dve", o1, accf[:, s, 0:64], rec[:, h, s:s + 1], ALU.mult, [bt, rec_t], [o1_t])
                    stt(o2, accf[:, 2 + s, 0:64], rec[:, h, 2 + s:3 + s], o1, ALU.mult, ALU.add,
                        [bt, rec_t, o1_t], [o2_t])
                    tt("dve", sqc, o2, o2, ALU.mult, [o2_t], [sqc_t])
                    rsum(ss[:, 0:1], sqc, [sqc_t], [ss_t])
                    rstd_from(ss[:, 1:2], ss[:, 0:1], 64.0, [ss_t], [ss_t])
                    stt(ost[:, s, h * 64:(h + 1) * 64], o2, ss[:, 1:2], gdf, ALU.mult, ALU.mult,
                        [o2_t, ss_t, gdf_t], [o_t])
            store_mix(o_t, ost, c2 * 256, 256, 512, 256)

    def phase_D(l):
        kt_t, KT = Alloc(OFF_KT, OFF_VA)("KTd", [4, T], BF16)
        va_t, VA = Alloc(OFF_VA, OFF_MR)("VAd", [NB, 4, 65], BF16)
        ar = Alloc(OFF_MR, OFF_PR)
        cos_t, cosT = ar("cosT", [T], F32)
        sin_t, sinT = ar("sinT", [T], F32)
        a = Alloc(OFF_WK, ARENA_W)
        w_t, w = a("wD", [8, 544], BF16)
        ws_t, ws = a("wDs", [8, 96], BF16)
        wq_t, wq = a("wuq", [3, 384], BF16)
        wqs_t, wqs = a("wuqs", [3, 4, 96], BF16)
        wkv_t, wkv = a("wukv", [512], BF16)
        stg_t, stg = a("stgD", [3, 384], F32)
        cq_t, cqT = a("cqT", [3, 512], BF16)
        ckv_t, ckvT = a("ckvT", [512], BF16)
        sq = Rot([a("sqD%d" % i, [512], BF16) for i in range(2)])
        rq_t, rq = a("rstdq", [512], F32)
        rkv_t, rkv = a("rstdkv", [512], F32)
        rtm_t, rtm = a("rstdtm", [4], F32)
        t1_t, t1 = a("t1D", [512], F32)
        t2_t, t2 = a("t2D", [512], F32)
        rec_t, rec = a("recD", [4, 4], F32)
        dma("pool", w, w_in_d[l, :, 2048:2592].rearrange("(k p) c -> p k c", p=128), writes=[w_t], tile=w_t)
        memset("pool", ws[:, :, 0:64], 0.0, [ws_t])
        dma("pool", ws[:, :, 64:80], w_in_d[l, :, 2576:2592].rearrange("(k p) c -> p k c", p=128), writes=[ws_t], tile=ws_t)
        dma("pool", ws[:, :, 80:96], w_in_d[l, :, 2560:2576].rearrange("(k p) c -> p k c", p=128), writes=[ws_t], tile=ws_t)
        ts("pool", ws[:, :, 64:80], ws[:, :, 64:80], -1.0, ALU.mult, [ws_t], [ws_t], s2=0.0, op1=ALU.add)
        dma("sp", stg, w_uq_d[l].rearrange("(k p) c -> p k c", p=128), writes=[stg_t], tile=stg_t)
        memset("pool", wqs[:, :, :, 0:64], 0.0, [wqs_t])
        for k in range(3):
            ts("dve", wq[:, k, :], stg[:, k, :], gq[:, k:k + 1], ALU.mult, [stg_t, gq_t], [wq_t])
            sv = stg[:, k, :].rearrange("p (h e) -> p h e", h=4)
            ts("dve", wqs[:, k, :, 64:80], sv[:, :, 80:96], gq[:, k:k + 1], ALU.mult, [stg_t, gq_t], [wqs_t],
               s2=-1.0, op1=ALU.mult)
            ts("dve", wqs[:, k, :, 80:96], sv[:, :, 64:80], gq[:, k:k + 1], ALU.mult, [stg_t, gq_t], [wqs_t])
        stg2_t, stg2 = a("stgD2", [512], F32)
        dma("sp", stg2, w_ukv_d[l], writes=[stg2_t], tile=stg2_t)
        ts("dve", wkv, stg2, gkv[:, 0:1], ALU.mult, [stg2_t, gkv_t], [wkv_t])
        memset("pool", VA[:, :, :, 64:65], 1.0, [va_t])
        R = slice(64, 96)
        for c in range(NCH):
            cs = slice(c * 512, (c + 1) * 512)
            for tab_t, tab, shift in ((sin_t, sinT, 0.0), (cos_t, cosT, math.pi / 2)):
                ts("dve", t1[R, :], posrow[R, cs], invf[R, 0:1], ALU.mult, [posrow_t, invf_t], [t1_t],
                   s2=shift, op1=ALU.add)
                ts("dve", t2[R, :], t1[R, :], 1.0 / TWO_PI, ALU.mult, [t1_t], [t2_t], s2=MAGIC, op1=ALU.add)
                ts("dve", t2[R, :], t2[R, :], MAGIC, ALU.subtract, [t2_t], [t2_t])
                stt(t1[R, :], t2[R, :], -C1, t1[R, :], ALU.mult, ALU.add, [t2_t, t1_t], [t1_t])
                stt(t1[R, :], t2[R, :], -C2, t1[R, :], ALU.mult, ALU.add, [t2_t, t1_t], [t1_t])
                ts("dve", t1[R, :], t1[R, :], math.pi, ALU.min, [t1_t], [t1_t], s2=-math.pi, op1=ALU.max)
                act(tab[R, cs], t1[R, :], AF.Sin, [t1_t], [tab_t])
        S.barrier()
        a3 = Alloc(OFF_PR, OFF_WK)
        ctx = AttnCtx(a3, bias=False)
        qts = Rot([a3("qtD%d" % i, [4, 512], BF16) for i in range(2)])
        osts = Rot([a3("ostD%d" % i, [4, 256], BF16) for i in range(2)])
        pb = Rot([(bank_t[i], i) for i in range(3)])
        for c in range(NCH):
            cs = slice(c * 512, (c + 1) * 512)
            bt, bi = pb.next()
            proj_fm(bi, bt, w, w_t, 384, 128, c)
            act(ckvT, psum[:, bi, :], AF.Copy, [bt], [ckv_t])
            s_t, sqap = sq.next()
            act(sqap, psum[:, bi, :], AF.Square, [bt], [s_t])
            mm(psum[:, 6, :], onesb, sqap, True, True, [onesb_t, s_t], [bank_t[6]])
            rstd_from(rkv, psum[:, 6, :], 128.0, [bank_t[6]], [rkv_t])
            for sblk in range(4):
                mm(psum[:, 6, sblk:sblk + 1], sqap[:, sblk * 128:(sblk + 1) * 128], onesb[:, 0:1], True, True,
                   [s_t, onesb_t], [bank_t[6]])
            rstd_from(rtm, psum[:, 6, 0:4], 128.0, [bank_t[6]], [rtm_t])
            bta, bia = pb.next()
            proj_fm(bia, bta, w, w_t, 448, 96, c)
            btb, bib = pb.next()
            for k in range(8):
                mm(psum[0:96, bib, :], ws[:, k, :], hT[:, k, cs], k == 0, k == 7, [ws_t, hT_t], [btb])
            tt("dve", t1[R, :], psum[R, bia, :], cosT[R, cs], ALU.mult, [bta, cos_t], [t1_t])
            tt("dve", t2[R, :], psum[R, bib, :], sinT[R, cs], ALU.mult, [btb, sin_t], [t2_t])
            for h in range(4):
                tt("pool", KT[R, h, cs], t1[R, :], t2[R, :], ALU.add, [t1_t, t2_t], [kt_t])
            for h in range(4):
                bt, bi = pb.next()
                mm(psum[0:64, bi, :], wkv[:, h * 128:h * 128 + 64], ckvT, True, True, [wkv_t, ckv_t], [bt])
                tt("dve", KT[0:64, h, cs], psum[0:64, bi, :], rkv[0:64, :], ALU.mult, [bt, rkv_t], [kt_t])
            for sblk in range(4):
                b = c * 4 + sblk
                bt, bi = pb.next()
                mm(psum[:, bi, 0:256],
                   ckvT[:, sblk * 128:(sblk + 1) * 128],
                   wkv.rearrange("p (h t d) -> p h t d", h=4, t=2)[:, :, 1, :], True, True, [ckv_t, wkv_t], [bt])
                ts("dve", VA[:, b, :, 0:64], psum[:, bi, 0:256].rearrange("p (h d) -> p h d", h=4),
                   rtm[:, sblk:sblk + 1], ALU.mult, [bt, rtm_t], [va_t])
        dsc = 96.0 ** -0.5
        accb = Rot([(bank_t[3 + i], 3 + i) for i in range(3)])
        for c in range(NCH):
            cs = slice(c * 512, (c + 1) * 512)
            for k in range(3):
                proj_fm(7, bank_t[7], w, w_t, k * 128, 128, c)
                act(cqT[:, k, :], psum[:, 7, :], AF.Copy, [bank_t[7]], [cq_t])
                s_t, sqap = sq.next()
                act(sqap, psum[:, 7, :], AF.Square, [bank_t[7]], [s_t])
                mm(psum[:, 6, :], onesb, sqap, k == 0, k == 2, [onesb_t, s_t], [bank_t[6]])
            rstd_from(rq, psum[:, 6, :], 384.0, [bank_t[6]], [rq_t])
            q_t, q = qts.next()
            for h in range(4):
                for k in range(3):
                    mm(psum[0:96, 7, :], wq[:, k, h * 96:(h + 1) * 96], cqT[:, k, :], k == 0, k == 2,
                       [wq_t, cq_t], [bank_t[7]])
                for k in range(3):
                    mm(psum[0:96, 6, :], wqs[:, k, h, :], cqT[:, k, :], k == 0, k == 2, [wqs_t, cq_t], [bank_t[6]])
                stt(q[0:64, h, :], psum[0:64, 7, :], dsc, rq[0:64, :], ALU.mult, ALU.mult, [bank_t[7], rq_t], [q_t])
                tt("dve", t1[R, :], psum[R, 7, :], cosT[R, cs], ALU.mult, [bank_t[7], cos_t], [t1_t])
                tt("dve", t2[R, :], psum[R, 6, :], sinT[R, cs], ALU.mult, [bank_t[6], sin_t], [t2_t])
                tt("pool", t1[R, :], t1[R, :], t2[R, :], ALU.add, [t1_t, t2_t], [t1_t])
                stt(q[R, h, :], t1[R, :], dsc, rq[R, :], ALU.mult, ALU.mult, [t1_t, rq_t], [q_t])
            o_t, ost = osts.next()
            for h in range(4):
                at, ai = accb.next()
                accv = psum[:, ai, 0:260].rearrange("p (s e) -> p s e", s=4)
                for j in range(NB):
                    attn_step(ctx, [(KT[0:96, h, j * 128:(j + 1) * 128], q[0:96, h, :], [kt_t, q_t])],
                              None, None, None, 512, 1, [(VA[:, j, h, :], [va_t])], [(accv, at)],
                              j == 0, j == NB - 1)
                flush(ctx)
                recip(rec[:, h, :], accv[:, :, 64], [at], [rec_t])
                tt("dve", ost[:, :, h * 64:(h + 1) * 64], accv[:, :, 0:64],
                   rec[:, h, :].unsqueeze(2).broadcast_to([128, 4, 64]), ALU.mult, [at, rec_t], [o_t])
            store_mix(o_t, ost, c * 512, 512, 768, 256)

    OFF_WGU = 0
    OFF_WDN = OFF_WGU + 8 * 2 * DFF * 2
    OFF_PW = OFF_WDN + 22 * D * 2

    def sumsq_a(a_w, src, which="acc"):
        acc_t, acc = a_w[which]
        for k in range(8):
            ap, t_ = src(k)
            if k == 0:
                act(acc, ap, AF.Square, [t_], [acc_t])
            else:
                tmp_t, tmp = a_w["tmp"].next()
                act(tmp, ap, AF.Square, [t_], [tmp_t])
                tt("pool", acc, acc, tmp, ALU.add, [acc_t, tmp_t], [acc_t])

    def sumsq_b(a_w, which="acc"):
        acc_t, acc = a_w[which]
        of_t, of_ = a_w["onesf"]
        mm(psum[:, 6, :], of_, acc, True, True, [of_t, acc_t], [bank_t[6]])

    def post_update(yT_t, yT, gidx, c, a_w, last_layer_ffn, stats_done=False):
        if not stats_done:
            sumsq_a(a_w, lambda j: (yT[:, j, :], yT_t))
        sumsq_b(a_w)
        r_t, r = a_w["rstd"].next()
        rstd_from(r, psum[:, 6, :], float(D), [bank_t[6]], [r_t])
        def xload(j):
            x_t, xap = a_w["xj"].next()
            dma("sp", xap, xT_d[j, :, c * 512:(c + 1) * 512], writes=[x_t], tile=x_t)
            return x_t, xap
        pend = [xload(j) for j in range(3)]
        for j in range(8):
            x_t, xap = pend.pop(0)
            tmp_t, tmp = a_w["tmp"].next()
            stt(tmp, yT[:, j, :], gs[:, gidx, j:j + 1], r, ALU.mult, ALU.mult, [yT_t, gs_t, r_t], [tmp_t])
            tt("pool", xap, xap, tmp, ALU.add, [x_t, tmp_t], [x_t])
            if not last_layer_ffn:
                dma("sp", xT_d[j, :, c * 512:(c + 1) * 512], xap, reads=[x_t], tile=x_t)
            else:
                o_t, oap = a_w["otok"].next()
                for sblk in range(4):
                    tr(psum[:, 7, sblk * 128:(sblk + 1) * 128], xap[:, sblk * 128:(sblk + 1) * 128], identf,
                       [x_t, identf_t], [bank_t[7]])
                cp("dve", oap, psum[:, 7, :].rearrange("p (s d) -> p s d", s=4), [bank_t[7]], [o_t])
                dma("sp", out_d[c * 512:(c + 1) * 512, j * 128:(j + 1) * 128].rearrange("(s p) d -> p s d", p=128),
                    oap, reads=[o_t], tile=o_t)
            if j + 3 < 8:
                pend.append(xload(j + 3))

    def phase_post(l):
        last = (l == nlayers - 1)
        wgu_t, wgu = Alloc(OFF_WGU, OFF_WDN)("wgu", [8, 2 * DFF], BF16)
        wdn_t, wdn = Alloc(OFF_WDN, OFF_PW)("wdn", [22, D], BF16)
        a = Alloc(OFF_PW, ARENA_W)
        yT_t, yT = a("yT", [8, 512], F32)
        aw = {"acc": a("accP", [512], F32), "onesf": a("onesf", [128], F32),
              "rstd": Rot([a("rstdP%d" % i, [512], F32) for i in range(1)]),
              "tmp": Rot([a("tmpP%d" % i, [512], F32) for i in range(2)]),
              "xj": Rot([a("xjP%d" % i, [512], F32) for i in range(3)])}
        memset("pool", aw["onesf"][1], 1.0, [aw["onesf"][0]])
        P2 = a.p
        wo_t, wo = a("wout", [8, D], BF16)
        mts = Rot([a("mixtok%d" % i, [4, D], BF16) for i in range(2)])
        mT_t, mixT = a("mixT", [8, 512], BF16)
        dma("pool", wo, w_out_d[l].rearrange("(k p) c -> p k c", p=128), writes=[wo_t], tile=wo_t)
        for piece in range(4):
            cs = slice(piece * 1408, (piece + 1) * 1408)
            dma("pool", wgu[:, :, cs], w_gu_d[l, :, cs].rearrange("(k p) c -> p k c", p=128), writes=[wgu_t], tile=wgu_t)
        for piece in range(2):
            fs = slice(piece * 11, (piece + 1) * 11)
            dma("pool", wdn[:, fs, :], w_dn_d[l, piece * 1408:(piece + 1) * 1408, :].rearrange("(f p) c -> p f c", p=128),
                writes=[wdn_t], tile=wdn_t)
        psb = psum[:, 0:2, :].bitcast(BF16)
        yb = Rot([(bank_t[i], i) for i in (2, 3, 4)])

        def wo_front(c):
            mt_t, mtok = mts.next()
            dma("sp", mtok, mix_d[c * 512:(c + 1) * 512, :].rearrange("(s p) d -> p s d", p=128), writes=[mt_t], tile=mt_t)
            for k in range(8):
                bi = k % 2
                for sblk in range(4):
                    tr(psb[:, bi, sblk * 128:(sblk + 1) * 128], mtok[:, sblk, k * 128:(k + 1) * 128], identb,
                       [mt_t, identb_t], [bank_t[bi]])
                cp("dve", mixT[:, k, :], psb[:, bi, 0:512], [bank_t[bi]], [mT_t])

        def wo_back(c):
            for j in range(8):
                bt, bi = yb.next()
                for k in range(8):
                    mm(psum[:, bi, :], wo[:, k, j * 128:(j + 1) * 128], mixT[:, k, :], k == 0, k == 7,
                       [wo_t, mT_t], [bt])
                act(yT[:, j, :], psum[:, bi, :], AF.Copy, [bt], [yT_t])

        wo_front(0)
        wo_back(0)
        for c in range(NCH):
            if c + 1 < NCH:
                wo_front(c + 1)
            post_update(yT_t, yT, 1, c, aw, False)
            if c + 1 < NCH:
                wo_back(c + 1)
        S.barrier()
        if stop_after == "wout%d" % l:
            return
        a = Alloc(P2, ARENA_W)
        hc_t, hc = a("hTc", [8, 512], BF16)
        at_t, actT = a("actT", [22, 512], BF16)
        sl_t, slu = a("silu", [2, 512], F32)
        if last:
            aw["otok"] = Rot([a("otok%d" % i, [4, 128], F32) for i in range(1)])
        aw["acc2"] = a("acc2", [512], F32)
        rstd2_t, rstd2 = a("rstd2", [512], F32)
        gb = Rot([(bank_t[0], 0), (bank_t[2], 2)])
        ub = Rot([(bank_t[1], 1), (bank_t[3], 3)])
        yb2 = Rot([(bank_t[4], 4), (bank_t[5], 5)])
        def norm_a(c):
            def src(k):
                x_t, xap = aw["xj"].next()
                dma("sp", xap, xT_d[k, :, c * 512:(c + 1) * 512], writes=[x_t], tile=x_t)
                return xap, x_t
            sumsq_a(aw, src, "acc2")

        def norm_b(c):
            sumsq_b(aw, "acc2")
            rstd_from(rstd2, psum[:, 6, :], float(D), [bank_t[6]], [rstd2_t])
            for k in range(8):
                x_t, xap = aw["xj"].next()
                dma("sp", xap, xT_d[k, :, c * 512:(c + 1) * 512], writes=[x_t], tile=x_t)
                tmp_t, tmp = aw["tmp"].next()
                stt(tmp, xap, gs[:, 2, k:k + 1], rstd2, ALU.mult, ALU.mult, [x_t, gs_t, rstd2_t], [tmp_t])
                act(hc[:, k, :], tmp, AF.Identity, [tmp_t, modT_t], [hc_t], bias=modT[:, 24 + k:25 + k])

        def gu(f0, f1):
            for f in range(f0, f1):
                gt, gi = gb.next()
                ut, ui = ub.next()
                for k in range(8):
                    mm(psum[:, gi, :], wgu[:, k, f * 128:(f + 1) * 128], hc[:, k, :], k == 0, k == 7, [wgu_t, hc_t], [gt])
                for k in range(8):
                    mm(psum[:, ui, :], wgu[:, k, DFF + f * 128:DFF + (f + 1) * 128], hc[:, k, :], k == 0, k == 7,
                       [wgu_t, hc_t], [ut])
                act(slu[:, f % 2, :], psum[:, gi, :], AF.Silu, [gt], [sl_t])
                tt("dve", actT[:, f, :], slu[:, f % 2, :], psum[:, ui, :], ALU.mult, [sl_t, ut], [at_t])

        def down(c):
            for j in range(8):
                bt, bi = yb2.next()
                for f in range(22):
                    mm(psum[:, bi, :], wdn[:, f, j * 128:(j + 1) * 128], actT[:, f, :], f == 0, f == 21, [wdn_t, at_t], [bt])
                cp("dve", yT[:, j, :], psum[:, bi, :], [bt], [yT_t])

        norm_a(0)
        norm_b(0)
        gu(0, 22)
        if NCH > 1:
            norm_a(1)
            norm_b(1)
        for c in range(NCH):
            down(c)
            sumsq_a(aw, lambda j: (yT[:, j, :], yT_t))
            if c + 2 < NCH:
                norm_a(c + 2)
            if c + 1 < NCH:
                gu(0, 8)
            post_update(yT_t, yT, 3, c, aw, last, stats_done=True)
            if c + 1 < NCH:
                gu(8, 22)
            if c + 2 < NCH:
                norm_b(c + 2)

    def run_phases():
        yield "xin", None
        for l in range(nlayers):
            def ph_mod(l=l):
                load_layer_small(l)
                phase_mod(l)
            yield "mod%d" % l, ph_mod
            yield "norm%d" % l, lambda l=l: phase_norm1(l)
            yield "A%d" % l, lambda l=l: phase_A(l)
            yield "B%d" % l, lambda l=l: phase_B(l)
            yield "C%d" % l, lambda l=l: phase_C(l)
            yield "D%d" % l, lambda l=l: phase_D(l)
            yield "post%d" % l, lambda l=l: phase_post(l)

    for name, fn in run_phases():
        if fn is not None:
            fn()
            S.barrier()
        if debug and name.startswith("norm"):
            dt_ = Tile("hdbg")
            dma("pool", hdbg_d.rearrange("k p t -> p k t"), hT, reads=[hT_t], tile=dt_)
            S.barrier()
        if name == stop_after or (stop_after is not None and name.startswith("post") and stop_after == "wout" + name[4:]):
            break
    if debug:
        dt2 = Tile("mixdbg")
        dma("pool", mixdbg_d, mix_d, tile=dt2)
    S.barrier()
    S.emit(nc)
    es.close()
    return nc, S


def _fm(v, n):
    return np.ascontiguousarray(v.reshape(v.shape[0], n, 128).transpose(0, 2, 1))


def _masks():
    p = np.arange(128)[:, None]
    q = np.arange(512)[None, :]
    mA = np.zeros((128, 20, 512), np.float32)
    for o in range(20):
        diff = q - p - (o - 8) * 128
        m = np.zeros((128, 512), np.float32)
        for d in (1, 4, 16):
            m += ((diff % d == 0) & (np.abs(diff) <= 64 * d)).astype(np.float32)
        mA[:, o, :] = np.where(m > 0, np.log(np.maximum(m, 1.0)), -30000.0)
    mB = np.zeros((128, 6, 512), np.float32)
    for o in range(6):
        diff = q - p - (o - 1) * 128
        mB[:, o, :] = np.where(np.abs(diff) <= 128, 0.0, -30000.0)
    return mA, mB


def make_in_maps(inputs, T, nb):
    f = np.float32
    mA, mB = _masks()
    invf = np.zeros((128, 1), f)
    half = 16
    inv = np.power(np.float32(10000.0), -np.arange(half, dtype=f) / np.float32(half)).astype(f)
    for r in range(128):
        invf[r, 0] = inv[r % 16]
    shared = {
        "w_ada": np.ascontiguousarray(inputs["w_ada"], f),
        "b_ada": _fm(np.asarray(inputs["b_ada"], f), 48),
        "g_pre_mix": _fm(np.asarray(inputs["g_pre_mix"], f), 8),
        "g_post_mix": _fm(np.asarray(inputs["g_post_mix"], f), 8),
        "g_pre_ffn": _fm(np.asarray(inputs["g_pre_ffn"], f), 8),
        "g_post_ffn": _fm(np.asarray(inputs["g_post_ffn"], f), 8),
        "w_in": np.ascontiguousarray(inputs["w_in"], f),
        "sink": np.ascontiguousarray(np.broadcast_to(np.asarray(inputs["sink_logits"], f)[:, None, :], (NL, 128, 4))),
        "lam": np.ascontiguousarray(np.broadcast_to(
            np.stack([np.asarray(inputs[k], f) for k in ("lam_q1", "lam_k1", "lam_q2", "lam_k2")], axis=1)[:, None],
            (NL, 128, 4, 32))),
        "g_diff": np.ascontiguousarray(np.broadcast_to(np.asarray(inputs["g_diff"], f)[:, None, :], (NL, 128, 64))),
        "g_q": _fm(np.asarray(inputs["g_mla_q"], f), 3),
        "g_kv": _fm(np.asarray(inputs["g_mla_kv"], f), 1),
        "w_uq": np.ascontiguousarray(inputs["w_uq"], f),
        "w_ukv": np.ascontiguousarray(inputs["w_ukv"], f),
        "w_out": np.ascontiguousarray(inputs["w_out"], f),
        "w_gu": np.ascontiguousarray(inputs["w_gate_up"], f),
        "w_dn": np.ascontiguousarray(inputs["w_down"], f),
        "maskA": mA, "maskB": mB,
        "ident": np.eye(128, dtype=f),
        "invf": invf,
    }
    x = np.asarray(inputs["x"], f)
    c = np.asarray(inputs["c"], f)
    pos = np.asarray(inputs["positions"], np.int32)
    maps = []
    for b in range(nb):
        m = dict(shared)
        m["x"] = np.ascontiguousarray(x[b])
        m["c"] = np.ascontiguousarray(c[b].reshape(8, 128).T)
        m["posrow"] = np.ascontiguousarray(np.broadcast_to(pos[b][None, :], (128, T)))
        m["poscol"] = np.ascontiguousarray(pos[b].reshape(T // 128, 128).T)
        maps.append(m)
    return maps


_CACHE = {}


def kernel(**inputs):
    x = np.asarray(inputs["x"])
    B, T, _ = x.shape
    if T not in _CACHE:
        _CACHE[T] = build(T)[0]
    nc = _CACHE[T]
    maps = make_in_maps(inputs, T, B)
    res = run_bass_kernel_spmd(nc, maps, core_ids=list(range(B)))
    return np.stack([np.asarray(r["out"], np.float32) for r in res.results], axis=0)
```

```python
import math
from contextlib import ExitStack

import numpy as np
import concourse.bass as bass
import concourse.mybir as mybir
from concourse.bass_utils import run_bass_kernel_spmd

F32 = mybir.dt.float32
BF16 = mybir.dt.bfloat16
I32 = mybir.dt.int32
U8 = mybir.dt.uint8
AF = mybir.ActivationFunctionType
ALU = mybir.AluOpType
AX = mybir.AxisListType

D = 1024
NL = 2
DFF = 2816
INW = 2592
EPS = 1e-6
SLOPES = [2.0 ** (-8.0 * j / 12.0) for j in range(1, 13)]
SL_B = SLOPES[0:4]
SL_C = SLOPES[4:8]
SL_A = SLOPES[8:12]
MAGIC = 12582912.0
TWO_PI = 2.0 * math.pi
C1 = 6.28125
C2 = TWO_PI - C1


class Tile:
    __slots__ = ("name", "lw", "rd")

    def __init__(self, name):
        self.name = name
        self.lw = None
        self.rd = {}


class Sched:
    ENGS = ("pe", "act", "dve", "pool", "sp")

    def __init__(self):
        self.ops = {e: [] for e in self.ENGS}
        self.counts = {"pe": 0, "act": 0, "dve": 0, "pool": 0}
        self.waited = {e: {} for e in self.ENGS}
        self.dma_keys = []
        self.n = 0

    def op(self, eng, fn, reads=(), writes=(), dma_tile=None):
        deps = {}
        for t in reads:
            if t.lw is not None and deps.get(t.lw[0], 0) < t.lw[1]:
                deps[t.lw[0]] = t.lw[1]
        for t in writes:
            if t.lw is not None and deps.get(t.lw[0], 0) < t.lw[1]:
                deps[t.lw[0]] = t.lw[1]
            for k, v in t.rd.items():
                if deps.get(k, 0) < v:
                    deps[k] = v
        if eng == "pe":
            deps.pop("pe", None)
        waits = []
        wd = self.waited[eng]
        for k, v in deps.items():
            if wd.get(k, 0) < v:
                wd[k] = v
                waits.append((k, v))
        if dma_tile is not None:
            key = "dma:" + dma_tile.name
            if key not in self.counts:
                self.counts[key] = 0
                self.dma_keys.append(key)
            self.counts[key] += 16
            inc = (key, 16)
        else:
            key = eng
            self.counts[key] += 1
            inc = (key, 1)
        val = self.counts[key]
        for t in writes:
            t.lw = (key, val)
            t.rd = {}
        for t in reads:
            if t.rd.get(key, 0) < val:
                t.rd[key] = val
        self.ops[eng].append((fn, waits, inc))
        self.n += 1

    def barrier(self):
        for e in self.ENGS:
            waits = []
            wd = self.waited[e]
            for k, v in self.counts.items():
                if v > 0 and wd.get(k, 0) < v:
                    wd[k] = v
                    waits.append((k, v))
            if waits:
                self.ops[e].append((None, waits, None))

    def emit(self, nc):
        keys = ["pe", "act", "dve", "pool"] + self.dma_keys
        with ExitStack() as es:
            sems = {}
            for i, k in enumerate(keys):
                sems[k] = es.enter_context(nc.semaphore("s%d" % i))
            block = es.enter_context(nc.Block())
            engmap = {"pe": block.tensor, "act": block.scalar, "dve": block.vector,
                      "pool": block.gpsimd, "sp": block.sync}
            for e in self.ENGS:
                ops = self.ops[e]

                def body(eng, ops=ops):
                    for fn, waits, inc in ops:
                        for k, v in waits:
                            eng.wait_ge(sems[k], v)
                        if fn is not None:
                            fn(eng).then_inc(sems[inc[0]], inc[1])
                engmap[e](body)


class Rot:
    def __init__(self, items):
        self.items = items
        self.i = 0

    def next(self):
        it = self.items[self.i % len(self.items)]
        self.i += 1
        return it


def build(T, nlayers=NL, stop_after=None, debug=False):
    NB = T // 128
    NCH = T // 512
    nc = bass.Bass("TRN2", target_bir_lowering=False)

    def din(name, shape, dt=F32):
        return nc.dram_tensor(name, list(shape), dt, kind="ExternalInput").ap()

    x_d = din("x", [T, D])
    c_d = din("c", [128, 8])
    posrow_d = din("posrow", [128, T], I32)
    poscol_d = din("poscol", [128, NB], I32)
    w_ada_d = din("w_ada", [NL, D, 6 * D])
    b_ada_d = din("b_ada", [NL, 128, 48])
    g_pre_mix_d = din("g_pre_mix", [NL, 128, 8])
    g_post_mix_d = din("g_post_mix", [NL, 128, 8])
    g_pre_ffn_d = din("g_pre_ffn", [NL, 128, 8])
    g_post_ffn_d = din("g_post_ffn", [NL, 128, 8])
    w_in_d = din("w_in", [NL, D, INW])
    sink_d = din("sink", [NL, 128, 4])
    lam_d = din("lam", [NL, 128, 4, 32])
    g_diff_d = din("g_diff", [NL, 128, 64])
    g_q_d = din("g_q", [NL, 128, 3])
    g_kv_d = din("g_kv", [NL, 128, 1])
    w_uq_d = din("w_uq", [NL, 384, 384])
    w_ukv_d = din("w_ukv", [NL, 128, 512])
    w_out_d = din("w_out", [NL, D, D])
    w_gu_d = din("w_gu", [NL, D, 2 * DFF])
    w_dn_d = din("w_dn", [NL, DFF, D])
    maskA_d = din("maskA", [128, 20, 512])
    maskB_d = din("maskB", [128, 6, 512])
    ident_d = din("ident", [128, 128])
    invf_d = din("invf", [128, 1])
    out_d = nc.dram_tensor("out", [T, D], F32, kind="ExternalOutput").ap()
    xT_d = nc.dram_tensor("xT_scr", [8, 128, T], F32, kind="ExternalOutput" if debug else "Internal").ap()
    mix_d = nc.dram_tensor("mix_scr", [T, D], BF16, kind="Internal").ap()
    if debug:
        mixdbg_d = nc.dram_tensor("mix_dbg", [T, D], F32, kind="ExternalOutput").ap()
        hdbg_d = nc.dram_tensor("h_dbg", [8, 128, T], F32, kind="ExternalOutput").ap()

    S = Sched()
    es = ExitStack()
    ARENA = 207 * 1024
    arena = es.enter_context(nc.sbuf_tensor("arena", [128, ARENA], U8))
    psum = es.enter_context(nc.psum_tensor("psum", [128, 8, 512], F32))
    bank_t = [Tile("bank%d" % i) for i in range(8)]

    def bank(i):
        return psum[:, i, :]

    class Alloc:
        def __init__(self, lo, hi):
            self.lo, self.hi, self.p = lo, hi, lo

        def __call__(self, name, shape, dt):
            esz = {F32: 4, BF16: 2, I32: 4}[dt]
            n = int(np.prod(shape)) * esz
            n = (n + 63) // 64 * 64
            assert self.p + n <= self.hi, (name, self.p, n, self.hi)
            ap = arena[:, self.p:self.p + n].bitcast(dt)
            ap = ap[:, 0:int(np.prod(shape))]
            if len(shape) == 2:
                ap = ap.rearrange("p (a b) -> p a b", a=shape[0])
            elif len(shape) == 3:
                ap = ap.rearrange("p (a b c) -> p a b c", a=shape[0], b=shape[1])
            elif len(shape) == 4:
                ap = ap.rearrange("p (a b c d) -> p a b c d", a=shape[0], b=shape[1], c=shape[2])
            self.p += n
            return Tile(name), ap

    CONST_SZ = 4 * 1024
    ca = Alloc(ARENA - CONST_SZ, ARENA)
    identb_t, identb = ca("identb", [128], BF16)
    identf_t, identf = ca("identf", [128], F32)
    onesb_t, onesb = ca("onesb", [128], BF16)
    eps_t, eps_ap = ca("eps", [1], F32)
    cact_t, cact = ca("cact", [8], BF16)
    cf_t, cf = ca("cf", [8], F32)
    modT_t, modT = ca("modT", [48], F32)
    gs_t, gs = ca("gs", [4, 8], F32)
    graw_t, graw = ca("graw", [4, 8], F32)
    poscol_t, poscol = ca("poscol", [NB], F32)
    nposcol_t, nposcol = ca("nposcol", [NB], F32)
    posci_t, posci = ca("posci", [NB], I32)
    invf_t, invf = ca("invf", [1], F32)
    small_t, small = ca("small", [64], F32)
    lamraw_t, lamraw = ca("lamraw", [4, 32], F32)
    gdf_t, gdf = ca("gdf", [64], F32)
    gq_t, gq = ca("gq", [3], F32)
    gkv_t, gkv = ca("gkv", [1], F32)
    ARENA_W = ARENA - CONST_SZ

    def dma(eng, out, in_, reads=(), writes=(), tile=None):
        S.op(eng, lambda e: e.dma_start(out=out, in_=in_), reads=reads, writes=writes, dma_tile=tile)

    def mm(out, lhsT, rhs, start, stop, reads, writes):
        S.op("pe", lambda e: e.matmul(out, lhsT, rhs, start=start, stop=stop, skip_group_check=True),
             reads=reads, writes=writes)

    def tr(out, in_, ident, reads, writes):
        S.op("pe", lambda e: e.transpose(out, in_, ident), reads=reads, writes=writes)

    def act(out, in_, func, reads, writes, scale=1.0, bias=None, eng="act"):
        if bias is None:
            S.op(eng, lambda e: e.activation(out=out, in_=in_, func=func, scale=scale), reads=reads, writes=writes)
        else:
            S.op(eng, lambda e: e.activation(out=out, in_=in_, func=func, scale=scale, bias=bias),
                 reads=reads, writes=writes)

    def tt(eng, out, in0, in1, op, reads, writes):
        S.op(eng, lambda e: e.tensor_tensor(out=out, in0=in0, in1=in1, op=op), reads=reads, writes=writes)

    def ts(eng, out, in0, s1, op0, reads, writes, s2=None, op1=None):
        if op1 is None:
            S.op(eng, lambda e: e.tensor_scalar(out=out, in0=in0, scalar1=s1, scalar2=None, op0=op0),
                 reads=reads, writes=writes)
        else:
            S.op(eng, lambda e: e.tensor_scalar(out=out, in0=in0, scalar1=s1, scalar2=s2, op0=op0, op1=op1),
                 reads=reads, writes=writes)

    def stt(out, in0, scalar, in1, op0, op1, reads, writes):
        S.op("dve", lambda e: e.scalar_tensor_tensor(out=out, in0=in0, scalar=scalar, in1=in1, op0=op0, op1=op1),
             reads=reads, writes=writes)

    def cp(eng, out, in_, reads, writes):
        S.op(eng, lambda e: e.tensor_copy(out=out, in_=in_), reads=reads, writes=writes)

    def memset(eng, ap, val, writes):
        S.op(eng, lambda e: e.memset(ap, val), writes=writes)

    def recip(out, in_, reads, writes):
        S.op("dve", lambda e: e.reciprocal(out=out, in_=in_), reads=reads, writes=writes)

    def rsum(out, in_, reads, writes):
        S.op("dve", lambda e: e.reduce_sum(out=out, in_=in_, axis=AX.X), reads=reads, writes=writes)

    def rstd_from(out, in_, n, reads, writes):
        act(out, in_, AF.Ln, reads + [eps_t], writes, scale=1.0 / n, bias=eps_ap[:, 0:1])
        act(out, out, AF.Exp, writes, writes, scale=-0.5)

    dma("pool", identb, ident_d, writes=[identb_t], tile=identb_t)
    dma("sp", identf, ident_d, writes=[identf_t], tile=identf_t)
    memset("dve", onesb, 1.0, [onesb_t])
    memset("dve", eps_ap, EPS, [eps_t])
    dma("sp", cf, c_d, writes=[cf_t], tile=cf_t)
    act(cact, cf, AF.Silu, [cf_t], [cact_t])
    dma("sp", posci, poscol_d, writes=[posci_t], tile=posci_t)
    cp("dve", poscol, posci, [posci_t], [poscol_t])
    ts("dve", nposcol, poscol, -1.0, ALU.mult, [poscol_t], [nposcol_t])
    dma("sp", invf, invf_d, writes=[invf_t], tile=invf_t)

    def phase_x_in():
        a = Alloc(0, ARENA_W)
        xin = Rot([a("xin%d" % i, [D], F32) for i in range(6)])
        xo = Rot([a("xo%d" % i, [8, 128], F32) for i in range(6)])
        pbk = Rot([(bank_t[i], i) for i in range(8)])
        for b in range(NB):
            xi_t, xi = xin.next()
            dma("sp", xi, x_d[b * 128:(b + 1) * 128, :], writes=[xi_t], tile=xi_t)
            xo_t, xoap = xo.next()
            for half in range(2):
                bt, bi = pbk.next()
                for kk in range(4):
                    k = half * 4 + kk
                    tr(psum[:, bi, kk * 128:(kk + 1) * 128], xi[:, k * 128:(k + 1) * 128], identf,
                       [xi_t, identf_t], [bt])
                eng = "dve" if half == 0 else "act"
                if eng == "dve":
                    cp("dve", xoap[:, half * 4:(half + 1) * 4, :],
                       psum[:, bi, :].rearrange("p (a b) -> p a b", a=4), [bt], [xo_t])
                else:
                    act(xoap[:, half * 4:(half + 1) * 4, :],
                        psum[:, bi, :].rearrange("p (a b) -> p a b", a=4), AF.Copy, [bt], [xo_t])
            dma("sp", xT_d[:, :, b * 128:(b + 1) * 128].rearrange("k p t -> p k t"), xoap,
                reads=[xo_t], tile=xo_t)
        return

    xT_dt = Tile("xT_dram")
    xTc_t = [Tile("xTd%d" % c) for c in range(NCH)]
    mixc_t = [Tile("mixd%d" % c) for c in range(NCH)]

    phase_x_in()
    S.barrier()

    def load_layer_small(l):
        dma("sp", graw[:, 0, :], g_pre_mix_d[l], writes=[graw_t], tile=graw_t)
        dma("sp", graw[:, 1, :], g_post_mix_d[l], writes=[graw_t], tile=graw_t)
        dma("sp", graw[:, 2, :], g_pre_ffn_d[l], writes=[graw_t], tile=graw_t)
        dma("sp", graw[:, 3, :], g_post_ffn_d[l], writes=[graw_t], tile=graw_t)
        dma("sp", modT, b_ada_d[l], writes=[modT_t], tile=modT_t)
        dma("sp", lamraw, lam_d[l], writes=[lamraw_t], tile=lamraw_t)
        dma("sp", gdf, g_diff_d[l], writes=[gdf_t], tile=gdf_t)
        dma("sp", gq, g_q_d[l], writes=[gq_t], tile=gq_t)
        dma("sp", gkv, g_kv_d[l], writes=[gkv_t], tile=gkv_t)
        dma("sp", small[:, 0:4], sink_d[l], writes=[small_t], tile=small_t)

    def phase_mod(l):
        a = Alloc(0, ARENA_W)
        wa = Rot([a("wada%d" % i, [8, 1024], BF16) for i in range(2)])
        bt = bank_t[0]
        for piece in range(6):
            w_t, w = wa.next()
            dma("pool", w, w_ada_d[l, :, piece * 1024:(piece + 1) * 1024].rearrange("(k p) c -> p k c", p=128),
                writes=[w_t], tile=w_t)
            for jj in range(8):
                j = piece * 8 + jj
                for k in range(8):
                    mm(psum[:, 0, j:j + 1], w[:, k, jj * 128:(jj + 1) * 128], cact[:, k:k + 1],
                       k == 0, k == 7, [w_t, cact_t], [bt])
        tt("dve", modT, modT, psum[:, 0, 0:48], ALU.add, [modT_t, bt], [modT_t])
        stt(gs[:, 0, :], modT[:, 8:16], 1.0, graw[:, 0, :], ALU.add, ALU.mult, [modT_t, graw_t], [gs_t])
        tt("dve", gs[:, 1, :], modT[:, 16:24], graw[:, 1, :], ALU.mult, [modT_t, graw_t], [gs_t])
        stt(gs[:, 2, :], modT[:, 32:40], 1.0, graw[:, 2, :], ALU.add, ALU.mult, [modT_t, graw_t], [gs_t])
        tt("dve", gs[:, 3, :], modT[:, 40:48], graw[:, 3, :], ALU.mult, [modT_t, graw_t], [gs_t])
        act(small[:, 4:8], small[:, 0:4], AF.Exp, [small_t], [small_t])
        tt("dve", lamraw[:, 0, :], lamraw[:, 0, :], lamraw[:, 1, :], ALU.mult, [lamraw_t], [lamraw_t])
        tt("dve", lamraw[:, 2, :], lamraw[:, 2, :], lamraw[:, 3, :], ALU.mult, [lamraw_t], [lamraw_t])
        rsum(small[:, 10:11], lamraw[:, 0, :], [lamraw_t], [small_t])
        rsum(small[:, 11:12], lamraw[:, 2, :], [lamraw_t], [small_t])
        act(small[:, 10:12], small[:, 10:12], AF.Exp, [small_t], [small_t])
        lam_init = 0.8 - 0.6 * math.exp(-0.3 * l)
        tt("dve", small[:, 8:9], small[:, 10:11], small[:, 11:12], ALU.subtract, [small_t], [small_t])
        ts("dve", small[:, 8:9], small[:, 8:9], lam_init, ALU.add, [small_t], [small_t])
        ts("dve", small[:, 9:10], small[:, 8:9], -1.0, ALU.mult, [small_t], [small_t])
        ts("dve", gdf, gdf, 1.0 - lam_init, ALU.mult, [gdf_t], [gdf_t])

    OFF_HT = 0
    OFF_KT = OFF_HT + 16 * T
    OFF_VA = OFF_KT + 8 * T
    OFF_MR = OFF_VA + NB * 4 * 65 * 2 + 64
    MR_SZ = max(32 * 1024, 8 * T)
    OFF_PR = OFF_MR + MR_SZ
    OFF_WK = OFF_PR + max(4 * T, 16384)
    hT_t, hT = Alloc(OFF_HT, OFF_KT)("hT", [8, T], BF16)
    posrow_t, posrow = Alloc(OFF_PR, OFF_WK)("posrow", [T], F32)

    def norm_to_h(c, src_t, src, gidx, shcol, a_work, dst_t, dst, bt_s, bi_s):
        sq = a_work["sq"]
        for k in range(8):
            sq_t, sqap = sq.next()
            act(sqap, src[:, k, :], AF.Square, [src_t], [sq_t], eng="act")
            mm(psum[:, bi_s, :], onesb, sqap, k == 0, k == 7, [onesb_t, sq_t], [bt_s])
        r_t, r = a_work["rstd"].next()
        rstd_from(r, psum[:, bi_s, :], float(D), [bt_s], [r_t])
        for k in range(8):
            tmp_t, tmp = a_work["tmp"].next()
            stt(tmp, src[:, k, :], gs[:, gidx, k:k + 1], r, ALU.mult, ALU.mult, [src_t, gs_t, r_t], [tmp_t])
            act(dst(k), tmp, AF.Identity, [tmp_t, modT_t], [dst_t], bias=modT[:, shcol + k:shcol + k + 1],
                eng="act")

    def phase_norm1(l):
        a2 = Alloc(OFF_KT, OFF_PR)
        xs = Rot([a2("xs%d" % i, [8, 512], F32) for i in range(2)])
        a = Alloc(OFF_WK, ARENA_W)
        w = {"sq": Rot([a("sq%d" % i, [512], BF16) for i in range(3)]),
             "rstd": Rot([a("rstd%d" % i, [512], F32) for i in range(2)]),
             "tmp": Rot([a("tmp%d" % i, [512], F32) for i in range(3)])}
        dma("sp", posrow.bitcast(I32), posrow_d, writes=[posrow_t], tile=posrow_t)
        cp("pool", posrow, posrow.bitcast(I32), [posrow_t], [posrow_t])
        sb = Rot([(bank_t[4], 4), (bank_t[5], 5)])
        for c in range(NCH):
            x_t, xap = xs.next()
            dma("sp", xap, xT_d[:, :, c * 512:(c + 1) * 512].rearrange("k p t -> p k t"), writes=[x_t], tile=x_t)
            bt_s, bi_s = sb.next()
            norm_to_h(c, x_t, xap, 0, 0, w, hT_t, lambda k, c=c: hT[:, k, c * 512:(c + 1) * 512], bt_s, bi_s)

    class AttnCtx:
        def __init__(self, a, bias=True, a_pt=None, sbanks=(0, 1, 2), depth=2):
            self.sb = Rot([(bank_t[i], i) for i in sbanks])
            self.pending = []
            self.depth = depth
            if bias:
                self.tt_ = Rot([a("tt%d" % i, [512], F32) for i in range(depth + 2)])
                self.dt = Rot([a("dt%d" % i, [512], F32) for i in range(2)])
            ap_ = a_pt if a_pt is not None else a
            self.pt = Rot([ap_("pt%d" % i, [512], BF16) for i in range(depth + 2)])

    def flush(ctx):
        while ctx.pending:
            ctx.pending.pop(0)()

    def dist_tile(ctx, c0, W, j):
        d_t, d = ctx.dt.next()
        act(d[:, 0:W], posrow[:, c0:c0 + W], AF.Abs, [posrow_t, nposcol_t], [d_t], bias=nposcol[:, j:j + 1])
        return d_t, d

    def attn_step(ctx, qk, slope, dtile, mask, W, G, vaug, acc, first, last):
        st, si = ctx.sb.next()
        pe_mask = mask is not None and mask[2] == "pe"
        if len(qk) == 1 and G > 1:
            lhsT, rhs, rds = qk[0]
            mm(psum[:, si, 0:G * W], lhsT, rhs, True, True, rds, [st])
        else:
            for g in range(G):
                lhsT, rhs, rds = qk[g]
                mm(psum[:, si, g * W:(g + 1) * W], lhsT, rhs, True, not pe_mask, rds, [st])
        if pe_mask:
            mm(psum[:, si, 0:W], identb, mask[0], False, True, [identb_t, mask[1]], [st])
            mask = None
        if slope is not None:
            d_t, d = dtile
            t_t, tap = ctx.tt_.next()
            if G == 1:
                stt(tap, d[:, 0:W], -slope, psum[:, si, :], ALU.mult, ALU.add, [d_t, st], [t_t])
            else:
                stt(tap.rearrange("p (g w) -> p g w", g=G), d[:, 0:W].unsqueeze(1).broadcast_to([128, G, W]),
                    -slope, psum[:, si, :].rearrange("p (g w) -> p g w", g=G), ALU.mult, ALU.add, [d_t, st], [t_t])
            src, src_t = tap, t_t
        else:
            src, src_t = psum[:, si, :], st
        p_t, p = ctx.pt.next()
        act(p, src, AF.Exp, [src_t], [p_t])
        if mask is not None:
            m_ap, m_t, m_eng = mask
            tt(m_eng, p, p, m_ap, ALU.mult, [p_t, m_t], [p_t])
        nsub = W // 128

        def back():
            for g in range(G):
                v_ap, v_rd = vaug[g]
                a_ap, a_t = acc[g]
                for s in range(nsub):
                    mm(a_ap[:, s, :], p[:, g * W + s * 128:g * W + (s + 1) * 128], v_ap,
                       first and g == 0 and s == 0, last, [p_t] + v_rd, [a_t])
        ctx.pending.append(back)
        if len(ctx.pending) > ctx.depth:
            ctx.pending.pop(0)()

    def proj_fm(bi, bt, w, w_t, col0, M, c, extra_cols=None):
        for k in range(8):
            mm(psum[:M, bi, :], w[:, k, col0:col0 + M], hT[:, k, c * 512:(c + 1) * 512], k == 0, k == 7,
               [w_t, hT_t], [bt])

    def store_mix(ost_t, ost, c0, ntok, col0, ncols):
        nsub = ntok // 128
        dma("sp", mix_d[c0:c0 + ntok, col0:col0 + ncols].rearrange("(s p) c -> p s c", p=128), ost,
            reads=[ost_t], tile=ost_t)

    def phase_A(l):
        kt_t, KT = Alloc(OFF_KT, OFF_VA)("KTa", [2, T], BF16)
        va_t, VA = Alloc(OFF_VA, OFF_MR)("VAa", [NB, 4, 65], BF16)
        mk_t, MK = Alloc(OFF_MR, OFF_PR)("maskA", [20, 512], BF16)
        a = Alloc(OFF_WK, ARENA_W)
        w_t, w = a("wA", [8, 768], BF16)
        ctx = AttnCtx(a, sbanks=(0, 1, 2, 7), depth=3)
        qts = Rot([a("qtA%d" % i, [4, 512], BF16) for i in range(2)])
        for q_t, q in qts.items:
            memset("pool", q, 0.0, [q_t])
        a_sp = Alloc(OFF_MR + 26 * 1024, OFF_PR)
        osts = Rot([a_sp("ostA%d" % i, [4, 256], BF16) for i in range(2)])
        rec_t, rec = a("recA", [4, 4], F32)
        dma("pool", w, w_in_d[l, :, 0:768].rearrange("(k p) c -> p k c", p=128), writes=[w_t], tile=w_t)
        dma("pool", MK, maskA_d, writes=[mk_t], tile=mk_t)
        memset("pool", VA[:, :, :, 64:65], 1.0, [va_t])
        pb = Rot([(bank_t[i], i) for i in range(3)])
        for c in range(NCH):
            for pr in range(2):
                bt, bi = pb.next()
                proj_fm(bi, bt, w, w_t, 256 + pr * 128, 128, c)
                cp("dve", KT[:, pr, c * 512:(c + 1) * 512], psum[:, bi, :], [bt], [kt_t])
        for b in range(NB):
            bt, bi = pb.next()
            for k in range(8):
                mm(psum[:, bi, 0:256], hT[:, k, b * 128:(b + 1) * 128], w[:, k, 512:768], k == 0, k == 7,
                   [hT_t, w_t], [bt])
            act(VA[:, b, :, 0:64], psum[:, bi, 0:256].rearrange("p (h d) -> p h d", h=4), AF.Copy, [bt], [va_t])
        for c in range(NCH):
            q_t, q = qts.next()
            for pr in range(2):
                qb_t, qbi = ctx.sb.next()
                proj_fm(qbi, qb_t, w, w_t, pr * 128, 128, c)
                act(q[0:64, 2 * pr, :], psum[0:64, qbi, :], AF.Identity, [qb_t], [q_t], scale=0.125)
                act(q[64:128, 2 * pr + 1, :], psum[64:128, qbi, :], AF.Identity, [qb_t], [q_t], scale=0.125)
            j0 = max(0, 4 * c - 8)
            j1 = min(NB - 1, 4 * c + 11)
            d_next = dist_tile(ctx, c * 512, 512, j0)
            for j in range(j0, j1 + 1):
                dtile = d_next
                if j < j1:
                    d_next = dist_tile(ctx, c * 512, 512, j + 1)
                o = j - 4 * c + 8
                for h in range(4):
                    hp, pr = (h % 2) * 64, h // 2
                    attn_step(ctx,
                              [(KT[:, pr, j * 128:(j + 1) * 128], q[:, h, :], [kt_t, q_t])],
                              SL_A[h], dtile, (MK[:, o, :], mk_t, "pe"), 512, 1,
                              [(VA[:, j, h, :], [va_t])],
                              [(psum[:, 3 + h, 0:260].rearrange("p (s e) -> p s e", s=4), bank_t[3 + h])],
                              j == j0, j == j1)
            flush(ctx)
            o_t, ost = osts.next()
            for h in range(4):
                accv = psum[:, 3 + h, 0:260].rearrange("p (s e) -> p s e", s=4)
                recip(rec[:, h, :], accv[:, :, 64], [bank_t[3 + h]], [rec_t])
                tt("dve", ost[:, :, h * 64:(h + 1) * 64], accv[:, :, 0:64],
                   rec[:, h, :].unsqueeze(2).broadcast_to([128, 4, 64]), ALU.mult, [bank_t[3 + h], rec_t], [o_t])
            store_mix(o_t, ost, c * 512, 512, 0, 256)

    def phase_B(l):
        kt_t, KT = Alloc(OFF_KT, OFF_VA)("KTb", [T], BF16)
        va_t, VA = Alloc(OFF_VA, OFF_MR)("VAb", [NB, 2, 65], BF16)
        mk_t, MK = Alloc(OFF_MR + 20 * 1024, OFF_PR)("maskB", [6, 512], BF16)
        a = Alloc(OFF_WK, ARENA_W)
        w_t, w = a("wB", [8, 512], BF16)
        ctx = AttnCtx(a, sbanks=(0, 1, 2, 7), depth=3)
        qts = Rot([a("qtB%d" % i, [4, 512], BF16) for i in range(2)])
        for q_t, q in qts.items:
            memset("pool", q, 0.0, [q_t])
        a_sp = Alloc(OFF_MR + 26 * 1024, OFF_PR)
        osts = Rot([a_sp("ostB%d" % i, [4, 256], BF16) for i in range(2)])
        rec_t, rec = a("recB", [4, 4], F32)
        for r in range(2):
            for g in range(2):
                dma("pool", w[:, :, r * 128 + g * 64:r * 128 + (g + 1) * 64],
                    w_in_d[l, :, 768 + (g * 2 + r) * 64:768 + (g * 2 + r + 1) * 64].rearrange("(k p) c -> p k c", p=128),
                    writes=[w_t], tile=w_t)
        dma("pool", w[:, :, 256:512], w_in_d[l, :, 1024:1280].rearrange("(k p) c -> p k c", p=128),
            writes=[w_t], tile=w_t)
        dma("pool", MK, maskB_d, writes=[mk_t], tile=mk_t)
        memset("pool", VA[:, :, :, 64:65], 1.0, [va_t])
        pb = Rot([(bank_t[i], i) for i in range(3)])
        for c in range(NCH):
            bt, bi = pb.next()
            proj_fm(bi, bt, w, w_t, 256, 128, c)
            cp("dve", KT[:, c * 512:(c + 1) * 512], psum[:, bi, :], [bt], [kt_t])
        for b in range(NB):
            bt, bi = pb.next()
            for k in range(8):
                mm(psum[:, bi, 0:128], hT[:, k, b * 128:(b + 1) * 128], w[:, k, 384:512], k == 0, k == 7,
                   [hT_t, w_t], [bt])
            act(VA[:, b, :, 0:64], psum[:, bi, 0:128].rearrange("p (h d) -> p h d", h=2), AF.Copy, [bt], [va_t])
        for c in range(NCH):
            q_t, q = qts.next()
            for r in range(2):
                qb_t, qbi = ctx.sb.next()
                proj_fm(qbi, qb_t, w, w_t, r * 128, 128, c)
                act(q[0:64, r, :], psum[0:64, qbi, :], AF.Identity, [qb_t], [q_t], scale=0.125)
                act(q[64:128, 2 + r, :], psum[64:128, qbi, :], AF.Identity, [qb_t], [q_t], scale=0.125)
            j0 = max(0, 4 * c - 1)
            j1 = min(NB - 1, 4 * c + 4)
            d_next = dist_tile(ctx, c * 512, 512, j0)
            for j in range(j0, j1 + 1):
                dtile = d_next
                if j < j1:
                    d_next = dist_tile(ctx, c * 512, 512, j + 1)
                o = j - 4 * c + 1
                for h in range(4):
                    g, r = h // 2, h % 2
                    attn_step(ctx,
                              [(KT[:, j * 128:(j + 1) * 128], q[:, h, :], [kt_t, q_t])],
                              SL_B[h], dtile, (MK[:, o, :], mk_t, "pe"), 512, 1,
                              [(VA[:, j, g, :], [va_t])],
                              [(psum[:, 3 + h, 0:260].rearrange("p (s e) -> p s e", s=4), bank_t[3 + h])],
                              j == j0, j == j1)
            flush(ctx)
            o_t, ost = osts.next()
            for h in range(4):
                accv = psum[:, 3 + h, 0:260].rearrange("p (s e) -> p s e", s=4)
                ts("dve", rec[:, h, :], accv[:, :, 64], small[:, 4 + h:5 + h], ALU.add,
                   [bank_t[3 + h], small_t], [rec_t])
                recip(rec[:, h, :], rec[:, h, :], [rec_t], [rec_t])
                tt("dve", ost[:, :, h * 64:(h + 1) * 64], accv[:, :, 0:64],
                   rec[:, h, :].unsqueeze(2).broadcast_to([128, 4, 64]), ALU.mult, [bank_t[3 + h], rec_t], [o_t])
            store_mix(o_t, ost, c * 512, 512, 256, 256)

    def phase_C(l):
        kt_t, KT = Alloc(OFF_KT, OFF_VA)("KTc", [2, T], BF16)
        va_t, VA = Alloc(OFF_VA, OFF_MR)("VAc", [NB, 4, 65], BF16)
        a = Alloc(OFF_WK, ARENA_W)
        w_t, w = a("wC", [8, 768], BF16)
        ctx = AttnCtx(a, sbanks=(0, 1, 2, 7), depth=3)
        qts = Rot([a("qtC%d" % i, [8, 256], BF16) for i in range(2)])
        for q_t, q in qts.items:
            memset("pool", q, 0.0, [q_t])
        osts = Rot([a("ostC%d" % i, [2, 256], BF16) for i in range(2)])
        rec_t, rec = a("recC", [4, 8], F32)
        o1_t, o1 = a("o1C", [64], F32)
        o2_t, o2 = a("o2C", [64], F32)
        sqc_t, sqc = a("sqC", [64], F32)
        ss_t, ss = a("ssC", [2], F32)
        dma("pool", w, w_in_d[l, :, 1280:2048].rearrange("(k p) c -> p k c", p=128), writes=[w_t], tile=w_t)
        memset("pool", VA[:, :, :, 64:65], 1.0, [va_t])
        pb = Rot([(bank_t[i], i) for i in range(3)])
        for c in range(NCH):
            for pr in range(2):
                bt, bi = pb.next()
                proj_fm(bi, bt, w, w_t, 256 + pr * 128, 128, c)
                cp("dve", KT[:, pr, c * 512:(c + 1) * 512], psum[:, bi, :], [bt], [kt_t])
        for b in range(NB):
            bt, bi = pb.next()
            for k in range(8):
                mm(psum[:, bi, 0:256], hT[:, k, b * 128:(b + 1) * 128], w[:, k, 512:768], k == 0, k == 7,
                   [hT_t, w_t], [bt])
            act(VA[:, b, :, 0:64], psum[:, bi, 0:256].rearrange("p (h d) -> p h d", h=4), AF.Copy, [bt], [va_t])
        qscale = 32.0 ** -0.5
        for c2 in range(T // 256):
            q_t, q = qts.next()
            for pr in range(2):
                qb_t, qbi = ctx.sb.next()
                for k in range(8):
                    mm(psum[:, qbi, 0:256], w[:, k, pr * 128:(pr + 1) * 128], hT[:, k, c2 * 256:(c2 + 1) * 256],
                       k == 0, k == 7, [w_t, hT_t], [qb_t])
                for blk in range(4):
                    eng = "dve" if blk % 2 == 0 else "act"
                    if eng == "dve":
                        ts("dve", q[blk * 32:(blk + 1) * 32, pr * 4 + blk, :], psum[blk * 32:(blk + 1) * 32, qbi, 0:256],
                           qscale, ALU.mult, [qb_t], [q_t])
                    else:
                        act(q[blk * 32:(blk + 1) * 32, pr * 4 + blk, :], psum[blk * 32:(blk + 1) * 32, qbi, 0:256],
                            AF.Identity, [qb_t], [q_t], scale=qscale)
            d_next = dist_tile(ctx, c2 * 256, 256, 0)
            for j in range(NB):
                dtile = d_next
                if j < NB - 1:
                    d_next = dist_tile(ctx, c2 * 256, 256, j + 1)
                for h in range(4):
                    hp, pr = (h % 2) * 64, h // 2
                    accv = psum[:, 3 + h, 0:260].rearrange("p (m s e) -> p m s e", m=2, s=2)
                    attn_step(ctx,
                              [(KT[:, pr, j * 128:(j + 1) * 128], q[:, h * 2:h * 2 + 2, :], [kt_t, q_t])],
                              SL_C[h], dtile, None, 256, 2,
                              [(VA[:, j, h, :], [va_t])] * 2,
                              [(accv[:, m, :, :], bank_t[3 + h]) for m in range(2)],
                              j == 0, j == NB - 1)
            flush(ctx)
            o_t, ost = osts.next()
            for h in range(4):
                bt = bank_t[3 + h]
                accf = psum[:, 3 + h, 0:260].rearrange("p (ms e) -> p ms e", ms=4)
                recip(rec[:, h, 0:4], accf[:, :, 64], [bt], [rec_t])
                ts("dve", rec[:, h, 2:4], rec[:, h, 2:4], small[:, 9:10], ALU.mult, [rec_t, small_t], [rec_t])
                for s in range(2):
                    ts("dve", o1, accf[:, s, 0:64], rec[:, h, s:s + 1], ALU.mult, [bt, rec_t], [o1_t])
                    stt(o2, accf[:, 2 + s, 0:64], rec[:, h, 2 + s:3 + s], o1, ALU.mult, ALU.add,
                        [bt, rec_t, o1_t], [o2_t])
                    tt("dve", sqc, o2, o2, ALU.mult, [o2_t], [sqc_t])
                    rsum(ss[:, 0:1], sqc, [sqc_t], [ss_t])
                    rstd_from(ss[:, 1:2], ss[:, 0:1], 64.0, [ss_t], [ss_t])
                    stt(ost[:, s, h * 64:(h + 1) * 64], o2, ss[:, 1:2], gdf, ALU.mult, ALU.mult,
                        [o2_t, ss_t, gdf_t], [o_t])
            store_mix(o_t, ost, c2 * 256, 256, 512, 256)

    def phase_D(l):
        kt_t, KT = Alloc(OFF_KT, OFF_VA)("KTd", [4, T], BF16)
        va_t, VA = Alloc(OFF_VA, OFF_MR)("VAd", [NB, 4, 65], BF16)
        ar = Alloc(OFF_MR, OFF_PR)
        cos_t, cosT = ar("cosT", [T], F32)
        sin_t, sinT = ar("sinT", [T], F32)
        a = Alloc(OFF_WK, ARENA_W)
        w_t, w = a("wD", [8, 544], BF16)
        ws_t, ws = a("wDs", [8, 96], BF16)
        wq_t, wq = a("wuq", [3, 384], BF16)
        wqs_t, wqs = a("wuqs", [3, 4, 96], BF16)
        wkv_t, wkv = a("wukv", [512], BF16)
        stg_t, stg = a("stgD", [3, 384], F32)
        cq_t, cqT = a("cqT", [3, 512], BF16)
        ckv_t, ckvT = a("ckvT", [512], BF16)
        sq = Rot([a("sqD%d" % i, [512], BF16) for i in range(2)])
        rq_t, rq = a("rstdq", [512], F32)
        rkv_t, rkv = a("rstdkv", [512], F32)
        rtm_t, rtm = a("rstdtm", [4], F32)
        t1_t, t1 = a("t1D", [512], F32)
        t2_t, t2 = a("t2D", [512], F32)
        rec_t, rec = a("recD", [4, 4], F32)
        dma("pool", w, w_in_d[l, :, 2048:2592].rearrange("(k p) c -> p k c", p=128), writes=[w_t], tile=w_t)
        memset("pool", ws[:, :, 0:64], 0.0, [ws_t])
        dma("pool", ws[:, :, 64:80], w_in_d[l, :, 2576:2592].rearrange("(k p) c -> p k c", p=128), writes=[ws_t], tile=ws_t)
        dma("pool", ws[:, :, 80:96], w_in_d[l, :, 2560:2576].rearrange("(k p) c -> p k c", p=128), writes=[ws_t], tile=ws_t)
        ts("pool", ws[:, :, 64:80], ws[:, :, 64:80], -1.0, ALU.mult, [ws_t], [ws_t], s2=0.0, op1=ALU.add)
        dma("sp", stg, w_uq_d[l].rearrange("(k p) c -> p k c", p=128), writes=[stg_t], tile=stg_t)
        memset("pool", wqs[:, :, :, 0:64], 0.0, [wqs_t])
        for k in range(3):
            ts("dve", wq[:, k, :], stg[:, k, :], gq[:, k:k + 1], ALU.mult, [stg_t, gq_t], [wq_t])
            sv = stg[:, k, :].rearrange("p (h e) -> p h e", h=4)
            ts("dve", wqs[:, k, :, 64:80], sv[:, :, 80:96], gq[:, k:k + 1], ALU.mult, [stg_t, gq_t], [wqs_t],
               s2=-1.0, op1=ALU.mult)
            ts("dve", wqs[:, k, :, 80:96], sv[:, :, 64:80], gq[:, k:k + 1], ALU.mult, [stg_t, gq_t], [wqs_t])
        stg2_t, stg2 = a("stgD2", [512], F32)
        dma("sp", stg2, w_ukv_d[l], writes=[stg2_t], tile=stg2_t)
        ts("dve", wkv, stg2, gkv[:, 0:1], ALU.mult, [stg2_t, gkv_t], [wkv_t])
        memset("pool", VA[:, :, :, 64:65], 1.0, [va_t])
        R = slice(64, 96)
        for c in range(NCH):
            cs = slice(c * 512, (c + 1) * 512)
            for tab_t, tab, shift in ((sin_t, sinT, 0.0), (cos_t, cosT, math.pi / 2)):
                ts("dve", t1[R, :], posrow[R, cs], invf[R, 0:1], ALU.mult, [posrow_t, invf_t], [t1_t],
                   s2=shift, op1=ALU.add)
                ts("dve", t2[R, :], t1[R, :], 1.0 / TWO_PI, ALU.mult, [t1_t], [t2_t], s2=MAGIC, op1=ALU.add)
                ts("dve", t2[R, :], t2[R, :], MAGIC, ALU.subtract, [t2_t], [t2_t])
                stt(t1[R, :], t2[R, :], -C1, t1[R, :], ALU.mult, ALU.add, [t2_t, t1_t], [t1_t])
                stt(t1[R, :], t2[R, :], -C2, t1[R, :], ALU.mult, ALU.add, [t2_t, t1_t], [t1_t])
                ts("dve", t1[R, :], t1[R, :], math.pi, ALU.min, [t1_t], [t1_t], s2=-math.pi, op1=ALU.max)
                act(tab[R, cs], t1[R, :], AF.Sin, [t1_t], [tab_t])
        S.barrier()
        a3 = Alloc(OFF_PR, OFF_WK)
        ctx = AttnCtx(a3, bias=False)
        qts = Rot([a3("qtD%d" % i, [4, 512], BF16) for i in range(2)])
        osts = Rot([a3("ostD%d" % i, [4, 256], BF16) for i in range(2)])
        pb = Rot([(bank_t[i], i) for i in range(3)])
        for c in range(NCH):
            cs = slice(c * 512, (c + 1) * 512)
            bt, bi = pb.next()
            proj_fm(bi, bt, w, w_t, 384, 128, c)
            act(ckvT, psum[:, bi, :], AF.Copy, [bt], [ckv_t])
            s_t, sqap = sq.next()
            act(sqap, psum[:, bi, :], AF.Square, [bt], [s_t])
            mm(psum[:, 6, :], onesb, sqap, True, True, [onesb_t, s_t], [bank_t[6]])
            rstd_from(rkv, psum[:, 6, :], 128.0, [bank_t[6]], [rkv_t])
            for sblk in range(4):
                mm(psum[:, 6, sblk:sblk + 1], sqap[:, sblk * 128:(sblk + 1) * 128], onesb[:, 0:1], True, True,
                   [s_t, onesb_t], [bank_t[6]])
            rstd_from(rtm, psum[:, 6, 0:4], 128.0, [bank_t[6]], [rtm_t])
            bta, bia = pb.next()
            proj_fm(bia, bta, w, w_t, 448, 96, c)
            btb, bib = pb.next()
            for k in range(8):
                mm(psum[0:96, bib, :], ws[:, k, :], hT[:, k, cs], k == 0, k == 7, [ws_t, hT_t], [btb])
            tt("dve", t1[R, :], psum[R, bia, :], cosT[R, cs], ALU.mult, [bta, cos_t], [t1_t])
            tt("dve", t2[R, :], psum[R, bib, :], sinT[R, cs], ALU.mult, [btb, sin_t], [t2_t])
            for h in range(4):
                tt("pool", KT[R, h, cs], t1[R, :], t2[R, :], ALU.add, [t1_t, t2_t], [kt_t])
            for h in range(4):
                bt, bi = pb.next()
                mm(psum[0:64, bi, :], wkv[:, h * 128:h * 128 + 64], ckvT, True, True, [wkv_t, ckv_t], [bt])
                tt("dve", KT[0:64, h, cs], psum[0:64, bi, :], rkv[0:64, :], ALU.mult, [bt, rkv_t], [kt_t])
            for sblk in range(4):
                b = c * 4 + sblk
                bt, bi = pb.next()
                mm(psum[:, bi, 0:256],
                   ckvT[:, sblk * 128:(sblk + 1) * 128],
                   wkv.rearrange("p (h t d) -> p h t d", h=4, t=2)[:, :, 1, :], True, True, [ckv_t, wkv_t], [bt])
                ts("dve", VA[:, b, :, 0:64], psum[:, bi, 0:256].rearrange("p (h d) -> p h d", h=4),
                   rtm[:, sblk:sblk + 1], ALU.mult, [bt, rtm_t], [va_t])
        dsc = 96.0 ** -0.5
        accb = Rot([(bank_t[3 + i], 3 + i) for i in range(3)])
        for c in range(NCH):
            cs = slice(c * 512, (c + 1) * 512)
            for k in range(3):
                proj_fm(7, bank_t[7], w, w_t, k * 128, 128, c)
                act(cqT[:, k, :], psum[:, 7, :], AF.Copy, [bank_t[7]], [cq_t])
                s_t, sqap = sq.next()
                act(sqap, psum[:, 7, :], AF.Square, [bank_t[7]], [s_t])
                mm(psum[:, 6, :], onesb, sqap, k == 0, k == 2, [onesb_t, s_t], [bank_t[6]])
            rstd_from(rq, psum[:, 6, :], 384.0, [bank_t[6]], [rq_t])
            q_t, q = qts.next()
            for h in range(4):
                for k in range(3):
                    mm(psum[0:96, 7, :], wq[:, k, h * 96:(h + 1) * 96], cqT[:, k, :], k == 0, k == 2,
                       [wq_t, cq_t], [bank_t[7]])
                for k in range(3):
                    mm(psum[0:96, 6, :], wqs[:, k, h, :], cqT[:, k, :], k == 0, k == 2, [wqs_t, cq_t], [bank_t[6]])
                stt(q[0:64, h, :], psum[0:64, 7, :], dsc, rq[0:64, :], ALU.mult, ALU.mult, [bank_t[7], rq_t], [q_t])
                tt("dve", t1[R, :], psum[R, 7, :], cosT[R, cs], ALU.mult, [bank_t[7], cos_t], [t1_t])
                tt("dve", t2[R, :], psum[R, 6, :], sinT[R, cs], ALU.mult, [bank_t[6], sin_t], [t2_t])
                tt("pool", t1[R, :], t1[R, :], t2[R, :], ALU.add, [t1_t, t2_t], [t1_t])
                stt(q[R, h, :], t1[R, :], dsc, rq[R, :], ALU.mult, ALU.mult, [t1_t, rq_t], [q_t])
            o_t, ost = osts.next()
            for h in range(4):
                at, ai = accb.next()
                accv = psum[:, ai, 0:260].rearrange("p (s e) -> p s e", s=4)
                for j in range(NB):
                    attn_step(ctx, [(KT[0:96, h, j * 128:(j + 1) * 128], q[0:96, h, :], [kt_t, q_t])],
                              None, None, None, 512, 1, [(VA[:, j, h, :], [va_t])], [(accv, at)],
                              j == 0, j == NB - 1)
                flush(ctx)
                recip(rec[:, h, :], accv[:, :, 64], [at], [rec_t])
                tt("dve", ost[:, :, h * 64:(h + 1) * 64], accv[:, :, 0:64],
                   rec[:, h, :].unsqueeze(2).broadcast_to([128, 4, 64]), ALU.mult, [at, rec_t], [o_t])
            store_mix(o_t, ost, c * 512, 512, 768, 256)

    OFF_WGU = 0
    OFF_WDN = OFF_WGU + 8 * 2 * DFF * 2
    OFF_PW = OFF_WDN + 22 * D * 2

    def sumsq_a(a_w, src, which="acc"):
        acc_t, acc = a_w[which]
        for k in range(8):
            ap, t_ = src(k)
            if k == 0:
                act(acc, ap, AF.Square, [t_], [acc_t])
            else:
                tmp_t, tmp = a_w["tmp"].next()
                act(tmp, ap, AF.Square, [t_], [tmp_t])
                tt("pool", acc, acc, tmp, ALU.add, [acc_t, tmp_t], [acc_t])

    def sumsq_b(a_w, which="acc"):
        acc_t, acc = a_w[which]
        of_t, of_ = a_w["onesf"]
        mm(psum[:, 6, :], of_, acc, True, True, [of_t, acc_t], [bank_t[6]])

    def post_update(yT_t, yT, gidx, c, a_w, last_layer_ffn, stats_done=False):
        if not stats_done:
            sumsq_a(a_w, lambda j: (yT[:, j, :], yT_t))
        sumsq_b(a_w)
        r_t, r = a_w["rstd"].next()
        rstd_from(r, psum[:, 6, :], float(D), [bank_t[6]], [r_t])
        def xload(j):
            x_t, xap = a_w["xj"].next()
            dma("sp", xap, xT_d[j, :, c * 512:(c + 1) * 512], writes=[x_t], tile=x_t)
            return x_t, xap
        pend = [xload(j) for j in range(3)]
        for j in range(8):
            x_t, xap = pend.pop(0)
            tmp_t, tmp = a_w["tmp"].next()
            stt(tmp, yT[:, j, :], gs[:, gidx, j:j + 1], r, ALU.mult, ALU.mult, [yT_t, gs_t, r_t], [tmp_t])
            tt("pool", xap, xap, tmp, ALU.add, [x_t, tmp_t], [x_t])
            if not last_layer_ffn:
                dma("sp", xT_d[j, :, c * 512:(c + 1) * 512], xap, reads=[x_t], tile=x_t)
            else:
                o_t, oap = a_w["otok"].next()
                for sblk in range(4):
                    tr(psum[:, 7, sblk * 128:(sblk + 1) * 128], xap[:, sblk * 128:(sblk + 1) * 128], identf,
                       [x_t, identf_t], [bank_t[7]])
                cp("dve", oap, psum[:, 7, :].rearrange("p (s d) -> p s d", s=4), [bank_t[7]], [o_t])
                dma("sp", out_d[c * 512:(c + 1) * 512, j * 128:(j + 1) * 128].rearrange("(s p) d -> p s d", p=128),
                    oap, reads=[o_t], tile=o_t)
            if j + 3 < 8:
                pend.append(xload(j + 3))

    def phase_post(l):
        last = (l == nlayers - 1)
        wgu_t, wgu = Alloc(OFF_WGU, OFF_WDN)("wgu", [8, 2 * DFF], BF16)
        wdn_t, wdn = Alloc(OFF_WDN, OFF_PW)("wdn", [22, D], BF16)
        a = Alloc(OFF_PW, ARENA_W)
        yT_t, yT = a("yT", [8, 512], F32)
        aw = {"acc": a("accP", [512], F32), "onesf": a("onesf", [128], F32),
              "rstd": Rot([a("rstdP%d" % i, [512], F32) for i in range(1)]),
              "tmp": Rot([a("tmpP%d" % i, [512], F32) for i in range(2)]),
              "xj": Rot([a("xjP%d" % i, [512], F32) for i in range(3)])}
        memset("pool", aw["onesf"][1], 1.0, [aw["onesf"][0]])
        P2 = a.p
        wo_t, wo = a("wout", [8, D], BF16)
        mts = Rot([a("mixtok%d" % i, [4, D], BF16) for i in range(2)])
        mT_t, mixT = a("mixT", [8, 512], BF16)
        dma("pool", wo, w_out_d[l].rearrange("(k p) c -> p k c", p=128), writes=[wo_t], tile=wo_t)
        for piece in range(4):
            cs = slice(piece * 1408, (piece + 1) * 1408)
            dma("pool", wgu[:, :, cs], w_gu_d[l, :, cs].rearrange("(k p) c -> p k c", p=128), writes=[wgu_t], tile=wgu_t)
        for piece in range(2):
            fs = slice(piece * 11, (piece + 1) * 11)
            dma("pool", wdn[:, fs, :], w_dn_d[l, piece * 1408:(piece + 1) * 1408, :].rearrange("(f p) c -> p f c", p=128),
                writes=[wdn_t], tile=wdn_t)
        psb = psum[:, 0:2, :].bitcast(BF16)
        yb = Rot([(bank_t[i], i) for i in (2, 3, 4)])

        def wo_front(c):
            mt_t, mtok = mts.next()
            dma("sp", mtok, mix_d[c * 512:(c + 1) * 512, :].rearrange("(s p) d -> p s d", p=128), writes=[mt_t], tile=mt_t)
            for k in range(8):
                bi = k % 2
                for sblk in range(4):
                    tr(psb[:, bi, sblk * 128:(sblk + 1) * 128], mtok[:, sblk, k * 128:(k + 1) * 128], identb,
                       [mt_t, identb_t], [bank_t[bi]])
                cp("dve", mixT[:, k, :], psb[:, bi, 0:512], [bank_t[bi]], [mT_t])

        def wo_back(c):
            for j in range(8):
                bt, bi = yb.next()
                for k in range(8):
                    mm(psum[:, bi, :], wo[:, k, j * 128:(j + 1) * 128], mixT[:, k, :], k == 0, k == 7,
                       [wo_t, mT_t], [bt])
                act(yT[:, j, :], psum[:, bi, :], AF.Copy, [bt], [yT_t])

        wo_front(0)
        wo_back(0)
        for c in range(NCH):
            if c + 1 < NCH:
                wo_front(c + 1)
            post_update(yT_t, yT, 1, c, aw, False)
            if c + 1 < NCH:
                wo_back(c + 1)
        S.barrier()
        if stop_after == "wout%d" % l:
            return
        a = Alloc(P2, ARENA_W)
        hc_t, hc = a("hTc", [8, 512], BF16)
        at_t, actT = a("actT", [22, 512], BF16)
        sl_t, slu = a("silu", [2, 512], F32)
        if last:
            aw["otok"] = Rot([a("otok%d" % i, [4, 128], F32) for i in range(1)])
        aw["acc2"] = a("acc2", [512], F32)
        rstd2_t, rstd2 = a("rstd2", [512], F32)
        gb = Rot([(bank_t[0], 0), (bank_t[2], 2)])
        ub = Rot([(bank_t[1], 1), (bank_t[3], 3)])
        yb2 = Rot([(bank_t[4], 4), (bank_t[5], 5)])
        def norm_a(c):
            def src(k):
                x_t, xap = aw["xj"].next()
                dma("sp", xap, xT_d[k, :, c * 512:(c + 1) * 512], writes=[x_t], tile=x_t)
                return xap, x_t
            sumsq_a(aw, src, "acc2")

        def norm_b(c):
            sumsq_b(aw, "acc2")
            rstd_from(rstd2, psum[:, 6, :], float(D), [bank_t[6]], [rstd2_t])
            for k in range(8):
                x_t, xap = aw["xj"].next()
                dma("sp", xap, xT_d[k, :, c * 512:(c + 1) * 512], writes=[x_t], tile=x_t)
                tmp_t, tmp = aw["tmp"].next()
                stt(tmp, xap, gs[:, 2, k:k + 1], rstd2, ALU.mult, ALU.mult, [x_t, gs_t, rstd2_t], [tmp_t])
                act(hc[:, k, :], tmp, AF.Identity, [tmp_t, modT_t], [hc_t], bias=modT[:, 24 + k:25 + k])

        def gu(f0, f1):
            for f in range(f0, f1):
                gt, gi = gb.next()
                ut, ui = ub.next()
                for k in range(8):
                    mm(psum[:, gi, :], wgu[:, k, f * 128:(f + 1) * 128], hc[:, k, :], k == 0, k == 7, [wgu_t, hc_t], [gt])
                for k in range(8):
                    mm(psum[:, ui, :], wgu[:, k, DFF + f * 128:DFF + (f + 1) * 128], hc[:, k, :], k == 0, k == 7,
                       [wgu_t, hc_t], [ut])
                act(slu[:, f % 2, :], psum[:, gi, :], AF.Silu, [gt], [sl_t])
                tt("dve", actT[:, f, :], slu[:, f % 2, :], psum[:, ui, :], ALU.mult, [sl_t, ut], [at_t])

        def down(c):
            for j in range(8):
                bt, bi = yb2.next()
                for f in range(22):
                    mm(psum[:, bi, :], wdn[:, f, j * 128:(j + 1) * 128], actT[:, f, :], f == 0, f == 21, [wdn_t, at_t], [bt])
                cp("dve", yT[:, j, :], psum[:, bi, :], [bt], [yT_t])

        norm_a(0)
        norm_b(0)
        gu(0, 22)
        if NCH > 1:
            norm_a(1)
            norm_b(1)
        for c in range(NCH):
            down(c)
            sumsq_a(aw, lambda j: (yT[:, j, :], yT_t))
            if c + 2 < NCH:
                norm_a(c + 2)
            if c + 1 < NCH:
                gu(0, 8)
            post_update(yT_t, yT, 3, c, aw, last, stats_done=True)
            if c + 1 < NCH:
                gu(8, 22)
            if c + 2 < NCH:
                norm_b(c + 2)

    def run_phases():
        yield "xin", None
        for l in range(nlayers):
            def ph_mod(l=l):
                load_layer_small(l)
                phase_mod(l)
            yield "mod%d" % l, ph_mod
            yield "norm%d" % l, lambda l=l: phase_norm1(l)
            yield "A%d" % l, lambda l=l: phase_A(l)
            yield "B%d" % l, lambda l=l: phase_B(l)
            yield "C%d" % l, lambda l=l: phase_C(l)
            yield "D%d" % l, lambda l=l: phase_D(l)
            yield "post%d" % l, lambda l=l: phase_post(l)

    for name, fn in run_phases():
        if fn is not None:
            fn()
            S.barrier()
        if debug and name.startswith("norm"):
            dt_ = Tile("hdbg")
            dma("pool", hdbg_d.rearrange("k p t -> p k t"), hT, reads=[hT_t], tile=dt_)
            S.barrier()
        if name == stop_after or (stop_after is not None and name.startswith("post") and stop_after == "wout" + name[4:]):
            break
    if debug:
        dt2 = Tile("mixdbg")
        dma("pool", mixdbg_d, mix_d, tile=dt2)
    S.barrier()
    S.emit(nc)
    es.close()
    return nc, S


def _fm(v, n):
    return np.ascontiguousarray(v.reshape(v.shape[0], n, 128).transpose(0, 2, 1))


def _masks():
    p = np.arange(128)[:, None]
    q = np.arange(512)[None, :]
    mA = np.zeros((128, 20, 512), np.float32)
    for o in range(20):
        diff = q - p - (o - 8) * 128
        m = np.zeros((128, 512), np.float32)
        for d in (1, 4, 16):
            m += ((diff % d == 0) & (np.abs(diff) <= 64 * d)).astype(np.float32)
        mA[:, o, :] = np.where(m > 0, np.log(np.maximum(m, 1.0)), -30000.0)
    mB = np.zeros((128, 6, 512), np.float32)
    for o in range(6):
        diff = q - p - (o - 1) * 128
        mB[:, o, :] = np.where(np.abs(diff) <= 128, 0.0, -30000.0)
    return mA, mB


def make_in_maps(inputs, T, nb):
    f = np.float32
    mA, mB = _masks()
    invf = np.zeros((128, 1), f)
    half = 16
    inv = np.power(np.float32(10000.0), -np.arange(half, dtype=f) / np.float32(half)).astype(f)
    for r in range(128):
        invf[r, 0] = inv[r % 16]
    shared = {
        "w_ada": np.ascontiguousarray(inputs["w_ada"], f),
        "b_ada": _fm(np.asarray(inputs["b_ada"], f), 48),
        "g_pre_mix": _fm(np.asarray(inputs["g_pre_mix"], f), 8),
        "g_post_mix": _fm(np.asarray(inputs["g_post_mix"], f), 8),
        "g_pre_ffn": _fm(np.asarray(inputs["g_pre_ffn"], f), 8),
        "g_post_ffn": _fm(np.asarray(inputs["g_post_ffn"], f), 8),
        "w_in": np.ascontiguousarray(inputs["w_in"], f),
        "sink": np.ascontiguousarray(np.broadcast_to(np.asarray(inputs["sink_logits"], f)[:, None, :], (NL, 128, 4))),
        "lam": np.ascontiguousarray(np.broadcast_to(
            np.stack([np.asarray(inputs[k], f) for k in ("lam_q1", "lam_k1", "lam_q2", "lam_k2")], axis=1)[:, None],
            (NL, 128, 4, 32))),
        "g_diff": np.ascontiguousarray(np.broadcast_to(np.asarray(inputs["g_diff"], f)[:, None, :], (NL, 128, 64))),
        "g_q": _fm(np.asarray(inputs["g_mla_q"], f), 3),
        "g_kv": _fm(np.asarray(inputs["g_mla_kv"], f), 1),
        "w_uq": np.ascontiguousarray(inputs["w_uq"], f),
        "w_ukv": np.ascontiguousarray(inputs["w_ukv"], f),
        "w_out": np.ascontiguousarray(inputs["w_out"], f),
        "w_gu": np.ascontiguousarray(inputs["w_gate_up"], f),
        "w_dn": np.ascontiguousarray(inputs["w_down"], f),
        "maskA": mA, "maskB": mB,
        "ident": np.eye(128, dtype=f),
        "invf": invf,
    }
    x = np.asarray(inputs["x"], f)
    c = np.asarray(inputs["c"], f)
    pos = np.asarray(inputs["positions"], np.int32)
    maps = []
    for b in range(nb):
        m = dict(shared)
        m["x"] = np.ascontiguousarray(x[b])
        m["c"] = np.ascontiguousarray(c[b].reshape(8, 128).T)
        m["posrow"] = np.ascontiguousarray(np.broadcast_to(pos[b][None, :], (128, T)))
        m["poscol"] = np.ascontiguousarray(pos[b].reshape(T // 128, 128).T)
        maps.append(m)
    return maps


_CACHE = {}


def kernel(**inputs):
    x = np.asarray(inputs["x"])
    B, T, _ = x.shape
    if T not in _CACHE:
        _CACHE[T] = build(T)[0]
    nc = _CACHE[T]
    maps = make_in_maps(inputs, T, B)
    res = run_bass_kernel_spmd(nc, maps, core_ids=list(range(B)))
    return np.stack([np.asarray(r["out"], np.float32) for r in res.results], axis=0)
```
